# Optimizing a Trainium2 kernel written in Bass

```python
import math
import jax, jax.numpy as jnp
from jax import lax
import numpy as np

D_MODEL = 2048
BATCH = 2
SEQ = 16384
DEPTH = 1
DEC_BATCH = 1
DEC_SEQ = 8192
PAST_LEN = 128

HEAD_DIM = 128
A_HEADS = 6
A_KV_HEADS = 2
A_GROUP = A_HEADS // A_KV_HEADS
WINDOW = 128
BLOCK = 128
N_BUCKETS = 32
MAX_DISTANCE = 128
B_HEADS = 6
Q_LORA = 512
KV_LORA = 512
QK_NOPE = 128
QK_ROPE = 64
V_DIM = 128
ROPE_THETA = 10000.0
Q_BLOCK = 128
C_HEADS = 4
N_MEM = 256
N_BRANCH = 3
D_FF = -(-8 * D_MODEL // (3 * 256)) * 256
ALPHA = (2 * DEPTH) ** 0.25
BETA = (8 * DEPTH) ** -0.25
LN_EPS = 1e-5
RMS_EPS = 1e-6
NEG = -1e30
IN_WIDTHS = (A_HEADS * HEAD_DIM, A_KV_HEADS * HEAD_DIM, A_KV_HEADS * HEAD_DIM, Q_LORA, KV_LORA, QK_ROPE, C_HEADS * HEAD_DIM)
D_IN = sum(IN_WIDTHS)

kernel_name = "hybrid_gated_window_mla_memxattn_encoder"


def layer_norm(x, g, b):
    xf = x.astype(jnp.float32)
    mu = jnp.mean(xf, axis=-1, keepdims=True)
    var = jnp.mean(jnp.square(xf - mu), axis=-1, keepdims=True)
    return ((xf - mu) * lax.rsqrt(var + LN_EPS) * g.astype(jnp.float32) + b.astype(jnp.float32)).astype(x.dtype)


def rms_norm(x, g):
    xf = x.astype(jnp.float32)
    ms = jnp.mean(jnp.square(xf), axis=-1, keepdims=True)
    return (xf * lax.rsqrt(ms + RMS_EPS) * g.astype(jnp.float32)).astype(x.dtype)


def rope_tables(S):
    half = QK_ROPE // 2
    inv = 1.0 / (ROPE_THETA ** (jnp.arange(half, dtype=jnp.float32) / half))
    ang = jnp.arange(S, dtype=jnp.float32)[:, None] * inv[None, :]
    return jnp.cos(ang), jnp.sin(ang)


def apply_rope(x, cos, sin):
    half = QK_ROPE // 2
    xf = x.astype(jnp.float32)
    x1, x2 = xf[..., :half], xf[..., half:]
    return jnp.concatenate([x1 * cos - x2 * sin, x2 * cos + x1 * sin], axis=-1).astype(x.dtype)


def t5_bucket(rel):
    half = N_BUCKETS // 2
    max_exact = half // 2
    ret = (rel > 0).astype(jnp.int32) * half
    n = jnp.abs(rel)
    large = max_exact + (jnp.log(jnp.maximum(n, 1).astype(jnp.float32) / max_exact)
                         / math.log(MAX_DISTANCE / max_exact) * (half - max_exact)).astype(jnp.int32)
    large = jnp.minimum(large, half - 1)
    return ret + jnp.where(n < max_exact, n, large)


def window_gqa(q, k, v, rel_bias, sink):
    B, S = q.shape[0], q.shape[1]
    nb = S // BLOCK
    qb = q.reshape(B, nb, BLOCK, A_KV_HEADS, A_GROUP, HEAD_DIM)

    def neighbours(t):
        t = t.reshape(B, nb, BLOCK, A_KV_HEADS, HEAD_DIM)
        tp = jnp.pad(t, ((0, 0), (1, 1), (0, 0), (0, 0), (0, 0)))
        return jnp.concatenate([tp[:, :-2], tp[:, 1:-1], tp[:, 2:]], axis=2)

    kw, vw = neighbours(k), neighbours(v)
    rel = (jnp.arange(3 * BLOCK) - BLOCK)[None, :] - jnp.arange(BLOCK)[:, None]
    band = jnp.abs(rel) <= WINDOW
    kblk = jnp.arange(nb)[:, None] + (jnp.arange(3 * BLOCK) // BLOCK)[None, :] - 1
    valid = (kblk >= 0) & (kblk < nb)
    mask = band[None] & valid[:, None, :]
    bias = rel_bias[t5_bucket(rel)].astype(jnp.float32)
    bias = bias.transpose(2, 0, 1).reshape(A_KV_HEADS, A_GROUP, BLOCK, 3 * BLOCK)
    s = jnp.einsum('bnqgrd,bnkgd->bngrqk', qb, kw).astype(jnp.float32) * (HEAD_DIM ** -0.5) + bias
    s = jnp.where(mask[None, :, None, None], s, NEG)
    sink_col = jnp.broadcast_to(sink.astype(jnp.float32).reshape(1, 1, A_KV_HEADS, A_GROUP, 1, 1), s.shape[:-1] + (1,))
    p = jax.nn.softmax(jnp.concatenate([s, sink_col], axis=-1), axis=-1)[..., :-1]
    o = jnp.einsum('bngrqk,bnkgd->bnqgrd', p.astype(v.dtype), vw)
    return o.reshape(B, S, A_HEADS * HEAD_DIM)


def mla_attend(q_nope, q_rope, k_nope, k_rope, v):
    B, S = q_nope.shape[0], q_nope.shape[1]
    nq = S // Q_BLOCK
    scale = (QK_NOPE + QK_ROPE) ** -0.5
    qn = jnp.moveaxis(q_nope.reshape(B, nq, Q_BLOCK, B_HEADS, QK_NOPE), 1, 0)
    qr = jnp.moveaxis(q_rope.reshape(B, nq, Q_BLOCK, B_HEADS, QK_ROPE), 1, 0)

    def block(args):
        qn_b, qr_b = args
        s = (jnp.einsum('bqhd,bkhd->bhqk', qn_b, k_nope) + jnp.einsum('bqhd,bkd->bhqk', qr_b, k_rope)).astype(jnp.float32) * scale
        p = jax.nn.softmax(s, axis=-1)
        return jnp.einsum('bhqk,bkhd->bqhd', p.astype(v.dtype), v)

    o = lax.map(block, (qn, qr))
    return jnp.moveaxis(o, 0, 1).reshape(B, S, B_HEADS * V_DIM)


def cross_attend(q, k, v):
    B, S = q.shape[0], q.shape[1]
    s = jnp.einsum('bqhd,bkhd->bhqk', q, k).astype(jnp.float32) * (HEAD_DIM ** -0.5)
    p = jax.nn.softmax(s, axis=-1)
    return jnp.einsum('bhqk,bkhd->bqhd', p.astype(v.dtype), v).reshape(B, S, C_HEADS * HEAD_DIM)


def encoder_layer(x, mem, w_in, rel_bias, sink, q_norm_g, w_uq, kv_norm_g, w_ukv, w_mem_kv,
                  w_gate, b_gate, w_br_a, w_br_b, w_br_c, w_o, ln1_g, ln1_b,
                  w_ffn_in, w_ffn_down, ln2_g, ln2_b):
    B, S, D = x.shape
    splits = [int(c) for c in np.cumsum(IN_WIDTHS)[:-1]]
    qa, ka, va, cq, ckv, kr, qc = jnp.split(x @ w_in, splits, axis=-1)

    a_out = window_gqa(qa.reshape(B, S, A_HEADS, HEAD_DIM), ka.reshape(B, S, A_KV_HEADS, HEAD_DIM),
                       va.reshape(B, S, A_KV_HEADS, HEAD_DIM), rel_bias, sink)

    cos, sin = rope_tables(S)
    qb = (rms_norm(cq, q_norm_g) @ w_uq).reshape(B, S, B_HEADS, QK_NOPE + QK_ROPE)
    q_nope = qb[..., :QK_NOPE]
    q_rope = apply_rope(qb[..., QK_NOPE:], cos[:, None, :], sin[:, None, :])
    kvb = (rms_norm(ckv, kv_norm_g) @ w_ukv).reshape(B, S, B_HEADS, QK_NOPE + V_DIM)
    k_nope, v_b = kvb[..., :QK_NOPE], kvb[..., QK_NOPE:]
    k_rope = apply_rope(kr, cos, sin)
    b_out = mla_attend(q_nope, q_rope, k_nope, k_rope, v_b)

    mkv = (mem @ w_mem_kv).reshape(B, N_MEM, 2, C_HEADS, HEAD_DIM)
    c_out = cross_attend(qc.reshape(B, S, C_HEADS, HEAD_DIM), mkv[:, :, 0], mkv[:, :, 1])

    g = jax.nn.sigmoid((x @ w_gate + b_gate).astype(jnp.float32)).astype(x.dtype).reshape(B, S, N_BRANCH, D)
    merged = g[:, :, 0] * (a_out @ w_br_a) + g[:, :, 1] * (b_out @ w_br_b) + g[:, :, 2] * (c_out @ w_br_c)
    h = layer_norm(ALPHA * x + merged @ w_o, ln1_g, ln1_b)

    gate, up = jnp.split(h @ w_ffn_in, 2, axis=-1)
    f = (jax.nn.silu(gate) * up) @ w_ffn_down
    return layer_norm(ALPHA * h + f, ln2_g, ln2_b)


def setup_inputs(seed: int = 0) -> dict:
    key = jax.random.key(seed)
    ks = iter(jax.random.split(key, 40))
    f32 = jnp.float32

    def nrm(shape, scale):
        return jax.random.normal(next(ks), shape, f32) * scale

    L, D = DEPTH, D_MODEL
    sd = D ** -0.5
    w_in_parts = []
    for i, w in enumerate(IN_WIDTHS):
        s = sd * BETA if i == 2 else sd
        w_in_parts.append(nrm((L, D, w), s))
    w_in = jnp.concatenate(w_in_parts, axis=-1)
    w_ukv = jnp.concatenate([nrm((L, KV_LORA, B_HEADS, 1, QK_NOPE), KV_LORA ** -0.5),
                             nrm((L, KV_LORA, B_HEADS, 1, V_DIM), KV_LORA ** -0.5 * BETA)], axis=3
                            ).reshape(L, KV_LORA, B_HEADS * (QK_NOPE + V_DIM))
    w_mem_kv = jnp.concatenate([nrm((L, D, C_HEADS * HEAD_DIM), sd),
                                nrm((L, D, C_HEADS * HEAD_DIM), sd * BETA)], axis=-1)
    return {
        "x_prompt": nrm((BATCH, SEQ, D), 1.0),
        "x_sample": nrm((DEC_BATCH, DEC_SEQ, D), 1.0),
        "mem_prompt": nrm((BATCH, N_MEM, D), 1.0),
        "mem_sample": nrm((DEC_BATCH, N_MEM, D), 1.0),
        "w_in": w_in,
        "rel_bias": nrm((N_BUCKETS, A_HEADS), 0.1),
        "sink": nrm((L, A_HEADS), 0.5),
        "q_norm_g": 1.0 + nrm((L, Q_LORA), 0.01),
        "w_uq": nrm((L, Q_LORA, B_HEADS * (QK_NOPE + QK_ROPE)), Q_LORA ** -0.5),
        "kv_norm_g": 1.0 + nrm((L, KV_LORA), 0.01),
        "w_ukv": w_ukv,
        "w_mem_kv": w_mem_kv,
        "w_gate": nrm((L, D, N_BRANCH * D), sd),
        "b_gate": nrm((L, N_BRANCH * D), 0.01),
        "w_br_a": nrm((L, A_HEADS * HEAD_DIM, D), (A_HEADS * HEAD_DIM) ** -0.5),
        "w_br_b": nrm((L, B_HEADS * V_DIM, D), (B_HEADS * V_DIM) ** -0.5),
        "w_br_c": nrm((L, C_HEADS * HEAD_DIM, D), (C_HEADS * HEAD_DIM) ** -0.5),
        "w_o": nrm((L, D, D), sd * BETA),
        "ln1_g": 1.0 + nrm((L, D), 0.01),
        "ln1_b": nrm((L, D), 0.01),
        "w_ffn_in": nrm((L, D, 2 * D_FF), sd * BETA),
        "w_ffn_down": nrm((L, D_FF, D), D_FF ** -0.5 * BETA),
        "ln2_g": 1.0 + nrm((L, D), 0.01),
        "ln2_b": nrm((L, D), 0.01),
    }


def reference(x_prompt, x_sample, mem_prompt, mem_sample, w_in, rel_bias, sink, q_norm_g, w_uq,
              kv_norm_g, w_ukv, w_mem_kv, w_gate, b_gate, w_br_a, w_br_b, w_br_c, w_o,
              ln1_g, ln1_b, w_ffn_in, w_ffn_down, ln2_g, ln2_b):
    def run(x, mem):
        for l in range(DEPTH):
            x = encoder_layer(x, mem, w_in[l], rel_bias, sink[l], q_norm_g[l], w_uq[l], kv_norm_g[l],
                              w_ukv[l], w_mem_kv[l], w_gate[l], b_gate[l], w_br_a[l], w_br_b[l],
                              w_br_c[l], w_o[l], ln1_g[l], ln1_b[l], w_ffn_in[l], w_ffn_down[l],
                              ln2_g[l], ln2_b[l])
        return x

    y_prompt = run(x_prompt, mem_prompt)
    y_sample = run(x_sample, mem_sample)
    return (y_prompt, y_sample)
```

```python
import math
import os
import numpy as np
import concourse.bass as bass
import concourse.mybir as mybir
from concourse.bass_utils import run_bass_kernel_spmd

F32 = mybir.dt.float32
BF16 = mybir.dt.bfloat16
AF = mybir.ActivationFunctionType
ALU = mybir.AluOpType

D = 2048
DC = 16
T = 512
HD = 128
A_HEADS, A_KV = 6, 2
B_HEADS = 6
C_HEADS = 4
N_MEM = 256
D_FF = 5632
FC = 44
ALPHA = 2.0 ** 0.25
LN_EPS = 1e-5
RMS_EPS = 1e-6
NEG = -1e30
UNIT = 4096
NWBUF = 3
NSLOT = 8


class Tl:
    __slots__ = ("w", "r", "rd", "const")

    def __init__(self, const=False):
        self.w = None
        self.r = {}
        self.rd = []
        self.const = const


class Op:
    __slots__ = ("eng", "name", "args", "kw", "deps", "sig", "val", "dma", "semkey", "idx")


class Sched:
    ENGS = ("pe", "act", "dve", "pool", "sp")

    def __init__(self):
        self.q = {e: [] for e in self.ENGS}
        self.ndma = {e: 0 for e in self.ENGS}
        self.slot_last = {}
        self.slot_cnt = {}
        self.fence = []
        self.fence_pending = set()
        self.nops = 0

    def op(self, eng, name, args=(), kw=None, reads=(), writes=(), dma=False):
        o = Op()
        o.eng, o.name, o.args, o.kw = eng, name, args, (kw or {})
        o.dma, o.sig, o.val = dma, False, 0
        deps = []
        for t in reads:
            if t.w is not None:
                deps.append(t.w)
        for t in writes:
            if t.w is not None:
                deps.append(t.w)
            deps.extend(t.r.values())
            deps.extend(t.rd)
        if dma:
            k = self.ndma[eng]
            self.ndma[eng] = k + 1
            s = (eng, k % NSLOT)
            o.semkey = s
            prev = self.slot_last.get(s)
            if prev is not None:
                deps.append(prev)
            self.slot_last[s] = o
            self.slot_cnt[s] = self.slot_cnt.get(s, 0) + 1
            o.val = 16 * self.slot_cnt[s]
        else:
            o.semkey = eng
        if eng in self.fence_pending:
            deps.extend(self.fence)
            self.fence_pending.discard(eng)
        dd = []
        seen = set()
        for d in deps:
            if d is o or id(d) in seen:
                continue
            seen.add(id(d))
            if (not d.dma) and (not dma) and d.eng == "pe" and eng == "pe":
                continue
            dd.append(d)
        best = {}
        for d in dd:
            b = best.get(d.semkey)
            if b is None or d.idx > b.idx:
                best[d.semkey] = d
        o.deps = list(best.values())
        for d in o.deps:
            d.sig = True
        o.idx = len(self.q[eng])
        for t in reads:
            if t.const:
                continue
            if dma:
                t.rd.append(o)
            else:
                t.r[eng] = o
        for t in writes:
            t.w = o
            t.r = {}
            t.rd = []
        self.q[eng].append(o)
        self.nops += 1
        return o

    def barrier(self):
        deps = []
        for e in self.ENGS:
            for o in reversed(self.q[e]):
                if not o.dma:
                    o.sig = True
                    deps.append(o)
                    break
        deps.extend(self.slot_last.values())
        self.fence = deps
        self.fence_pending = set(self.ENGS)

    def finish(self):
        for e in self.ENGS:
            lasts = [o for (s, o) in self.slot_last.items() if s[0] == e]
            if lasts:
                f = Op()
                f.eng, f.name, f.args, f.kw = e, None, (), {}
                f.dma, f.sig, f.val, f.semkey = False, False, 0, e
                f.idx = len(self.q[e])
                f.deps = lasts
                self.q[e].append(f)

    def emit(self, nc, block, sems):
        for e in self.ENGS:
            c = 0
            for o in self.q[e]:
                if o.dma:
                    continue
                if o.sig:
                    c += 1
                o.val = c
        handles = {"pe": block.tensor, "act": block.scalar, "dve": block.vector,
                   "pool": block.gpsimd, "sp": block.sync}
        for e in self.ENGS:
            ops = self.q[e]
            if not ops:
                continue

            def body(h, ops=ops):
                waited = {}
                for o in ops:
                    need = {}
                    for d in o.deps:
                        if need.get(d.semkey, 0) < d.val:
                            need[d.semkey] = d.val
                    for k, v in need.items():
                        if waited.get(k, 0) < v:
                            h.wait_ge(sems[k], v)
                            waited[k] = v
                    if o.name is None:
                        continue
                    ins = getattr(h, o.name)(*o.args, **o.kw)
                    if o.dma:
                        ins.then_inc(sems[o.semkey], 16)
                    elif o.sig:
                        ins.then_inc(sems[o.semkey], 1)

            handles[e](body)


def plan_units():
    U = []
    for i in range(3):
        U.append([("w_in", 0, 16, i * 256, 256, None)])
    U.append([("w_in", 0, 16, 768, 256, None)])
    U.append([("w_in", 0, 16, 1024, 256, None)])
    for i in range(2):
        U.append([("w_in", 0, 16, 1280 + i * 256, 256, None)])
    for i in range(2):
        U.append([("w_in", 0, 16, 2368 + i * 256, 256, None)])
    for half in range(2):
        segs = []
        for h in range(3 * half, 3 * half + 3):
            segs.append(("w_uq", 0, 4, h * 192, 128, "gq"))
            segs.append(("w_uq", 0, 4, h * 192 + 128, 128, "gq_rope"))
        U.append(segs)
    for c in range(DC):
        U.append([("w_gate", 0, 16, c * 128, 128, None), ("w_gate", 0, 16, D + c * 128, 128, None)])
        U.append([("w_gate", 0, 16, 2 * D + c * 128, 128, None), ("w_br_a", 0, 6, c * 128, 128, None),
                  ("w_br_b", 0, 6, c * 128, 128, None), ("w_br_c", 0, 4, c * 128, 128, None)])
    for i in range(8):
        U.append([("w_o", 0, 16, i * 256, 256, None)])
    for j in range(FC):
        U.append([("w_ffn_in", 0, 16, j * 128, 128, None), ("w_ffn_in", 0, 16, D_FF + j * 128, 128, None)])
    for m in range(DC):
        for half in range(2):
            U.append([("w_ffn_down", half * 22 * 128, 22, m * 128, 128, None)])
    return U


WNAMES = ["w_in", "w_uq", "w_ukv", "w_mem_kv", "w_gate", "w_br_a", "w_br_b", "w_br_c", "w_o",
          "w_ffn_in", "w_ffn_down"]
WSHAPES = {"w_in": (2048, 2880), "w_uq": (512, 1152), "w_ukv": (512, 1536), "w_mem_kv": (2048, 1024),
           "w_gate": (2048, 6144), "w_br_a": (768, 2048), "w_br_b": (768, 2048), "w_br_c": (512, 2048),
           "w_o": (2048, 2048), "w_ffn_in": (2048, 11264), "w_ffn_down": (5632, 2048)}


def _rs(n):
    names = "abcdefg"[:n]
    return names


def build(P_SEQ, S_SEQ, OWN_P, OWN_S, dbg=()):
    nc = bass.Bass("TRN2", target_bir_lowering=False)
    S = Sched()
    dbg = set(dbg)

    def din(name, shape, dt=F32):
        return nc.dram_tensor(name, list(shape), dt, kind="ExternalInput").ap()

    xin = {"p": din("xp", (P_SEQ, D)), "s": din("xs", (S_SEQ, D))}
    memin = {"p": din("mem_p", (N_MEM, D)), "s": din("mem_s", (N_MEM, D))}
    csin = {"p": din("cs_p", (128, P_SEQ)), "s": din("cs_s", (128, S_SEQ))}
    W = {n: din(n, WSHAPES[n]) for n in WNAMES}
    rel_bias = din("rel_bias", (32, 6))
    sink = din("sink", (128, 6))
    q_norm_g = din("q_norm_g", (128, 4))
    kv_norm_g = din("kv_norm_g", (128, 4))
    b_gate = din("b_gate", (128, 48))
    lnp = {n: din(n, (128, 16)) for n in ("ln1_g", "ln1_b", "ln2_g", "ln2_b")}
    flags_in = din("flags", (128, 4))
    ident_in = din("ident", (128, 128))
    si_in = din("stack_ident", (128, 64))
    sel_in = din("t5_sel", (32, 3 * 128 * 128))
    wmask_in = din("win_mask", (128, 3 * 128))
    yout = {"p": nc.dram_tensor("y_p", [OWN_P, D], F32, kind="ExternalOutput").ap(),
            "s": nc.dram_tensor("y_s", [OWN_S, D], F32, kind="ExternalOutput").ap()}
    SEQ = {"p": P_SEQ, "s": S_SEQ}
    OWN = {"p": OWN_P, "s": OWN_S}

    units = plan_units()
    NU = len(units)
    wst = nc.dram_tensor("wst", [NU, 128, UNIT], BF16, kind="Internal").ap()
    KnT = {k: nc.dram_tensor("knT_" + k, [B_HEADS, 128, SEQ[k]], BF16, kind="Internal").ap() for k in "ps"}
    KrT = {k: nc.dram_tensor("krT_" + k, [64, SEQ[k]], BF16, kind="Internal").ap() for k in "ps"}
    Vsc = {k: nc.dram_tensor("v_" + k, [B_HEADS, 128, SEQ[k] // 128, 128], BF16, kind="Internal").ap() for k in "ps"}
    biasd = nc.dram_tensor("biasd", [6, 3 * 128 * 128], F32, kind="Internal").ap()
    dbg_out = {}

    import contextlib
    with contextlib.ExitStack() as es:
        def sb(name, shape, dt):
            return es.enter_context(nc.sbuf_tensor(name, list(shape), dt))

        ARENA_F32 = 45 * 1024
        arena = sb("arena", (128, ARENA_F32), F32)
        psums = [es.enter_context(nc.psum_tensor("ps%d" % i, [128, 512], F32)) for i in range(8)]
        PS = [Tl() for _ in range(8)]

        ident32 = sb("ident32", (128, 128), F32)
        ident16 = sb("ident16", (128, 128), BF16)
        ones16 = sb("ones16", (128, 128), BF16)
        ones32 = sb("ones32", (128, 128), F32)
        si16 = sb("si16", (128, 64), BF16)
        si32 = sb("si32", (128, 64), F32)
        biasT = sb("biasT", (128, 6, 3, 128), F32)
        wmask = sb("wmask", (128, 3, 128), F32)
        se = sb("se", (128, 6), F32)
        flags = sb("flags_sb", (128, 4), F32)
        bgc = sb("bgc", (128, 48), F32)
        lncol = {n: sb(n + "_c", (128, 16), F32) for n in lnp}
        gqc = sb("gqc", (128, 4), F32)
        gkvc = sb("gkvc", (128, 4), F32)
        kcT = {k: sb("kcT_" + k, (128, C_HEADS, N_MEM), BF16) for k in "ps"}
        vc = {k: sb("vc_" + k, (128, 2, C_HEADS * HD), BF16) for k in "ps"}
        eps_ln = sb("eps_ln", (128, 1), F32)
        eps_rms = sb("eps_rms", (128, 1), F32)
        CONST = Tl(const=True)
        KC_T = {k: Tl() for k in "ps"}
        VC_T = {k: Tl() for k in "ps"}

        class Arena:
            def __init__(self):
                self.off = 0

            def reset(self, off=0):
                self.off = off

            def take(self, free, dt):
                n = int(np.prod(free))
                nb = n * (4 if dt == F32 else 2)
                nw = (nb + 63) // 64 * 16
                assert self.off + nw <= ARENA_F32, ("arena overflow", self.off, nw)
                ap = arena[:, self.off:self.off + nw]
                self.off += nw
                if dt != F32:
                    ap = ap.bitcast(dt)
                ap = ap[:, 0:n]
                if len(free) > 1:
                    names = _rs(len(free))
                    pat = "p (" + " ".join(names) + ") -> p " + " ".join(names)
                    ap = ap.rearrange(pat, **{names[i]: int(free[i]) for i in range(len(free))})
                return ap

        AR = Arena()

        psrr = [0]

        def ps_next(pool=(0, 1, 2, 3, 4)):
            i = pool[psrr[0] % len(pool)]
            psrr[0] += 1
            return i

        def mm(pi, M, N, lhsT, rhs, start, stop, reads, n0=0):
            S.op("pe", "matmul", (psums[pi][0:M, n0:n0 + N],), dict(lhsT=lhsT, rhs=rhs, start=start, stop=stop),
                 reads=reads, writes=[PS[pi]])

        def tr(pi, n0, in_ap, reads):
            S.op("pe", "transpose", (psums[pi][:, n0:n0 + 128], in_ap, ident32[:, :]), {},
                 reads=list(reads) + [CONST], writes=[PS[pi]])

        psums16 = [p[:, :].bitcast(BF16) for p in psums]

        def tr16(pi, n0, in_ap, reads):
            S.op("pe", "transpose", (psums16[pi][:, n0:n0 + 128], in_ap, ident16[:, :]), {},
                 reads=list(reads) + [CONST], writes=[PS[pi]])

        def act(out, in_, func, reads, writes, bias=None, scale=None):
            kw = {}
            if bias is not None:
                kw["bias"] = bias
            if scale is not None:
                kw["scale"] = scale
            S.op("act", "activation", (out, in_, func), kw, reads=reads, writes=writes)

        def tt(eng, out, a, b, op, reads, writes):
            S.op(eng, "tensor_tensor", (out, a, b, op), {}, reads=reads, writes=writes)

        def ts(eng, out, a, s1, s2, op0, op1, reads, writes):
            if op1 is None:
                S.op(eng, "tensor_scalar", (out, a, s1, None, op0), {}, reads=reads, writes=writes)
            else:
                S.op(eng, "tensor_scalar", (out, a, s1, s2, op0, op1), {}, reads=reads, writes=writes)

        def stt(out, a, s, b, op0, op1, reads, writes):
            S.op("dve", "scalar_tensor_tensor", (out, a, s, b, op0, op1), {}, reads=reads, writes=writes)

        def cp(eng, out, in_, reads, writes):
            if eng == "act":
                S.op("act", "copy", (out, in_), {}, reads=reads, writes=writes)
            else:
                S.op(eng, "tensor_copy", (out, in_), {}, reads=reads, writes=writes)

        def dma(stream, out, in_, reads, writes):
            if stream == "pool":
                stream = os.environ.get("KPOOLQ", "pool")
            return S.op(stream, "dma_start", (), dict(out=out, in_=in_), reads=reads, writes=writes, dma=True)

        evrr = [0]

        def ev_eng():
            evrr[0] += 1
            return "act" if evrr[0] % 2 else "dve"

        def dbg_dump(name, ap, shape, dt, reads):
            if name not in dbg or name in dbg_out:
                return
            o = nc.dram_tensor("dbg_" + name, list(shape), dt, kind="ExternalOutput").ap()
            dbg_out[name] = o
            dma("pool", o, ap, reads, [])

        STOP = os.environ.get("KSTOP", "")
        SKIP = os.environ.get("KSKIP", "").split(",")
        c0 = Tl()
        dma("sp", ident32[:, :], ident_in, [], [c0])
        dma("sp", si32[:, :], si_in, [], [c0])
        dma("sp", wmask[:, :, :], wmask_in.rearrange("p (j q) -> p j q", j=3), [], [c0])
        dma("sp", flags[:, :], flags_in, [], [c0])
        dma("sp", se[:, :], sink, [], [c0])
        dma("sp", bgc[:, :], b_gate, [], [c0])
        for n in lnp:
            dma("sp", lncol[n][:, :], lnp[n], [], [c0])
        dma("sp", gqc[:, :], q_norm_g, [], [c0])
        dma("sp", gkvc[:, :], kv_norm_g, [], [c0])
        S.op("dve", "memset", (ones16[:, :], 1.0), {}, writes=[c0])
        S.op("dve", "memset", (ones32[:, :], 1.0), {}, writes=[c0])
        S.op("dve", "memset", (eps_ln[:, :], LN_EPS), {}, writes=[c0])
        S.op("dve", "memset", (eps_rms[:, :], RMS_EPS), {}, writes=[c0])
        cp("dve", si16[:, :], si32[:, :], [c0], [c0])
        cp("dve", ident16[:, :], ident32[:, :], [c0], [c0])
        act(se[:, :], se[:, :], AF.Exp, [c0], [c0])

        AR.reset()
        PIECE = 12288
        relb = AR.take((6,), F32)
        selb = AR.take((PIECE,), F32)
        bflat = AR.take((PIECE,), F32)
        t_rel, t_sel, t_bf, t_bd = Tl(), Tl(), Tl(), Tl()
        dma("sp", relb[0:32, :], rel_bias, [], [t_rel])
        for pc in range(4):
            dma("sp", selb[0:32, :], sel_in[:, pc * PIECE:(pc + 1) * PIECE], [], [t_sel])
            for i in range(PIECE // 512):
                pi = ps_next()
                mm(pi, 6, 512, relb[0:32, 0:6], selb[0:32, i * 512:(i + 1) * 512], True, True, [t_rel, t_sel])
                cp(ev_eng(), bflat[0:6, i * 512:(i + 1) * 512], psums[pi][0:6, :], [PS[pi]], [t_bf])
            dma("sp", biasd[:, pc * PIECE:(pc + 1) * PIECE], bflat[0:6, :], [t_bf], [t_bd])
        for h in range(6):
            dma("sp", biasT[:, h, :, :], biasd[h, :].rearrange("(j k q) -> k j q", j=3, k=128), [t_bd], [c0])
        for h in range(6):
            tt("dve", biasT[:, h, :, :], biasT[:, h, :, :], wmask[:, :, :], ALU.add, [c0], [c0])
        S.barrier()

        AR.reset()
        wckv = AR.take((16, 640), BF16)
        wukv = AR.take((4, 1536), BF16)
        wmem = AR.take((16, 1024), BF16)
        P1W = Tl()
        p1_base = AR.off
        st32 = [AR.take((UNIT,), F32) for _ in range(3)]
        st16 = [AR.take((UNIT,), BF16) for _ in range(3)]
        T32 = [Tl() for _ in range(3)]
        T16 = [Tl() for _ in range(3)]
        cvrr = [0]

        def cast_eng():
            cvrr[0] += 1
            return ("dve", "pool", "act")[cvrr[0] % 3]

        def conv_seg(seg, dst3, dst_t, slot):
            wname, row0, kcs, col0, ncols, special = seg
            s32 = st32[slot][:, 0:kcs * ncols].rearrange("p (k n) -> p k n", k=kcs)
            ncl = 64 if special in ("gq_rope", "rope") else ncols
            src = W[wname][row0:row0 + kcs * 128, col0:col0 + ncl].rearrange("(k p) n -> p k n", p=128)
            dma("sp", s32[:, :, 0:ncl], src, [], [T32[slot]])
            if special is None:
                cp(cast_eng(), dst3, s32, [T32[slot]], [dst_t])
            elif special in ("gq", "gkv"):
                gc = gqc if special == "gq" else gkvc
                for k in range(kcs):
                    ts("dve", dst3[:, k, :], s32[:, k, :], gc[:, k:k + 1], None, ALU.mult, None,
                       [T32[slot], c0], [dst_t])
            elif special in ("gq_rope", "rope"):
                for k in range(kcs):
                    if special == "gq_rope":
                        ts("dve", dst3[:, k, 0:64], s32[:, k, 0:64], gqc[:, k:k + 1], None, ALU.mult, None,
                           [T32[slot], c0], [dst_t])
                        ts("dve", dst3[:, k, 64:96], s32[:, k, 32:64], gqc[:, k:k + 1], -1.0, ALU.mult, ALU.mult,
                           [T32[slot], c0], [dst_t])
                        ts("dve", dst3[:, k, 96:128], s32[:, k, 0:32], gqc[:, k:k + 1], None, ALU.mult, None,
                           [T32[slot], c0], [dst_t])
                    else:
                        cp("dve", dst3[:, k, 0:64], s32[:, k, 0:64], [T32[slot]], [dst_t])
                        ts("dve", dst3[:, k, 64:96], s32[:, k, 32:64], -1.0, None, ALU.mult, None,
                           [T32[slot]], [dst_t])
                        cp("dve", dst3[:, k, 96:128], s32[:, k, 0:32], [T32[slot]], [dst_t])

        slot_rr = [0]

        def nslot():
            slot_rr[0] += 1
            return slot_rr[0] % 3

        for i in range(2):
            conv_seg(("w_in", 0, 16, 1792 + i * 256, 256, None), wckv[:, :, i * 256:(i + 1) * 256], P1W, nslot())
        conv_seg(("w_in", 0, 16, 2304, 128, "rope"), wckv[:, :, 512:640], P1W, nslot())
        for h in range(6):
            conv_seg(("w_ukv", 0, 4, h * 256, 128, "gkv"), wukv[:, :, h * 128:(h + 1) * 128], P1W, nslot())
            conv_seg(("w_ukv", 0, 4, h * 256 + 128, 128, "gkv"), wukv[:, :, 768 + h * 128:768 + (h + 1) * 128],
                     P1W, nslot())
        for i in range(4):
            conv_seg(("w_mem_kv", 0, 16, i * 256, 256, None), wmem[:, :, i * 256:(i + 1) * 256], P1W, nslot())
        for u, segs in enumerate(units):
            b = u % 3
            off = 0
            for seg in segs:
                kcs, ncols = seg[2], seg[4]
                dst3 = st16[b][:, off:off + kcs * ncols].rearrange("p (k n) -> p k n", k=kcs)
                conv_seg(seg, dst3, T16[b], nslot())
                off += kcs * ncols
            dma("act", wst[u, :, 0:off], st16[b][:, 0:off], [T16[b]], [])
        S.barrier()

        if STOP == "p0":
            SEQ = {"p": 0, "s": 0}
            OWN = {"p": 0, "s": 0}
            memin = {}
        AR.reset(p1_base)
        xrow = AR.take((4, D), F32)
        XR = [Tl() for _ in range(4)]
        hi16 = AR.take((4, D), BF16)
        HI = [Tl() for _ in range(4)]
        xT16 = AR.take((DC, T), BF16)
        XT = [Tl() for _ in range(DC)]
        ckvT = AR.take((4, T), BF16)
        CK = [Tl() for _ in range(4)]
        sq16 = AR.take((4, T), BF16)
        SQ = [Tl() for _ in range(4)]
        krc = AR.take((T,), BF16)
        KRC = Tl()
        krT = [AR.take((T,), BF16) for _ in range(2)]
        KRT = [Tl() for _ in range(2)]
        knst = [AR.take((B_HEADS, T), BF16) for _ in range(2)]
        KNS = [Tl() for _ in range(2)]
        vst = [AR.take((B_HEADS, 4, HD), BF16) for _ in range(2)]
        VS = [Tl() for _ in range(2)]
        cs2 = [AR.take((T,), F32) for _ in range(2)]
        CS = [Tl() for _ in range(2)]
        rstd_b = AR.take((T,), F32)
        RB = Tl()
        rstd_c = AR.take((4,), F32)
        RC = Tl()

        def load_x(stream, src_rows_ap, nblk, b0=0):
            for b in range(nblk):
                dma(stream, xrow[:, b0 + b, :], src_rows_ap[b * 128:(b + 1) * 128, :], [], [XR[b0 + b]])

        def transpose_x(nblk, with_lo=False):
            for b in range(nblk):
                cp(("pool", "dve")[b % 2], hi16[:, b, :], xrow[:, b, :], [XR[b]], [HI[b]])
                if with_lo:
                    tt("dve", lo16[:, b, :], xrow[:, b, :], hi16[:, b, :], ALU.subtract, [XR[b], HI[b]], [LO[b]])
            n = nblk * 128
            for dc in range(DC):
                pi = ps_next()
                for b in range(nblk):
                    tr16(pi, b * 128, hi16[:, b, dc * 128:(dc + 1) * 128], [HI[b]])
                if with_lo:
                    for b in range(nblk):
                        tr16(pi, 512 + b * 128, lo16[:, b, dc * 128:(dc + 1) * 128], [LO[b]])
                cp("act" if with_lo else ev_eng(), xT16[:, dc, 0:n], psums16[pi][:, 0:n], [PS[pi]], [XT[dc]])
                if with_lo:
                    tq = dc % 2
                    act(hT32[:, dc, :], psums16[pi][:, 0:512], AF.Copy, [PS[pi]], [HT[dc]], scale=ALPHA)
                    act(sm32[tq][:, :], psums16[pi][:, 512:1024], AF.Copy, [PS[pi]], [SM[tq]], scale=ALPHA)
                    tt("dve", hT32[:, dc, :], hT32[:, dc, :], sm32[tq][:, :], ALU.add, [HT[dc], SM[tq]], [HT[dc]])

        def rstd_from(pi, n, out_ap, out_t, d_in, eps_t):
            act(out_ap, psums[pi][:, 0:n], AF.Sqrt, [PS[pi], c0], [out_t], bias=eps_t[:, 0:1], scale=1.0 / d_in)
            S.op("dve", "reciprocal", (out_ap, out_ap), {}, reads=[out_t], writes=[out_t])

        for k in ([] if "mem" in SKIP else [kk for kk in memin if ("mem" + kk) not in SKIP]):
            load_x("act", memin[k], 2)
            transpose_x(2)
            for h in ([] if "memkc" in SKIP else range(C_HEADS)):
                pi = ps_next()
                for kc in range(DC):
                    mm(pi, 128, N_MEM, wmem[:, kc, h * 128:(h + 1) * 128], xT16[:, kc, 0:N_MEM], kc == 0, kc == DC - 1,
                       [P1W, XT[kc]])
                cp(ev_eng(), kcT[k][:, h, :], psums[pi][:, 0:N_MEM], [PS[pi]], [KC_T[k]])
            for mb in ([] if "memvc" in SKIP else range(2)):
                pi = ps_next()
                for kc in range(DC):
                    mm(pi, 128, 512, xT16[:, kc, mb * 128:(mb + 1) * 128], wmem[:, kc, 512:1024], kc == 0, kc == DC - 1,
                       [P1W, XT[kc]])
                cp(ev_eng(), vc[k][:, mb, :], psums[pi][:, :], [PS[pi]], [VC_T[k]])

        NOBAR = os.environ.get("KNOBAR", "p1tile").split(",")

        def SB(name):
            if name not in NOBAR and "all" not in NOBAR:
                S.barrier()

        SB("mem")
        tcount = 0
        p1tiles = [(k, t) for k in "ps" for t in range(SEQ[k] // T)]
        if STOP == "p1a":
            p1tiles = []
            tiles_override = True
        if STOP.startswith("p1n"):
            p1tiles = p1tiles[:int(STOP[3:])]
        XQ = os.environ.get("KXQ", "sp")
        if p1tiles:
            load_x(XQ, xin[p1tiles[0][0]][0:T, :], 4)
        for p1i, (k, t) in enumerate(p1tiles):
            if True:
                b2 = tcount % 2
                tcount += 1
                if "cs" not in SKIP:
                    dma(XQ, cs2[b2][:, :], csin[k][:, t * T:(t + 1) * T], [], [CS[b2]])
                transpose_x(4)
                if p1i + 1 < len(p1tiles):
                    k2, t2 = p1tiles[p1i + 1]
                    load_x(XQ, xin[k2][t2 * T:(t2 + 1) * T, :], 4)
                for m in range(4):
                    pi = ps_next()
                    for kc in range(DC):
                        mm(pi, 128, T, wckv[:, kc, m * 128:(m + 1) * 128], xT16[:, kc, :], kc == 0, kc == DC - 1,
                           [P1W, XT[kc]])
                    cp("dve", ckvT[:, m, :], psums[pi][:, :], [PS[pi]], [CK[m]])
                    if "sq" not in SKIP:
                        act(sq16[:, m, :], psums[pi][:, :], AF.Square, [PS[pi], CK[m]], [SQ[m]])
                if 'rope' not in SKIP:
                    pi = ps_next()
                    for kc in range(DC):
                        mm(pi, 128, T, wckv[:, kc, 512:640], xT16[:, kc, :], kc == 0, kc == DC - 1, [P1W, XT[kc]])
                    tt("dve", krc[:, :], cs2[b2][:, :], psums[pi][:, :], ALU.mult, [PS[pi], CS[b2]], [KRC])
                    pi = ps_next()
                    mm(pi, 64, T, si16[:, :], krc[:, :], True, True, [KRC, c0])
                    cp("act", krT[b2][0:64, :], psums[pi][0:64, :], [PS[pi]], [KRT[b2]])
                    if "kr" not in SKIP:
                        dma("pool", KrT[k][:, t * T:(t + 1) * T], krT[b2][0:64, :], [KRT[b2]], [])
                if 'stats' not in SKIP:
                    pi = ps_next()
                    for m in range(4):
                        mm(pi, 128, T, ones16[:, :], sq16[:, m, :], m == 0, m == 3, [SQ[m], c0])
                    rstd_from(pi, T, rstd_b[:, :], RB, 512.0, eps_rms)
                    pi = ps_next()
                    for b in range(4):
                        for m in range(4):
                            mm(pi, 128, 1, sq16[:, m, b * 128:(b + 1) * 128], ones16[:, 0:1], m == 0, m == 3, [SQ[m], c0],
                               n0=b)
                    rstd_from(pi, 4, rstd_c[:, :], RC, 512.0, eps_rms)
                if 'kn_c' not in SKIP:
                    for h in range(B_HEADS):
                        pi = ps_next()
                        for kc in range(4):
                            mm(pi, 128, T, wukv[:, kc, h * 128:(h + 1) * 128], ckvT[:, kc, :], kc == 0, kc == 3,
                               [P1W, CK[kc]])
                        tt("dve", knst[b2][:, h, :], rstd_b[:, :], psums[pi][:, :], ALU.mult, [PS[pi], RB], [KNS[b2]])
                    if "kn" not in SKIP:
                        dma("pool", KnT[k][:, :, t * T:(t + 1) * T].rearrange("h p n -> p h n"), knst[b2][:, :, :],
                            [KNS[b2]], [])
                if 'vv' not in SKIP:
                    for b in range(4):
                        pa, pb = ps_next(), ps_next()
                        for kc in range(4):
                            mm(pa, 128, 512, ckvT[:, kc, b * 128:(b + 1) * 128], wukv[:, kc, 768:1280], kc == 0, kc == 3,
                               [P1W, CK[kc]])
                        for kc in range(4):
                            mm(pb, 128, 256, ckvT[:, kc, b * 128:(b + 1) * 128], wukv[:, kc, 1280:1536], kc == 0, kc == 3,
                               [P1W, CK[kc]])
                        act(vst[b2][:, 0:4, b, :], psums[pa][:, :].rearrange("p (h d) -> p h d", h=4), AF.Copy,
                            [PS[pa], RC], [VS[b2]], scale=rstd_c[:, b:b + 1])
                        act(vst[b2][:, 4:6, b, :], psums[pb][:, 0:256].rearrange("p (h d) -> p h d", h=2), AF.Copy,
                            [PS[pb], RC], [VS[b2]], scale=rstd_c[:, b:b + 1])
                    if "v" not in SKIP:
                        dma("pool", Vsc[k][:, :, t * 4:(t + 1) * 4, :].rearrange("h p b d -> p h b d"), vst[b2][:, :, :, :],
                            [VS[b2]], [])
                SB("p1tile")
        S.barrier()

        AR.reset()
        hT32 = AR.take((DC, T), F32)
        HT = [Tl() for _ in range(DC)]
        xT16 = AR.take((DC, T), BF16)
        XT = [Tl() for _ in range(DC)]
        wbuf = [AR.take((UNIT,), BF16) for _ in range(NWBUF)]
        WB = [Tl() for _ in range(NWBUF)]
        base_b = AR.off
        xrow = AR.take((4, D), F32)
        XR = [Tl() for _ in range(4)]
        qaT = AR.take((A_HEADS, T), BF16)
        QA = [Tl() for _ in range(A_HEADS)]
        kaT = AR.take((A_KV, 6 * 128), BF16)
        KA = [Tl() for _ in range(A_KV)]
        va = AR.take((6, A_KV * HD), BF16)
        VA = Tl()
        cqT = AR.take((4, T), BF16)
        CQ = [Tl() for _ in range(4)]
        hilo_base = AR.off
        qnT = AR.take((B_HEADS, T), BF16)
        QN = [Tl() for _ in range(B_HEADS)]
        qrT = AR.take((B_HEADS, T), BF16)
        QR = [Tl() for _ in range(B_HEADS)]
        qcT = AR.take((C_HEADS, T), BF16)
        QC = [Tl() for _ in range(C_HEADS)]
        aoT = AR.take((A_HEADS, T), BF16)
        AO = [Tl() for _ in range(A_HEADS)]
        boT = AR.take((B_HEADS, T), BF16)
        BO = [Tl() for _ in range(B_HEADS)]
        coT = AR.take((C_HEADS, T), BF16)
        CO = [Tl() for _ in range(C_HEADS)]
        hilo_end = AR.off
        AR.reset(hilo_base)
        hi16 = AR.take((4, D), BF16)
        lo16 = AR.take((4, D), BF16)
        assert AR.off <= hilo_end, "hi/lo overlay too big"
        AR.reset(hilo_end)
        HI = [Tl() for _ in range(4)]
        LO = [Tl() for _ in range(4)]
        mgT = AR.take((DC, T), BF16)
        MG = [Tl() for _ in range(DC)]
        sm32 = [AR.take((T,), F32) for _ in range(6)]
        SM = [Tl() for _ in range(6)]
        end_attn = AR.off
        AR.reset(base_b)
        KCH = min(2048, P_SEQ, S_SEQ)
        knb = [AR.take((KCH,), BF16) for _ in range(2)]
        krb = [AR.take((KCH,), BF16) for _ in range(2)]
        vb = [AR.take((KCH // 128, HD), BF16) for _ in range(2)]
        KVB = [Tl() for _ in range(2)]
        pT = [AR.take((T,), BF16) for _ in range(3)]
        PT = [Tl() for _ in range(3)]
        acc = [AR.take((T,), F32) for _ in range(2)]
        ACC = [Tl() for _ in range(2)]
        assert AR.off <= base_b + 4 * D, "kv stream overflows xrow region"
        AR.reset(base_b)
        actT = AR.take((FC, T), BF16)
        ystage = [AR.take((D,), F32) for _ in range(2)]
        ylo = AR.take((DC, T), BF16)
        YL = [Tl() for _ in range(DC)]
        fsc = [AR.take((T,), F32) for _ in range(2)]
        FSC = [Tl() for _ in range(2)]
        assert AR.off <= end_attn, ("ffn region", AR.off, end_attn)
        ATT_ALL = XR + QA + KA + [VA] + CQ + QN + QR + QC + AO + BO + CO + MG + SM + KVB + PT + ACC + HI + LO
        ACTT = [Tl() for _ in range(FC)]
        YS = [Tl() for _ in range(2)]
        FFN_ALL = ACTT + YS + YL + FSC

        def fence_region(old, new):
            ws = []
            rs = {}
            rds = []
            for t in old:
                if t.w is not None:
                    ws.append(t.w)
                for e, o in t.r.items():
                    rs[(e, id(o))] = o
                rds.extend(t.rd)
            for t in new:
                t.w = None
                t.r = {}
                t.rd = list(rds) + ws + list(rs.values())

        wctr = [0]
        wload = [0]
        total_units = [0]

        def w_prefetch():
            while wload[0] < min(wctr[0] + NWBUF, total_units[0]):
                i = wload[0]
                u = i % NU
                n = sum(s[2] * s[4] for s in units[u])
                dma("sp", wbuf[i % NWBUF][:, 0:n], wst[u, :, 0:n], [], [WB[i % NWBUF]])
                wload[0] += 1

        def w_next():
            w_prefetch()
            i = wctr[0]
            wctr[0] += 1
            b = i % NWBUF
            u = i % NU
            views = []
            off = 0
            for seg in units[u]:
                kcs, ncols = seg[2], seg[4]
                views.append(wbuf[b][:, off:off + kcs * ncols].rearrange("p (k n) -> p k n", k=kcs))
                off += kcs * ncols
            return WB[b], views

        tiles = [(k, t) for k in "ps" for t in range(OWN[k] // T)]
        if STOP.startswith("p1"):
            tiles = []
        total_units[0] = NU * len(tiles)

        def layer_norm(gname, bname, to_bf16):
            mean_b, msq, var_b, nmr = sm32[0], sm32[1], sm32[2], sm32[3]
            ts("dve", mean_b[:, :], psums[5][:, :], 1.0 / D, None, ALU.mult, None, [PS[5]], [SM[0]])
            tt("dve", msq[:, :], mean_b[:, :], mean_b[:, :], ALU.mult, [SM[0]], [SM[1]])
            stt(var_b[:, :], psums[6][:, :], 1.0 / D, msq[:, :], ALU.mult, ALU.subtract, [PS[6], SM[1]], [SM[2]])
            act(var_b[:, :], var_b[:, :], AF.Sqrt, [SM[2], c0], [SM[2]], bias=eps_ln[:, 0:1], scale=1.0)
            S.op("dve", "reciprocal", (var_b[:, :], var_b[:, :]), {}, reads=[SM[2]], writes=[SM[2]])
            stt(nmr[:, :], mean_b[:, :], -1.0, var_b[:, :], ALU.mult, ALU.mult, [SM[0], SM[2]], [SM[3]])
            for c in range(DC):
                tt("dve", hT32[:, c, :], hT32[:, c, :], var_b[:, :], ALU.mult, [HT[c], SM[2]], [HT[c]])
                tt("dve", hT32[:, c, :], hT32[:, c, :], nmr[:, :], ALU.add, [HT[c], SM[3]], [HT[c]])
                act(hT32[:, c, :], hT32[:, c, :], AF.Identity, [HT[c], c0], [HT[c]],
                    bias=lncol[bname][:, c:c + 1], scale=lncol[gname][:, c:c + 1])
                if to_bf16:
                    cp("pool", xT16[:, c, :], hT32[:, c, :], [HT[c]], [XT[c]])

        def stats_accum(c):
            u16, s16 = pT[0], pT[1]
            cp("act", u16[:, :], hT32[:, c, :], [HT[c]], [PT[0]])
            act(s16[:, :], hT32[:, c, :], AF.Square, [HT[c]], [PT[1]])
            mm(5, 128, T, ones16[:, :], u16[:, :], c == 0, c == DC - 1, [PT[0], c0])
            mm(6, 128, T, ones16[:, :], s16[:, :], c == 0, c == DC - 1, [PT[1], c0])

        for ti, (k, t) in enumerate(tiles):
            Sq = SEQ[k]
            nt_own = OWN[k] // T
            r0 = t * T
            fence_region(FFN_ALL, ATT_ALL)
            prev0 = (r0 - 128) % Sq
            next0 = (r0 + T) % Sq
            dma("act", xrow[:, 0, :], xin[k][prev0:prev0 + 128, :], [], [XR[0]])
            dma("act", xrow[:, 1, :], xin[k][next0:next0 + 128, :], [], [XR[1]])
            xh = mgT[:, :, 0:256]
            for b in range(2):
                cp(("pool", "dve")[b % 2], hi16[:, b, :], xrow[:, b, :], [XR[b]], [HI[b]])
            for dc in range(DC):
                pi = ps_next()
                for b in range(2):
                    tr16(pi, b * 128, hi16[:, b, dc * 128:(dc + 1) * 128], [HI[b]])
                cp(ev_eng(), xh[:, dc, :], psums16[pi][:, 0:256], [PS[pi]], [MG[dc]])
            load_x("act", xin[k][r0:r0 + T, :], 4)
            dma("act", sm32[4][:, :], csin[k][:, r0:r0 + T], [], [SM[4]])

            transpose_x(4, True)
            fence_region(XR, KVB + PT + ACC)
            fence_region(HI + LO, QN + QR + QC + AO + BO + CO)
            for i in range(3):
                wt, (wv,) = w_next()
                for j in range(2):
                    m = 2 * i + j
                    pi = ps_next()
                    for kc in range(DC):
                        mm(pi, 128, T, wv[:, kc, j * 128:(j + 1) * 128], xT16[:, kc, :], kc == 0, kc == DC - 1,
                           [wt, XT[kc]])
                    cp(ev_eng(), qaT[:, m, :], psums[pi][:, :], [PS[pi]], [QA[m]])
            wt, (wv,) = w_next()
            for g in range(A_KV):
                pi = ps_next()
                for kc in range(DC):
                    mm(pi, 128, T, wv[:, kc, g * 128:(g + 1) * 128], xT16[:, kc, :], kc == 0, kc == DC - 1, [wt, XT[kc]])
                cp(ev_eng(), kaT[:, g, 128:640], psums[pi][:, :], [PS[pi]], [KA[g]])
                pi = ps_next()
                for kc in range(DC):
                    mm(pi, 128, 256, wv[:, kc, g * 128:(g + 1) * 128], xh[:, kc, :], kc == 0, kc == DC - 1, [wt, MG[kc]])
                cp(ev_eng(), kaT[:, g, 0:128], psums[pi][:, 0:128], [PS[pi]], [KA[g]])
                cp(ev_eng(), kaT[:, g, 640:768], psums[pi][:, 128:256], [PS[pi]], [KA[g]])
            wt, (wv,) = w_next()
            for b in range(6):
                pi = ps_next()
                for kc in range(DC):
                    if b == 0:
                        l = xh[:, kc, 0:128]
                        rd = MG[kc]
                    elif b == 5:
                        l = xh[:, kc, 128:256]
                        rd = MG[kc]
                    else:
                        l = xT16[:, kc, (b - 1) * 128:b * 128]
                        rd = XT[kc]
                    mm(pi, 128, 256, l, wv[:, kc, :], kc == 0, kc == DC - 1, [wt, rd])
                cp(ev_eng(), va[:, b, :], psums[pi][:, 0:256], [PS[pi]], [VA])
            for i in range(2):
                wt, (wv,) = w_next()
                for j in range(2):
                    m = 2 * i + j
                    pi = ps_next()
                    for kc in range(DC):
                        mm(pi, 128, T, wv[:, kc, j * 128:(j + 1) * 128], xT16[:, kc, :], kc == 0, kc == DC - 1,
                           [wt, XT[kc]])
                    cp("dve", cqT[:, m, :], psums[pi][:, :], [PS[pi]], [CQ[m]])
                    act(pT[m % 3][:, :], psums[pi][:, :], AF.Square, [PS[pi], CQ[m]], [PT[m % 3]])
                    mm(7, 128, T, ones16[:, :], pT[m % 3][:, :], m == 0, m == 3, [PT[m % 3], c0])
            rq = sm32[5]
            rstd_from(7, T, rq[:, :], SM[5], 512.0, eps_rms)
            csr = sm32[4]
            tt("dve", csr[:, :], csr[:, :], rq[:, :], ALU.mult, [SM[4], SM[5]], [SM[4]])
            for i in range(2):
                wt, (wv,) = w_next()
                for j in range(2):
                    m = 2 * i + j
                    pi = ps_next()
                    for kc in range(DC):
                        mm(pi, 128, T, wv[:, kc, j * 128:(j + 1) * 128], xT16[:, kc, :], kc == 0, kc == DC - 1,
                           [wt, XT[kc]])
                    cp(ev_eng(), qcT[:, m, :], psums[pi][:, :], [PS[pi]], [QC[m]])
            for half in range(2):
                wt, wvs = w_next()
                for hh in range(3):
                    h = 3 * half + hh
                    pi = ps_next()
                    for kc in range(4):
                        mm(pi, 128, T, wvs[2 * hh][:, kc, :], cqT[:, kc, :], kc == 0, kc == 3, [wt, CQ[kc]])
                    tt("dve", qnT[:, h, :], rq[:, :], psums[pi][:, :], ALU.mult, [PS[pi], SM[5]], [QN[h]])
                    pi = ps_next()
                    for kc in range(4):
                        mm(pi, 128, T, wvs[2 * hh + 1][:, kc, :], cqT[:, kc, :], kc == 0, kc == 3, [wt, CQ[kc]])
                    tmp = pT[hh]
                    tt("dve", tmp[:, :], csr[:, :], psums[pi][:, :], ALU.mult, [PS[pi], SM[4]], [PT[hh]])
                    pi = ps_next()
                    mm(pi, 64, T, si16[:, :], tmp[:, :], True, True, [PT[hh], c0])
                    cp("act", qrT[0:64, h, :], psums[pi][0:64, :], [PS[pi]], [QR[h]])
            dbg_dump("qnT", qnT[:, :, :], (128, B_HEADS, T), BF16, QN)
            dbg_dump("qrT", qrT[0:64, :, :], (64, B_HEADS, T), BF16, QR)

            sc_a = HD ** -0.5
            for n in range(4):
                for g in range(A_KV):
                    pO, pL = 5, 6
                    for j in range(3):
                        pi = ps_next()
                        kb = n + j
                        mm(pi, 128, 384, kaT[:, g, kb * 128:(kb + 1) * 128], qaT[:, 3 * g:3 * g + 3, n * 128:(n + 1) * 128],
                           True, True, [KA[g]] + QA[3 * g:3 * g + 3])
                        s32 = sm32[j % 2]
                        stt(s32[:, 0:384].rearrange("p (h q) -> p h q", h=3), psums[pi][:, 0:384].rearrange("p (h q) -> p h q", h=3),
                            sc_a, biasT[:, 3 * g:3 * g + 3, j, :], ALU.mult, ALU.add, [PS[pi], c0], [SM[j % 2]])
                        p16 = pT[j]
                        act(p16[:, 0:384], s32[:, 0:384], AF.Exp, [SM[j % 2]], [PT[j]])
                        first_edge = (t == 0 and n == 0 and j == 0)
                        last_edge = (t == nt_own - 1 and n == 3 and j == 2)
                        if first_edge or last_edge:
                            fc = (0 if first_edge else 1) + (0 if k == "p" else 2)
                            ts("dve", p16[:, 0:384], p16[:, 0:384], flags[:, fc:fc + 1], None, ALU.mult, None,
                               [PT[j], c0], [PT[j]])
                        mm(pO, 128, 384, va[:, kb, g * 128:(g + 1) * 128], p16[:, 0:384], j == 0, j == 2, [VA, PT[j]])
                        mm(pL, 128, 384, ones16[:, :], p16[:, 0:384], j == 0, j == 2, [PT[j], c0])
                    l32 = sm32[2]
                    for r in range(3):
                        h = 3 * g + r
                        ts("dve", l32[:, r * 128:(r + 1) * 128], psums[pL][:, r * 128:(r + 1) * 128], se[:, h:h + 1], None,
                           ALU.add, None, [PS[pL], c0], [SM[2]])
                    S.op("dve", "reciprocal", (l32[:, 0:384], l32[:, 0:384]), {}, reads=[SM[2]], writes=[SM[2]])
                    tt("dve", aoT[:, 3 * g:3 * g + 3, n * 128:(n + 1) * 128], l32[:, 0:384].rearrange("p (h q) -> p h q", h=3),
                       psums[pO][:, 0:384].rearrange("p (h q) -> p h q", h=3), ALU.mult, [PS[pO], SM[2]], AO[3 * g:3 * g + 3])
            dbg_dump("aoT", aoT[:, :, :], (128, A_HEADS, T), BF16, AO)

            for h in range(C_HEADS):
                pO, pL = 5, 6
                for mb in range(2):
                    pi = ps_next()
                    mm(pi, 128, T, kcT[k][:, h, mb * 128:(mb + 1) * 128], qcT[:, h, :], True, True, [KC_T[k], QC[h]])
                    act(pT[mb][:, :], psums[pi][:, :], AF.Exp, [PS[pi]], [PT[mb]], scale=sc_a)
                    mm(pO, 128, T, vc[k][:, mb, h * 128:(h + 1) * 128], pT[mb][:, :], mb == 0, mb == 1, [VC_T[k], PT[mb]])
                    mm(pL, 128, T, ones16[:, :], pT[mb][:, :], mb == 0, mb == 1, [PT[mb], c0])
                S.op("dve", "reciprocal", (sm32[2][:, :], psums[pL][:, :]), {}, reads=[PS[pL]], writes=[SM[2]])
                tt("dve", coT[:, h, :], sm32[2][:, :], psums[pO][:, :], ALU.mult, [PS[pO], SM[2]], [CO[h]])
            dbg_dump("coT", coT[:, :, :], (128, C_HEADS, T), BF16, CO)

            sc_b = (128 + 64) ** -0.5
            nch = Sq // KCH
            kvi = 0
            items = [(h_, c_) for h_ in range(B_HEADS) for c_ in range(nch)]

            def kv_load(idx):
                h_, c_ = items[idx]
                b_ = idx % 2
                dma("act", knb[b_][:, :], KnT[k][h_, :, c_ * KCH:(c_ + 1) * KCH], [], [KVB[b_]])
                dma("act", krb[b_][0:64, :], KrT[k][:, c_ * KCH:(c_ + 1) * KCH], [], [KVB[b_]])
                dma("act", vb[b_][:, :, :], Vsc[k][h_, :, c_ * (KCH // 128):(c_ + 1) * (KCH // 128), :], [], [KVB[b_]])

            LOOK = 2
            ntile_ch = KCH // 128
            nkt = Sq // 128
            allk = [(h_, c_, kt_) for h_ in range(B_HEADS) for c_ in range(nch) for kt_ in range(ntile_ch)]
            kv_load(0)
            if len(items) > 1:
                kv_load(1)
            pend = {}

            def emit_S(i):
                h_, c_, kt_ = allk[i]
                bb_ = (h_ * nch + c_) % 2
                pi_ = ps_next()
                mm(pi_, 128, T, knb[bb_][:, kt_ * 128:(kt_ + 1) * 128], qnT[:, h_, :], True, False, [KVB[bb_], QN[h_]])
                mm(pi_, 128, T, krb[bb_][0:64, kt_ * 128:(kt_ + 1) * 128], qrT[0:64, h_, :], False, True, [KVB[bb_], QR[h_]])
                pend[i] = pi_

            for i in range(min(LOOK, len(allk))):
                emit_S(i)
            pO, pL = 5, 6
            for gi, (h, ch, kt) in enumerate(allk):
                if gi + LOOK < len(allk):
                    emit_S(gi + LOOK)
                pi = pend.pop(gi)
                g_ch = h * nch + ch
                bb = g_ch % 2
                it = ch * ntile_ch + kt
                pp = gi % 3
                act(pT[pp][:, :], psums[pi][:, :], AF.Exp, [PS[pi]], [PT[pp]], scale=sc_b)
                a = it % 2
                if it < 2:
                    cp("dve", acc[a][:, :], pT[pp][:, :], [PT[pp]], [ACC[a]])
                else:
                    tt("dve", acc[a][:, :], acc[a][:, :], pT[pp][:, :], ALU.add, [ACC[a], PT[pp]], [ACC[a]])
                mm(pO, 128, T, vb[bb][:, kt, :], pT[pp][:, :], it == 0, it == nkt - 1, [KVB[bb], PT[pp]])
                if kt == ntile_ch - 1 and g_ch + 2 < len(items):
                    kv_load(g_ch + 2)
                if it == nkt - 1:
                    tt("dve", acc[0][:, :], acc[0][:, :], acc[1][:, :], ALU.add, [ACC[0], ACC[1]], [ACC[0]])
                    cp("dve", sm32[0][:, :].bitcast(BF16)[:, 0:T], acc[0][:, :], [ACC[0]], [SM[0]])
                    tt("dve", sm32[1][:, :].bitcast(BF16)[:, 0:T], acc[0][:, :], sm32[0][:, :].bitcast(BF16)[:, 0:T],
                       ALU.subtract, [ACC[0], SM[0]], [SM[1]])
                    mm(pL, 128, T, ones16[:, :], sm32[0][:, :].bitcast(BF16)[:, 0:T], True, False, [SM[0], c0])
                    mm(pL, 128, T, ones16[:, :], sm32[1][:, :].bitcast(BF16)[:, 0:T], False, True, [SM[1], c0])
                    S.op("dve", "reciprocal", (sm32[2][:, :], psums[pL][:, :]), {}, reads=[PS[pL]], writes=[SM[2]])
                    tt("dve", boT[:, h, :], sm32[2][:, :], psums[pO][:, :], ALU.mult, [PS[pO], SM[2]], [BO[h]])
            dbg_dump("boT", boT[:, :, :], (128, B_HEADS, T), BF16, BO)

            for c in range(DC):
                gts = []

                def gate(bi, wt, wg):
                    pi = ps_next()
                    for kc in range(DC):
                        mm(pi, 128, T, wg[:, kc, :], xT16[:, kc, :], kc == 0, kc == DC - 1, [wt, XT[kc]])
                    gt = sm32[bi]
                    act(gt[:, :], psums[pi][:, :], AF.Sigmoid, [PS[pi], c0], [SM[bi]],
                        bias=bgc[:, bi * 16 + c:bi * 16 + c + 1], scale=1.0)
                    gts.append(gt)

                wt1, (wg0, wg1) = w_next()
                gate(0, wt1, wg0)
                gate(1, wt1, wg1)
                wt2, (wg2, wa, wb_, wc) = w_next()
                gate(2, wt2, wg2)
                for bi, (wv, src, srct, nk) in enumerate(((wa, aoT, AO, 6), (wb_, boT, BO, 6), (wc, coT, CO, 4))):
                    pi = ps_next()
                    for kc in range(nk):
                        mm(pi, 128, T, wv[:, kc, :], src[:, kc, :], kc == 0, kc == nk - 1, [wt2, srct[kc]])
                    tt("dve", gts[bi][:, :], gts[bi][:, :], psums[pi][:, :], ALU.mult, [SM[bi], PS[pi]], [SM[bi]])
                tt("dve", gts[0][:, :], gts[0][:, :], gts[1][:, :], ALU.add, [SM[0], SM[1]], [SM[0]])
                tt("dve", mgT[:, c, :], gts[0][:, :], gts[2][:, :], ALU.add, [SM[0], SM[2]], [MG[c]])
            for i in range(8):
                wt, (wv,) = w_next()
                for j in range(2):
                    m = 2 * i + j
                    pi = ps_next()
                    for kc in range(DC):
                        mm(pi, 128, T, wv[:, kc, j * 128:(j + 1) * 128], mgT[:, kc, :], kc == 0, kc == DC - 1, [wt, MG[kc]])
                    tt("dve", hT32[:, m, :], hT32[:, m, :], psums[pi][:, :], ALU.add, [HT[m], PS[pi]], [HT[m]])
                    stats_accum(m)
            dbg_dump("pre1", hT32[:, :, :], (128, DC, T), F32, HT)
            layer_norm("ln1_g", "ln1_b", True)
            dbg_dump("mean1", sm32[0][:, :], (128, T), F32, [SM[0]])
            dbg_dump("rstd1", sm32[2][:, :], (128, T), F32, [SM[2]])
            dbg_dump("h1", hT32[:, :, :], (128, DC, T), F32, HT)
            fence_region(ATT_ALL, FFN_ALL)
            for j in range(FC):
                wt, (wgt, wup) = w_next()
                pg, pu = ps_next(), ps_next()
                for kc in range(DC):
                    mm(pg, 128, T, wgt[:, kc, :], xT16[:, kc, :], kc == 0, kc == DC - 1, [wt, XT[kc]])
                for kc in range(DC):
                    mm(pu, 128, T, wup[:, kc, :], xT16[:, kc, :], kc == 0, kc == DC - 1, [wt, XT[kc]])
                sg = sm32[3 + (j % 2)] if False else None
                fq = j % 2
                act(fsc[fq][:, :], psums[pg][:, :], AF.Silu, [PS[pg]], [FSC[fq]])
                tt("dve", actT[:, j, :], fsc[fq][:, :], psums[pu][:, :], ALU.mult, [FSC[fq], PS[pu]], [ACTT[j]])
            for m in range(DC):
                pi = ps_next()
                for half in range(2):
                    wt, (wv,) = w_next()
                    for kc in range(22):
                        jj = half * 22 + kc
                        mm(pi, 128, T, wv[:, kc, :], actT[:, jj, :], jj == 0, jj == FC - 1, [wt, ACTT[jj]])
                stt(hT32[:, m, :], hT32[:, m, :], ALPHA, psums[pi][:, :], ALU.mult, ALU.add, [HT[m], PS[pi]], [HT[m]])
                u16 = ystage[0].bitcast(BF16)[:, 0:T]
                s16 = ystage[0].bitcast(BF16)[:, T:2 * T]
                cp("act", u16, hT32[:, m, :], [HT[m]], [YS[0]])
                mm(5, 128, T, ones16[:, :], u16, m == 0, m == DC - 1, [YS[0], c0])
                act(s16, hT32[:, m, :], AF.Square, [HT[m]], [YS[0]])
                mm(6, 128, T, ones16[:, :], s16, m == 0, m == DC - 1, [YS[0], c0])
            sm_save = (sm32[0], sm32[1], sm32[2], sm32[3])
            SM_save = (SM[0], SM[1], SM[2], SM[3])
            ys1 = ystage[1]
            for q in range(4):
                sm32[q] = ys1[:, q * T:(q + 1) * T]
                SM[q] = YS[1]
            layer_norm("ln2_g", "ln2_b", False)
            for q in range(4):
                sm32[q] = sm_save[q]
                SM[q] = SM_save[q]
            for c in range(DC):
                cp(("pool", "act")[c % 2], xT16[:, c, :], hT32[:, c, :], [HT[c]], [XT[c]])
                tt("dve", ylo[:, c, :], hT32[:, c, :], xT16[:, c, :], ALU.subtract, [HT[c], XT[c]], [YL[c]])
            for b in range(4):
                yb = b % 2
                for c4 in range(4):
                    pi = ps_next()
                    for cc in range(4):
                        c = c4 * 4 + cc
                        tr16(pi, cc * 128, xT16[:, c, b * 128:(b + 1) * 128], [XT[c]])
                    for cc in range(4):
                        c = c4 * 4 + cc
                        tr16(pi, 512 + cc * 128, ylo[:, c, b * 128:(b + 1) * 128], [YL[c]])
                    fq = c4 % 2
                    cp("act", ystage[yb][:, c4 * 512:(c4 + 1) * 512], psums16[pi][:, 0:512], [PS[pi]], [YS[yb]])
                    cp("act", fsc[fq][:, :], psums16[pi][:, 512:1024], [PS[pi]], [FSC[fq]])
                    tt("dve", ystage[yb][:, c4 * 512:(c4 + 1) * 512], ystage[yb][:, c4 * 512:(c4 + 1) * 512],
                       fsc[fq][:, :], ALU.add, [YS[yb], FSC[fq]], [YS[yb]])
                dma("pool", yout[k][r0 + b * 128:r0 + (b + 1) * 128, :], ystage[yb][:, :], [YS[yb]], [])
        S.finish()

        sem_ctx = {}
        for e in ("pe", "act", "dve", "pool"):
            sem_ctx[e] = es.enter_context(nc.semaphore("s_" + e))
        for e in ("act", "pool", "sp"):
            for i in range(NSLOT):
                sem_ctx[(e, i)] = es.enter_context(nc.semaphore("d_%s%d" % (e, i)))
        block = es.enter_context(nc.Block())
        S.emit(nc, block, sem_ctx)
    return nc, S, list(dbg_out.keys())


def _t5_bucket(rel):
    half = 16
    max_exact = 8
    ret = (rel > 0).astype(np.int32) * half
    n = np.abs(rel)
    large = max_exact + (np.log(np.maximum(n, 1).astype(np.float32) / max_exact)
                         / math.log(128 / max_exact) * (half - max_exact)).astype(np.int32)
    large = np.minimum(large, half - 1)
    return ret + np.where(n < max_exact, n, large)


def _constants():
    ident = np.eye(128, dtype=np.float32)
    si = np.zeros((128, 64), np.float32)
    si[np.arange(64), np.arange(64)] = 1.0
    si[np.arange(64) + 64, np.arange(64)] = 1.0
    kk = np.arange(128)[:, None]
    qq = np.arange(128)[None, :]
    sel = np.zeros((32, 3, 128, 128), np.float32)
    msk = np.zeros((128, 3, 128), np.float32)
    for j in range(3):
        rel = (j - 1) * 128 + kk - qq
        b = _t5_bucket(rel)
        for bb in range(32):
            sel[bb, j][b == bb] = 1.0
        msk[:, j, :] = np.where(np.abs(rel) <= 128, 0.0, NEG)
    return ident, si, sel.reshape(32, -1), msk.reshape(128, -1)


def _rope_cs(pos):
    half = 32
    inv = (1.0 / (np.float32(10000.0) ** (np.arange(half, dtype=np.float32) / np.float32(half)))).astype(np.float32)
    ang = (pos.astype(np.float32)[None, :] * inv[:, None]).astype(np.float32)
    c, s = np.cos(ang).astype(np.float32), np.sin(ang).astype(np.float32)
    return np.ascontiguousarray(np.concatenate([c, c, s, s], axis=0))


def make_in_maps(inp, n_seq_cores, n_cores):
    xp_all = np.asarray(inp["x_prompt"])
    xs_all = np.asarray(inp["x_sample"])[0]
    P_SEQ = xp_all.shape[1]
    S_SEQ = xs_all.shape[0]
    own_p = P_SEQ // n_seq_cores
    own_s = S_SEQ // n_cores
    ident, si, sel, msk = _constants()
    shared = {"ident": ident, "stack_ident": si, "t5_sel": sel, "win_mask": msk,
              "rel_bias": np.asarray(inp["rel_bias"], np.float32)}
    for n in WNAMES:
        shared[n] = np.ascontiguousarray(np.asarray(inp[n])[0])
    for n in ("q_norm_g", "kv_norm_g", "b_gate", "ln1_g", "ln1_b", "ln2_g", "ln2_b"):
        v = np.asarray(inp[n], np.float32).reshape(-1)
        shared[n] = np.ascontiguousarray(v.reshape(-1, 128).T)
    shared["sink"] = np.ascontiguousarray(np.broadcast_to(np.asarray(inp["sink"], np.float32).reshape(1, 6), (128, 6)))
    maps = []
    for c in range(n_cores):
        seq, pc = c // n_seq_cores, c % n_seq_cores
        m = dict(shared)
        m["xp"] = np.ascontiguousarray(np.roll(xp_all[seq], -pc * own_p, axis=0))
        m["xs"] = np.ascontiguousarray(np.roll(xs_all, -c * own_s, axis=0))
        m["mem_p"] = np.ascontiguousarray(np.asarray(inp["mem_prompt"])[seq])
        m["mem_s"] = np.ascontiguousarray(np.asarray(inp["mem_sample"])[0])
        m["cs_p"] = _rope_cs((np.arange(P_SEQ) + pc * own_p) % P_SEQ)
        m["cs_s"] = _rope_cs((np.arange(S_SEQ) + c * own_s) % S_SEQ)
        fl = np.zeros((128, 4), np.float32)
        fl[:, 0] = 0.0 if pc == 0 else 1.0
        fl[:, 1] = 0.0 if pc == n_seq_cores - 1 else 1.0
        fl[:, 2] = 0.0 if c == 0 else 1.0
        fl[:, 3] = 0.0 if c == n_cores - 1 else 1.0
        m["flags"] = fl
        maps.append(m)
    return maps, P_SEQ, S_SEQ, own_p, own_s


def run(inp, dbg=()):
    n_cores = 8
    n_seq_cores = 4
    maps, P_SEQ, S_SEQ, own_p, own_s = make_in_maps(inp, n_seq_cores, n_cores)
    nc, S, dnames = build(P_SEQ, S_SEQ, own_p, own_s, dbg)
    if os.environ.get("KTRACE"):
        res = run_bass_kernel_spmd(nc, maps, core_ids=list(range(n_cores)), trace=True)
        print("EXEC_TIME_NS", res.exec_time_ns, flush=True)
    else:
        res = run_bass_kernel_spmd(nc, maps, core_ids=list(range(n_cores)))
    B = np.asarray(inp["x_prompt"]).shape[0]
    yp = np.zeros((B, P_SEQ, D), np.float32)
    ys = np.zeros((1, S_SEQ, D), np.float32)
    for c in range(n_cores):
        seq, pc = c // n_seq_cores, c % n_seq_cores
        r = res.results[c]
        yp[seq, pc * own_p:(pc + 1) * own_p] = r["y_p"]
        ys[0, c * own_s:(c + 1) * own_s] = r["y_s"]
    return (yp, ys), res, dnames


def kernel(**inputs):
    (yp, ys), _, _ = run(inputs)
    return (yp, ys)
```

```python
import math
import os
import numpy as np
import concourse.bass as bass
import concourse.mybir as mybir
from concourse.bass_utils import run_bass_kernel_spmd

F32 = mybir.dt.float32
BF16 = mybir.dt.bfloat16
AF = mybir.ActivationFunctionType
ALU = mybir.AluOpType

D = 2048
DC = 16
T = 512
HD = 128
A_HEADS, A_KV = 6, 2
B_HEADS = 6
C_HEADS = 4
N_MEM = 256
D_FF = 5632
FC = 44
ALPHA = 2.0 ** 0.25
LN_EPS = 1e-5
RMS_EPS = 1e-6
NEG = -1e30
UNIT = 4096
NWBUF = 3
NSLOT = 8


class Tl:
    __slots__ = ("w", "r", "rd", "const")

    def __init__(self, const=False):
        self.w = None
        self.r = {}
        self.rd = []
        self.const = const


class Op:
    __slots__ = ("eng", "name", "args", "kw", "deps", "sig", "val", "dma", "semkey", "idx")


class Sched:
    ENGS = ("pe", "act", "dve", "pool", "sp")

    def __init__(self):
        self.q = {e: [] for e in self.ENGS}
        self.ndma = {e: 0 for e in self.ENGS}
        self.slot_last = {}
        self.slot_cnt = {}
        self.fence = []
        self.fence_pending = set()
        self.nops = 0

    def op(self, eng, name, args=(), kw=None, reads=(), writes=(), dma=False):
        o = Op()
        o.eng, o.name, o.args, o.kw = eng, name, args, (kw or {})
        o.dma, o.sig, o.val = dma, False, 0
        deps = []
        for t in reads:
            if t.w is not None:
                deps.append(t.w)
        for t in writes:
            if t.w is not None:
                deps.append(t.w)
            deps.extend(t.r.values())
            deps.extend(t.rd)
        if dma:
            k = self.ndma[eng]
            self.ndma[eng] = k + 1
            s = (eng, k % NSLOT)
            o.semkey = s
            prev = self.slot_last.get(s)
            if prev is not None:
                deps.append(prev)
            self.slot_last[s] = o
            self.slot_cnt[s] = self.slot_cnt.get(s, 0) + 1
            o.val = 16 * self.slot_cnt[s]
        else:
            o.semkey = eng
        if eng in self.fence_pending:
            deps.extend(self.fence)
            self.fence_pending.discard(eng)
        dd = []
        seen = set()
        for d in deps:
            if d is o or id(d) in seen:
                continue
            seen.add(id(d))
            if (not d.dma) and (not dma) and d.eng == "pe" and eng == "pe":
                continue
            dd.append(d)
        best = {}
        for d in dd:
            b = best.get(d.semkey)
            if b is None or d.idx > b.idx:
                best[d.semkey] = d
        o.deps = list(best.values())
        for d in o.deps:
            d.sig = True
        o.idx = len(self.q[eng])
        for t in reads:
            if t.const:
                continue
            if dma:
                t.rd.append(o)
            else:
                t.r[eng] = o
        for t in writes:
            t.w = o
            t.r = {}
            t.rd = []
        self.q[eng].append(o)
        self.nops += 1
        return o

    def barrier(self):
        deps = []
        for e in self.ENGS:
            for o in reversed(self.q[e]):
                if not o.dma:
                    o.sig = True
                    deps.append(o)
                    break
        deps.extend(self.slot_last.values())
        self.fence = deps
        self.fence_pending = set(self.ENGS)

    def finish(self):
        for e in self.ENGS:
            lasts = [o for (s, o) in self.slot_last.items() if s[0] == e]
            if lasts:
                f = Op()
                f.eng, f.name, f.args, f.kw = e, None, (), {}
                f.dma, f.sig, f.val, f.semkey = False, False, 0, e
                f.idx = len(self.q[e])
                f.deps = lasts
                self.q[e].append(f)

    def emit(self, nc, block, sems):
        for e in self.ENGS:
            c = 0
            for o in self.q[e]:
                if o.dma:
                    continue
                if o.sig:
                    c += 1
                o.val = c
        handles = {"pe": block.tensor, "act": block.scalar, "dve": block.vector,
                   "pool": block.gpsimd, "sp": block.sync}
        for e in self.ENGS:
            ops = self.q[e]
            if not ops:
                continue

            def body(h, ops=ops):
                waited = {}
                for o in ops:
                    need = {}
                    for d in o.deps:
                        if need.get(d.semkey, 0) < d.val:
                            need[d.semkey] = d.val
                    for k, v in need.items():
                        if waited.get(k, 0) < v:
                            h.wait_ge(sems[k], v)
                            waited[k] = v
                    if o.name is None:
                        continue
                    ins = getattr(h, o.name)(*o.args, **o.kw)
                    if o.dma:
                        ins.then_inc(sems[o.semkey], 16)
                    elif o.sig:
                        ins.then_inc(sems[o.semkey], 1)

            handles[e](body)


def plan_units():
    U = []
    for i in range(3):
        U.append([("w_in", 0, 16, i * 256, 256, None)])
    U.append([("w_in", 0, 16, 768, 256, None)])
    U.append([("w_in", 0, 16, 1024, 256, None)])
    for i in range(2):
        U.append([("w_in", 0, 16, 1280 + i * 256, 256, None)])
    for i in range(2):
        U.append([("w_in", 0, 16, 2368 + i * 256, 256, None)])
    for half in range(2):
        segs = []
        for h in range(3 * half, 3 * half + 3):
            segs.append(("w_uq", 0, 4, h * 192, 128, "gq"))
            segs.append(("w_uq", 0, 4, h * 192 + 128, 128, "gq_rope"))
        U.append(segs)
    for c in range(DC):
        U.append([("w_gate", 0, 16, c * 128, 128, None), ("w_gate", 0, 16, D + c * 128, 128, None)])
        U.append([("w_gate", 0, 16, 2 * D + c * 128, 128, None), ("w_br_a", 0, 6, c * 128, 128, None),
                  ("w_br_b", 0, 6, c * 128, 128, None), ("w_br_c", 0, 4, c * 128, 128, None)])
    for i in range(8):
        U.append([("w_o", 0, 16, i * 256, 256, None)])
    for j in range(FC):
        U.append([("w_ffn_in", 0, 16, j * 128, 128, None), ("w_ffn_in", 0, 16, D_FF + j * 128, 128, None)])
    for m in range(DC):
        for half in range(2):
            U.append([("w_ffn_down", half * 22 * 128, 22, m * 128, 128, None)])
    return U


WNAMES = ["w_in", "w_uq", "w_ukv", "w_mem_kv", "w_gate", "w_br_a", "w_br_b", "w_br_c", "w_o",
          "w_ffn_in", "w_ffn_down"]
WSHAPES = {"w_in": (2048, 2880), "w_uq": (512, 1152), "w_ukv": (512, 1536), "w_mem_kv": (2048, 1024),
           "w_gate": (2048, 6144), "w_br_a": (768, 2048), "w_br_b": (768, 2048), "w_br_c": (512, 2048),
           "w_o": (2048, 2048), "w_ffn_in": (2048, 11264), "w_ffn_down": (5632, 2048)}


def _rs(n):
    names = "abcdefg"[:n]
    return names


def build(P_SEQ, S_SEQ, OWN_P, OWN_S, dbg=()):
    nc = bass.Bass("TRN2", target_bir_lowering=False)
    S = Sched()
    dbg = set(dbg)

    def din(name, shape, dt=F32):
        return nc.dram_tensor(name, list(shape), dt, kind="ExternalInput").ap()

    xin = {"p": din("xp", (P_SEQ, D)), "s": din("xs", (S_SEQ, D))}
    memin = {"p": din("mem_p", (N_MEM, D)), "s": din("mem_s", (N_MEM, D))}
    csin = {"p": din("cs_p", (128, P_SEQ)), "s": din("cs_s", (128, S_SEQ))}
    W = {n: din(n, WSHAPES[n]) for n in WNAMES}
    rel_bias = din("rel_bias", (32, 6))
    sink = din("sink", (128, 6))
    q_norm_g = din("q_norm_g", (128, 4))
    kv_norm_g = din("kv_norm_g", (128, 4))
    b_gate = din("b_gate", (128, 48))
    lnp = {n: din(n, (128, 16)) for n in ("ln1_g", "ln1_b", "ln2_g", "ln2_b")}
    flags_in = din("flags", (128, 4))
    ident_in = din("ident", (128, 128))
    si_in = din("stack_ident", (128, 64))
    sel_in = din("t5_sel", (32, 3 * 128 * 128))
    wmask_in = din("win_mask", (128, 3 * 128))
    yout = {"p": nc.dram_tensor("y_p", [OWN_P, D], F32, kind="ExternalOutput").ap(),
            "s": nc.dram_tensor("y_s", [OWN_S, D], F32, kind="ExternalOutput").ap()}
    SEQ = {"p": P_SEQ, "s": S_SEQ}
    OWN = {"p": OWN_P, "s": OWN_S}

    units = plan_units()
    NU = len(units)
    wst = nc.dram_tensor("wst", [NU, 128, UNIT], BF16, kind="Internal").ap()
    KnT = {k: nc.dram_tensor("knT_" + k, [B_HEADS, 128, SEQ[k]], BF16, kind="Internal").ap() for k in "ps"}
    KrT = {k: nc.dram_tensor("krT_" + k, [64, SEQ[k]], BF16, kind="Internal").ap() for k in "ps"}
    Vsc = {k: nc.dram_tensor("v_" + k, [B_HEADS, 128, SEQ[k] // 128, 128], BF16, kind="Internal").ap() for k in "ps"}
    biasd = nc.dram_tensor("biasd", [6, 3 * 128 * 128], F32, kind="Internal").ap()
    dbg_out = {}

    import contextlib
    with contextlib.ExitStack() as es:
        def sb(name, shape, dt):
            return es.enter_context(nc.sbuf_tensor(name, list(shape), dt))

        ARENA_F32 = 45 * 1024
        arena = sb("arena", (128, ARENA_F32), F32)
        psums = [es.enter_context(nc.psum_tensor("ps%d" % i, [128, 512], F32)) for i in range(8)]
        PS = [Tl() for _ in range(8)]

        ident32 = sb("ident32", (128, 128), F32)
        ident16 = sb("ident16", (128, 128), BF16)
        ones16 = sb("ones16", (128, 128), BF16)
        ones32 = sb("ones32", (128, 128), F32)
        si16 = sb("si16", (128, 64), BF16)
        si32 = sb("si32", (128, 64), F32)
        biasT = sb("biasT", (128, 6, 3, 128), F32)
        wmask = sb("wmask", (128, 3, 128), F32)
        se = sb("se", (128, 6), F32)
        flags = sb("flags_sb", (128, 4), F32)
        bgc = sb("bgc", (128, 48), F32)
        lncol = {n: sb(n + "_c", (128, 16), F32) for n in lnp}
        gqc = sb("gqc", (128, 4), F32)
        gkvc = sb("gkvc", (128, 4), F32)
        kcT = {k: sb("kcT_" + k, (128, C_HEADS, N_MEM), BF16) for k in "ps"}
        vc = {k: sb("vc_" + k, (128, 2, C_HEADS * HD), BF16) for k in "ps"}
        eps_ln = sb("eps_ln", (128, 1), F32)
        eps_rms = sb("eps_rms", (128, 1), F32)
        CONST = Tl(const=True)
        KC_T = {k: Tl() for k in "ps"}
        VC_T = {k: Tl() for k in "ps"}

        class Arena:
            def __init__(self):
                self.off = 0

            def reset(self, off=0):
                self.off = off

            def take(self, free, dt):
                n = int(np.prod(free))
                nb = n * (4 if dt == F32 else 2)
                nw = (nb + 63) // 64 * 16
                assert self.off + nw <= ARENA_F32, ("arena overflow", self.off, nw)
                ap = arena[:, self.off:self.off + nw]
                self.off += nw
                if dt != F32:
                    ap = ap.bitcast(dt)
                ap = ap[:, 0:n]
                if len(free) > 1:
                    names = _rs(len(free))
                    pat = "p (" + " ".join(names) + ") -> p " + " ".join(names)
                    ap = ap.rearrange(pat, **{names[i]: int(free[i]) for i in range(len(free))})
                return ap

        AR = Arena()

        psrr = [0]

        def ps_next(pool=(0, 1, 2, 3, 4)):
            i = pool[psrr[0] % len(pool)]
            psrr[0] += 1
            return i

        def mm(pi, M, N, lhsT, rhs, start, stop, reads, n0=0):
            S.op("pe", "matmul", (psums[pi][0:M, n0:n0 + N],), dict(lhsT=lhsT, rhs=rhs, start=start, stop=stop),
                 reads=reads, writes=[PS[pi]])

        def tr(pi, n0, in_ap, reads):
            S.op("pe", "transpose", (psums[pi][:, n0:n0 + 128], in_ap, ident32[:, :]), {},
                 reads=list(reads) + [CONST], writes=[PS[pi]])

        psums16 = [p[:, :].bitcast(BF16) for p in psums]

        def tr16(pi, n0, in_ap, reads):
            S.op("pe", "transpose", (psums16[pi][:, n0:n0 + 128], in_ap, ident16[:, :]), {},
                 reads=list(reads) + [CONST], writes=[PS[pi]])

        def act(out, in_, func, reads, writes, bias=None, scale=None):
            kw = {}
            if bias is not None:
                kw["bias"] = bias
            if scale is not None:
                kw["scale"] = scale
            S.op("act", "activation", (out, in_, func), kw, reads=reads, writes=writes)

        def tt(eng, out, a, b, op, reads, writes):
            S.op(eng, "tensor_tensor", (out, a, b, op), {}, reads=reads, writes=writes)

        def ts(eng, out, a, s1, s2, op0, op1, reads, writes):
            if op1 is None:
                S.op(eng, "tensor_scalar", (out, a, s1, None, op0), {}, reads=reads, writes=writes)
            else:
                S.op(eng, "tensor_scalar", (out, a, s1, s2, op0, op1), {}, reads=reads, writes=writes)

        def stt(out, a, s, b, op0, op1, reads, writes):
            S.op("dve", "scalar_tensor_tensor", (out, a, s, b, op0, op1), {}, reads=reads, writes=writes)

        def cp(eng, out, in_, reads, writes):
            if eng == "act":
                S.op("act", "copy", (out, in_), {}, reads=reads, writes=writes)
            else:
                S.op(eng, "tensor_copy", (out, in_), {}, reads=reads, writes=writes)

        def dma(stream, out, in_, reads, writes):
            if stream == "pool":
                stream = os.environ.get("KPOOLQ", "pool")
            return S.op(stream, "dma_start", (), dict(out=out, in_=in_), reads=reads, writes=writes, dma=True)

        evrr = [0]

        def ev_eng():
            evrr[0] += 1
            return "act" if evrr[0] % 2 else "dve"

        def dbg_dump(name, ap, shape, dt, reads):
            if name not in dbg or name in dbg_out:
                return
            o = nc.dram_tensor("dbg_" + name, list(shape), dt, kind="ExternalOutput").ap()
            dbg_out[name] = o
            dma("pool", o, ap, reads, [])

        STOP = os.environ.get("KSTOP", "")
        SKIP = os.environ.get("KSKIP", "").split(",")
        c0 = Tl()
        dma("sp", ident32[:, :], ident_in, [], [c0])
        dma("sp", si32[:, :], si_in, [], [c0])
        dma("sp", wmask[:, :, :], wmask_in.rearrange("p (j q) -> p j q", j=3), [], [c0])
        dma("sp", flags[:, :], flags_in, [], [c0])
        dma("sp", se[:, :], sink, [], [c0])
        dma("sp", bgc[:, :], b_gate, [], [c0])
        for n in lnp:
            dma("sp", lncol[n][:, :], lnp[n], [], [c0])
        dma("sp", gqc[:, :], q_norm_g, [], [c0])
        dma("sp", gkvc[:, :], kv_norm_g, [], [c0])
        S.op("dve", "memset", (ones16[:, :], 1.0), {}, writes=[c0])
        S.op("dve", "memset", (ones32[:, :], 1.0), {}, writes=[c0])
        S.op("dve", "memset", (eps_ln[:, :], LN_EPS), {}, writes=[c0])
        S.op("dve", "memset", (eps_rms[:, :], RMS_EPS), {}, writes=[c0])
        cp("dve", si16[:, :], si32[:, :], [c0], [c0])
        cp("dve", ident16[:, :], ident32[:, :], [c0], [c0])
        act(se[:, :], se[:, :], AF.Exp, [c0], [c0])

        AR.reset()
        PIECE = 12288
        relb = AR.take((6,), F32)
        selb = AR.take((PIECE,), F32)
        bflat = AR.take((PIECE,), F32)
        t_rel, t_sel, t_bf, t_bd = Tl(), Tl(), Tl(), Tl()
        dma("sp", relb[0:32, :], rel_bias, [], [t_rel])
        for pc in range(4):
            dma("sp", selb[0:32, :], sel_in[:, pc * PIECE:(pc + 1) * PIECE], [], [t_sel])
            for i in range(PIECE // 512):
                pi = ps_next()
                mm(pi, 6, 512, relb[0:32, 0:6], selb[0:32, i * 512:(i + 1) * 512], True, True, [t_rel, t_sel])
                cp(ev_eng(), bflat[0:6, i * 512:(i + 1) * 512], psums[pi][0:6, :], [PS[pi]], [t_bf])
            dma("sp", biasd[:, pc * PIECE:(pc + 1) * PIECE], bflat[0:6, :], [t_bf], [t_bd])
        for h in range(6):
            dma("sp", biasT[:, h, :, :], biasd[h, :].rearrange("(j k q) -> k j q", j=3, k=128), [t_bd], [c0])
        for h in range(6):
            tt("dve", biasT[:, h, :, :], biasT[:, h, :, :], wmask[:, :, :], ALU.add, [c0], [c0])
        S.barrier()

        AR.reset()
        wckv = AR.take((16, 640), BF16)
        wukv = AR.take((4, 1536), BF16)
        wmem = AR.take((16, 1024), BF16)
        P1W = Tl()
        p1_base = AR.off
        st32 = [AR.take((UNIT,), F32) for _ in range(3)]
        st16 = [AR.take((UNIT,), BF16) for _ in range(3)]
        T32 = [Tl() for _ in range(3)]
        T16 = [Tl() for _ in range(3)]
        cvrr = [0]

        def cast_eng():
            cvrr[0] += 1
            return ("dve", "pool", "act")[cvrr[0] % 3]

        def conv_seg(seg, dst3, dst_t, slot):
            wname, row0, kcs, col0, ncols, special = seg
            s32 = st32[slot][:, 0:kcs * ncols].rearrange("p (k n) -> p k n", k=kcs)
            ncl = 64 if special in ("gq_rope", "rope") else ncols
            src = W[wname][row0:row0 + kcs * 128, col0:col0 + ncl].rearrange("(k p) n -> p k n", p=128)
            dma("sp", s32[:, :, 0:ncl], src, [], [T32[slot]])
            if special is None:
                cp(cast_eng(), dst3, s32, [T32[slot]], [dst_t])
            elif special in ("gq", "gkv"):
                gc = gqc if special == "gq" else gkvc
                for k in range(kcs):
                    ts("dve", dst3[:, k, :], s32[:, k, :], gc[:, k:k + 1], None, ALU.mult, None,
                       [T32[slot], c0], [dst_t])
            elif special in ("gq_rope", "rope"):
                for k in range(kcs):
                    if special == "gq_rope":
                        ts("dve", dst3[:, k, 0:64], s32[:, k, 0:64], gqc[:, k:k + 1], None, ALU.mult, None,
                           [T32[slot], c0], [dst_t])
                        ts("dve", dst3[:, k, 64:96], s32[:, k, 32:64], gqc[:, k:k + 1], -1.0, ALU.mult, ALU.mult,
                           [T32[slot], c0], [dst_t])
                        ts("dve", dst3[:, k, 96:128], s32[:, k, 0:32], gqc[:, k:k + 1], None, ALU.mult, None,
                           [T32[slot], c0], [dst_t])
                    else:
                        cp("dve", dst3[:, k, 0:64], s32[:, k, 0:64], [T32[slot]], [dst_t])
                        ts("dve", dst3[:, k, 64:96], s32[:, k, 32:64], -1.0, None, ALU.mult, None,
                           [T32[slot]], [dst_t])
                        cp("dve", dst3[:, k, 96:128], s32[:, k, 0:32], [T32[slot]], [dst_t])

        slot_rr = [0]

        def nslot():
            slot_rr[0] += 1
            return slot_rr[0] % 3

        for i in range(2):
            conv_seg(("w_in", 0, 16, 1792 + i * 256, 256, None), wckv[:, :, i * 256:(i + 1) * 256], P1W, nslot())
        conv_seg(("w_in", 0, 16, 2304, 128, "rope"), wckv[:, :, 512:640], P1W, nslot())
        for h in range(6):
            conv_seg(("w_ukv", 0, 4, h * 256, 128, "gkv"), wukv[:, :, h * 128:(h + 1) * 128], P1W, nslot())
            conv_seg(("w_ukv", 0, 4, h * 256 + 128, 128, "gkv"), wukv[:, :, 768 + h * 128:768 + (h + 1) * 128],
                     P1W, nslot())
        for i in range(4):
            conv_seg(("w_mem_kv", 0, 16, i * 256, 256, None), wmem[:, :, i * 256:(i + 1) * 256], P1W, nslot())
        for u, segs in enumerate(units):
            b = u % 3
            off = 0
            for seg in segs:
                kcs, ncols = seg[2], seg[4]
                dst3 = st16[b][:, off:off + kcs * ncols].rearrange("p (k n) -> p k n", k=kcs)
                conv_seg(seg, dst3, T16[b], nslot())
                off += kcs * ncols
            dma("act", wst[u, :, 0:off], st16[b][:, 0:off], [T16[b]], [])
        S.barrier()

        if STOP == "p0":
            SEQ = {"p": 0, "s": 0}
            OWN = {"p": 0, "s": 0}
            memin = {}
        AR.reset(p1_base)
        xrow = AR.take((4, D), F32)
        XR = [Tl() for _ in range(4)]
        hi16 = AR.take((4, D), BF16)
        HI = [Tl() for _ in range(4)]
        xT16 = AR.take((DC, T), BF16)
        XT = [Tl() for _ in range(DC)]
        ckvT = AR.take((4, T), BF16)
        CK = [Tl() for _ in range(4)]
        sq16 = AR.take((4, T), BF16)
        SQ = [Tl() for _ in range(4)]
        krc = AR.take((T,), BF16)
        KRC = Tl()
        krT = [AR.take((T,), BF16) for _ in range(2)]
        KRT = [Tl() for _ in range(2)]
        knst = [AR.take((B_HEADS, T), BF16) for _ in range(2)]
        KNS = [Tl() for _ in range(2)]
        vst = [AR.take((B_HEADS, 4, HD), BF16) for _ in range(2)]
        VS = [Tl() for _ in range(2)]
        cs2 = [AR.take((T,), F32) for _ in range(2)]
        CS = [Tl() for _ in range(2)]
        rstd_b = AR.take((T,), F32)
        RB = Tl()
        rstd_c = AR.take((4,), F32)
        RC = Tl()

        def load_x(stream, src_rows_ap, nblk, b0=0):
            for b in range(nblk):
                dma(stream, xrow[:, b0 + b, :], src_rows_ap[b * 128:(b + 1) * 128, :], [], [XR[b0 + b]])

        def transpose_x(nblk, with_lo=False):
            for b in range(nblk):
                cp(("pool", "dve")[b % 2], hi16[:, b, :], xrow[:, b, :], [XR[b]], [HI[b]])
                if with_lo:
                    tt("dve", lo16[:, b, :], xrow[:, b, :], hi16[:, b, :], ALU.subtract, [XR[b], HI[b]], [LO[b]])
            n = nblk * 128
            for dc in range(DC):
                pi = ps_next()
                for b in range(nblk):
                    tr16(pi, b * 128, hi16[:, b, dc * 128:(dc + 1) * 128], [HI[b]])
                if with_lo:
                    for b in range(nblk):
                        tr16(pi, 512 + b * 128, lo16[:, b, dc * 128:(dc + 1) * 128], [LO[b]])
                cp("act" if with_lo else ev_eng(), xT16[:, dc, 0:n], psums16[pi][:, 0:n], [PS[pi]], [XT[dc]])
                if with_lo:
                    tq = dc % 2
                    act(hT32[:, dc, :], psums16[pi][:, 0:512], AF.Copy, [PS[pi]], [HT[dc]], scale=ALPHA)
                    act(sm32[tq][:, :], psums16[pi][:, 512:1024], AF.Copy, [PS[pi]], [SM[tq]], scale=ALPHA)
                    tt("dve", hT32[:, dc, :], hT32[:, dc, :], sm32[tq][:, :], ALU.add, [HT[dc], SM[tq]], [HT[dc]])

        def rstd_from(pi, n, out_ap, out_t, d_in, eps_t):
            act(out_ap, psums[pi][:, 0:n], AF.Sqrt, [PS[pi], c0], [out_t], bias=eps_t[:, 0:1], scale=1.0 / d_in)
            S.op("dve", "reciprocal", (out_ap, out_ap), {}, reads=[out_t], writes=[out_t])

        for k in ([] if "mem" in SKIP else [kk for kk in memin if ("mem" + kk) not in SKIP]):
            load_x("act", memin[k], 2)
            transpose_x(2)
            for h in ([] if "memkc" in SKIP else range(C_HEADS)):
                pi = ps_next()
                for kc in range(DC):
                    mm(pi, 128, N_MEM, wmem[:, kc, h * 128:(h + 1) * 128], xT16[:, kc, 0:N_MEM], kc == 0, kc == DC - 1,
                       [P1W, XT[kc]])
                cp(ev_eng(), kcT[k][:, h, :], psums[pi][:, 0:N_MEM], [PS[pi]], [KC_T[k]])
            for mb in ([] if "memvc" in SKIP else range(2)):
                pi = ps_next()
                for kc in range(DC):
                    mm(pi, 128, 512, xT16[:, kc, mb * 128:(mb + 1) * 128], wmem[:, kc, 512:1024], kc == 0, kc == DC - 1,
                       [P1W, XT[kc]])
                cp(ev_eng(), vc[k][:, mb, :], psums[pi][:, :], [PS[pi]], [VC_T[k]])

        NOBAR = os.environ.get("KNOBAR", "p1tile").split(",")

        def SB(name):
            if name not in NOBAR and "all" not in NOBAR:
                S.barrier()

        SB("mem")
        tcount = 0
        p1tiles = [(k, t) for k in "ps" for t in range(SEQ[k] // T)]
        if STOP == "p1a":
            p1tiles = []
            tiles_override = True
        if STOP.startswith("p1n"):
            p1tiles = p1tiles[:int(STOP[3:])]
        XQ = os.environ.get("KXQ", "sp")
        if p1tiles:
            load_x(XQ, xin[p1tiles[0][0]][0:T, :], 4)
        for p1i, (k, t) in enumerate(p1tiles):
            if True:
                b2 = tcount % 2
                tcount += 1
                if "cs" not in SKIP:
                    dma(XQ, cs2[b2][:, :], csin[k][:, t * T:(t + 1) * T], [], [CS[b2]])
                transpose_x(4)
                if p1i + 1 < len(p1tiles):
                    k2, t2 = p1tiles[p1i + 1]
                    load_x(XQ, xin[k2][t2 * T:(t2 + 1) * T, :], 4)
                for m in range(4):
                    pi = ps_next()
                    for kc in range(DC):
                        mm(pi, 128, T, wckv[:, kc, m * 128:(m + 1) * 128], xT16[:, kc, :], kc == 0, kc == DC - 1,
                           [P1W, XT[kc]])
                    cp("dve", ckvT[:, m, :], psums[pi][:, :], [PS[pi]], [CK[m]])
                    if "sq" not in SKIP:
                        act(sq16[:, m, :], psums[pi][:, :], AF.Square, [PS[pi], CK[m]], [SQ[m]])
                if 'rope' not in SKIP:
                    pi = ps_next()
                    for kc in range(DC):
                        mm(pi, 128, T, wckv[:, kc, 512:640], xT16[:, kc, :], kc == 0, kc == DC - 1, [P1W, XT[kc]])
                    tt("dve", krc[:, :], cs2[b2][:, :], psums[pi][:, :], ALU.mult, [PS[pi], CS[b2]], [KRC])
                    pi = ps_next()
                    mm(pi, 64, T, si16[:, :], krc[:, :], True, True, [KRC, c0])
                    cp("act", krT[b2][0:64, :], psums[pi][0:64, :], [PS[pi]], [KRT[b2]])
                    if "kr" not in SKIP:
                        dma("pool", KrT[k][:, t * T:(t + 1) * T], krT[b2][0:64, :], [KRT[b2]], [])
                if 'stats' not in SKIP:
                    pi = ps_next()
                    for m in range(4):
                        mm(pi, 128, T, ones16[:, :], sq16[:, m, :], m == 0, m == 3, [SQ[m], c0])
                    rstd_from(pi, T, rstd_b[:, :], RB, 512.0, eps_rms)
                    pi = ps_next()
                    for b in range(4):
                        for m in range(4):
                            mm(pi, 128, 1, sq16[:, m, b * 128:(b + 1) * 128], ones16[:, 0:1], m == 0, m == 3, [SQ[m], c0],
                               n0=b)
                    rstd_from(pi, 4, rstd_c[:, :], RC, 512.0, eps_rms)
                if 'kn_c' not in SKIP:
                    for h in range(B_HEADS):
                        pi = ps_next()
                        for kc in range(4):
                            mm(pi, 128, T, wukv[:, kc, h * 128:(h + 1) * 128], ckvT[:, kc, :], kc == 0, kc == 3,
                               [P1W, CK[kc]])
                        tt("dve", knst[b2][:, h, :], rstd_b[:, :], psums[pi][:, :], ALU.mult, [PS[pi], RB], [KNS[b2]])
                    if "kn" not in SKIP:
                        dma("pool", KnT[k][:, :, t * T:(t + 1) * T].rearrange("h p n -> p h n"), knst[b2][:, :, :],
                            [KNS[b2]], [])
                if 'vv' not in SKIP:
                    for b in range(4):
                        pa, pb = ps_next(), ps_next()
                        for kc in range(4):
                            mm(pa, 128, 512, ckvT[:, kc, b * 128:(b + 1) * 128], wukv[:, kc, 768:1280], kc == 0, kc == 3,
                               [P1W, CK[kc]])
                        for kc in range(4):
                            mm(pb, 128, 256, ckvT[:, kc, b * 128:(b + 1) * 128], wukv[:, kc, 1280:1536], kc == 0, kc == 3,
                               [P1W, CK[kc]])
                        act(vst[b2][:, 0:4, b, :], psums[pa][:, :].rearrange("p (h d) -> p h d", h=4), AF.Copy,
                            [PS[pa], RC], [VS[b2]], scale=rstd_c[:, b:b + 1])
                        act(vst[b2][:, 4:6, b, :], psums[pb][:, 0:256].rearrange("p (h d) -> p h d", h=2), AF.Copy,
                            [PS[pb], RC], [VS[b2]], scale=rstd_c[:, b:b + 1])
                    if "v" not in SKIP:
                        dma("pool", Vsc[k][:, :, t * 4:(t + 1) * 4, :].rearrange("h p b d -> p h b d"), vst[b2][:, :, :, :],
                            [VS[b2]], [])
                SB("p1tile")
        S.barrier()

        AR.reset()
        hT32 = AR.take((DC, T), F32)
        HT = [Tl() for _ in range(DC)]
        xT16 = AR.take((DC, T), BF16)
        XT = [Tl() for _ in range(DC)]
        wbuf = [AR.take((UNIT,), BF16) for _ in range(NWBUF)]
        WB = [Tl() for _ in range(NWBUF)]
        base_b = AR.off
        xrow = AR.take((4, D), F32)
        XR = [Tl() for _ in range(4)]
        qaT = AR.take((A_HEADS, T), BF16)
        QA = [Tl() for _ in range(A_HEADS)]
        kaT = AR.take((A_KV, 6 * 128), BF16)
        KA = [Tl() for _ in range(A_KV)]
        va = AR.take((6, A_KV * HD), BF16)
        VA = Tl()
        cqT = AR.take((4, T), BF16)
        CQ = [Tl() for _ in range(4)]
        hilo_base = AR.off
        qnT = AR.take((B_HEADS, T), BF16)
        QN = [Tl() for _ in range(B_HEADS)]
        qrT = AR.take((B_HEADS, T), BF16)
        QR = [Tl() for _ in range(B_HEADS)]
        qcT = AR.take((C_HEADS, T), BF16)
        QC = [Tl() for _ in range(C_HEADS)]
        aoT = AR.take((A_HEADS, T), BF16)
        AO = [Tl() for _ in range(A_HEADS)]
        boT = AR.take((B_HEADS, T), BF16)
        BO = [Tl() for _ in range(B_HEADS)]
        coT = AR.take((C_HEADS, T), BF16)
        CO = [Tl() for _ in range(C_HEADS)]
        hilo_end = AR.off
        AR.reset(hilo_base)
        hi16 = AR.take((4, D), BF16)
        lo16 = AR.take((4, D), BF16)
        assert AR.off <= hilo_end, "hi/lo overlay too big"
        AR.reset(hilo_end)
        HI = [Tl() for _ in range(4)]
        LO = [Tl() for _ in range(4)]
        mgT = AR.take((DC, T), BF16)
        MG = [Tl() for _ in range(DC)]
        sm32 = [AR.take((T,), F32) for _ in range(6)]
        SM = [Tl() for _ in range(6)]
        end_attn = AR.off
        AR.reset(base_b)
        KCH = min(2048, P_SEQ, S_SEQ)
        knb = [AR.take((KCH,), BF16) for _ in range(2)]
        krb = [AR.take((KCH,), BF16) for _ in range(2)]
        vb = [AR.take((KCH // 128, HD), BF16) for _ in range(2)]
        KVB = [Tl() for _ in range(2)]
        pT = [AR.take((T,), BF16) for _ in range(3)]
        PT = [Tl() for _ in range(3)]
        acc = [AR.take((T,), F32) for _ in range(2)]
        ACC = [Tl() for _ in range(2)]
        assert AR.off <= base_b + 4 * D, "kv stream overflows xrow region"
        AR.reset(base_b)
        actT = AR.take((FC, T), BF16)
        ystage = [AR.take((D,), F32) for _ in range(2)]
        ylo = AR.take((DC, T), BF16)
        YL = [Tl() for _ in range(DC)]
        fsc = [AR.take((T,), F32) for _ in range(2)]
        FSC = [Tl() for _ in range(2)]
        assert AR.off <= end_attn, ("ffn region", AR.off, end_attn)
        ATT_ALL = XR + QA + KA + [VA] + CQ + QN + QR + QC + AO + BO + CO + MG + SM + KVB + PT + ACC + HI + LO
        ACTT = [Tl() for _ in range(FC)]
        YS = [Tl() for _ in range(2)]
        FFN_ALL = ACTT + YS + YL + FSC

        def fence_region(old, new):
            ws = []
            rs = {}
            rds = []
            for t in old:
                if t.w is not None:
                    ws.append(t.w)
                for e, o in t.r.items():
                    rs[(e, id(o))] = o
                rds.extend(t.rd)
            for t in new:
                t.w = None
                t.r = {}
                t.rd = list(rds) + ws + list(rs.values())

        wctr = [0]
        wload = [0]
        total_units = [0]

        def w_prefetch():
            while wload[0] < min(wctr[0] + NWBUF, total_units[0]):
                i = wload[0]
                u = i % NU
                n = sum(s[2] * s[4] for s in units[u])
                dma("sp", wbuf[i % NWBUF][:, 0:n], wst[u, :, 0:n], [], [WB[i % NWBUF]])
                wload[0] += 1

        def w_next():
            w_prefetch()
            i = wctr[0]
            wctr[0] += 1
            b = i % NWBUF
            u = i % NU
            views = []
            off = 0
            for seg in units[u]:
                kcs, ncols = seg[2], seg[4]
                views.append(wbuf[b][:, off:off + kcs * ncols].rearrange("p (k n) -> p k n", k=kcs))
                off += kcs * ncols
            return WB[b], views

        tiles = [(k, t) for k in "ps" for t in range(OWN[k] // T)]
        if STOP.startswith("p1"):
            tiles = []
        total_units[0] = NU * len(tiles)

        def layer_norm(gname, bname, to_bf16):
            mean_b, msq, var_b, nmr = sm32[0], sm32[1], sm32[2], sm32[3]
            ts("dve", mean_b[:, :], psums[5][:, :], 1.0 / D, None, ALU.mult, None, [PS[5]], [SM[0]])
            tt("dve", msq[:, :], mean_b[:, :], mean_b[:, :], ALU.mult, [SM[0]], [SM[1]])
            stt(var_b[:, :], psums[6][:, :], 1.0 / D, msq[:, :], ALU.mult, ALU.subtract, [PS[6], SM[1]], [SM[2]])
            act(var_b[:, :], var_b[:, :], AF.Sqrt, [SM[2], c0], [SM[2]], bias=eps_ln[:, 0:1], scale=1.0)
            S.op("dve", "reciprocal", (var_b[:, :], var_b[:, :]), {}, reads=[SM[2]], writes=[SM[2]])
            stt(nmr[:, :], mean_b[:, :], -1.0, var_b[:, :], ALU.mult, ALU.mult, [SM[0], SM[2]], [SM[3]])
            for c in range(DC):
                tt("dve", hT32[:, c, :], hT32[:, c, :], var_b[:, :], ALU.mult, [HT[c], SM[2]], [HT[c]])
                tt("dve", hT32[:, c, :], hT32[:, c, :], nmr[:, :], ALU.add, [HT[c], SM[3]], [HT[c]])
                act(hT32[:, c, :], hT32[:, c, :], AF.Identity, [HT[c], c0], [HT[c]],
                    bias=lncol[bname][:, c:c + 1], scale=lncol[gname][:, c:c + 1])
                if to_bf16:
                    cp("pool", xT16[:, c, :], hT32[:, c, :], [HT[c]], [XT[c]])

        def stats_accum(c):
            u16, s16 = pT[0], pT[1]
            cp("act", u16[:, :], hT32[:, c, :], [HT[c]], [PT[0]])
            act(s16[:, :], hT32[:, c, :], AF.Square, [HT[c]], [PT[1]])
            mm(5, 128, T, ones16[:, :], u16[:, :], c == 0, c == DC - 1, [PT[0], c0])
            mm(6, 128, T, ones16[:, :], s16[:, :], c == 0, c == DC - 1, [PT[1], c0])

        for ti, (k, t) in enumerate(tiles):
            Sq = SEQ[k]
            nt_own = OWN[k] // T
            r0 = t * T
            fence_region(FFN_ALL, ATT_ALL)
            prev0 = (r0 - 128) % Sq
            next0 = (r0 + T) % Sq
            dma("act", xrow[:, 0, :], xin[k][prev0:prev0 + 128, :], [], [XR[0]])
            dma("act", xrow[:, 1, :], xin[k][next0:next0 + 128, :], [], [XR[1]])
            xh = mgT[:, :, 0:256]
            for b in range(2):
                cp(("pool", "dve")[b % 2], hi16[:, b, :], xrow[:, b, :], [XR[b]], [HI[b]])
            for dc in range(DC):
                pi = ps_next()
                for b in range(2):
                    tr16(pi, b * 128, hi16[:, b, dc * 128:(dc + 1) * 128], [HI[b]])
                cp(ev_eng(), xh[:, dc, :], psums16[pi][:, 0:256], [PS[pi]], [MG[dc]])
            load_x("act", xin[k][r0:r0 + T, :], 4)
            dma("act", sm32[4][:, :], csin[k][:, r0:r0 + T], [], [SM[4]])

            transpose_x(4, True)
            fence_region(XR, KVB + PT + ACC)
            fence_region(HI + LO, QN + QR + QC + AO + BO + CO)
            for b_ in range(2):
                S.op("dve", "memset", (krb[b_][64:128, :], 0.0), {}, writes=[KVB[b_]])
            for h_ in range(B_HEADS):
                S.op("dve", "memset", (qrT[64:128, h_, :], 0.0), {}, writes=[QR[h_]])
            for i in range(3):
                wt, (wv,) = w_next()
                for j in range(2):
                    m = 2 * i + j
                    pi = ps_next()
                    for kc in range(DC):
                        mm(pi, 128, T, wv[:, kc, j * 128:(j + 1) * 128], xT16[:, kc, :], kc == 0, kc == DC - 1,
                           [wt, XT[kc]])
                    cp(ev_eng(), qaT[:, m, :], psums[pi][:, :], [PS[pi]], [QA[m]])
            wt, (wv,) = w_next()
            for g in range(A_KV):
                pi = ps_next()
                for kc in range(DC):
                    mm(pi, 128, T, wv[:, kc, g * 128:(g + 1) * 128], xT16[:, kc, :], kc == 0, kc == DC - 1, [wt, XT[kc]])
                cp(ev_eng(), kaT[:, g, 128:640], psums[pi][:, :], [PS[pi]], [KA[g]])
                pi = ps_next()
                for kc in range(DC):
                    mm(pi, 128, 256, wv[:, kc, g * 128:(g + 1) * 128], xh[:, kc, :], kc == 0, kc == DC - 1, [wt, MG[kc]])
                cp(ev_eng(), kaT[:, g, 0:128], psums[pi][:, 0:128], [PS[pi]], [KA[g]])
                cp(ev_eng(), kaT[:, g, 640:768], psums[pi][:, 128:256], [PS[pi]], [KA[g]])
            wt, (wv,) = w_next()
            for b in range(6):
                pi = ps_next()
                for kc in range(DC):
                    if b == 0:
                        l = xh[:, kc, 0:128]
                        rd = MG[kc]
                    elif b == 5:
                        l = xh[:, kc, 128:256]
                        rd = MG[kc]
                    else:
                        l = xT16[:, kc, (b - 1) * 128:b * 128]
                        rd = XT[kc]
                    mm(pi, 128, 256, l, wv[:, kc, :], kc == 0, kc == DC - 1, [wt, rd])
                cp(ev_eng(), va[:, b, :], psums[pi][:, 0:256], [PS[pi]], [VA])
            for i in range(2):
                wt, (wv,) = w_next()
                for j in range(2):
                    m = 2 * i + j
                    pi = ps_next()
                    for kc in range(DC):
                        mm(pi, 128, T, wv[:, kc, j * 128:(j + 1) * 128], xT16[:, kc, :], kc == 0, kc == DC - 1,
                           [wt, XT[kc]])
                    cp("dve", cqT[:, m, :], psums[pi][:, :], [PS[pi]], [CQ[m]])
                    act(pT[m % 3][:, :], psums[pi][:, :], AF.Square, [PS[pi], CQ[m]], [PT[m % 3]])
                    mm(7, 128, T, ones16[:, :], pT[m % 3][:, :], m == 0, m == 3, [PT[m % 3], c0])
            rq = sm32[5]
            rstd_from(7, T, rq[:, :], SM[5], 512.0, eps_rms)
            csr = sm32[4]
            tt("dve", csr[:, :], csr[:, :], rq[:, :], ALU.mult, [SM[4], SM[5]], [SM[4]])
            for i in range(2):
                wt, (wv,) = w_next()
                for j in range(2):
                    m = 2 * i + j
                    pi = ps_next()
                    for kc in range(DC):
                        mm(pi, 128, T, wv[:, kc, j * 128:(j + 1) * 128], xT16[:, kc, :], kc == 0, kc == DC - 1,
                           [wt, XT[kc]])
                    cp(ev_eng(), qcT[:, m, :], psums[pi][:, :], [PS[pi]], [QC[m]])
            for half in range(2):
                wt, wvs = w_next()
                for hh in range(3):
                    h = 3 * half + hh
                    pi = ps_next()
                    for kc in range(4):
                        mm(pi, 128, T, wvs[2 * hh][:, kc, :], cqT[:, kc, :], kc == 0, kc == 3, [wt, CQ[kc]])
                    tt("dve", qnT[:, h, :], rq[:, :], psums[pi][:, :], ALU.mult, [PS[pi], SM[5]], [QN[h]])
                    pi = ps_next()
                    for kc in range(4):
                        mm(pi, 128, T, wvs[2 * hh + 1][:, kc, :], cqT[:, kc, :], kc == 0, kc == 3, [wt, CQ[kc]])
                    tmp = pT[hh]
                    tt("dve", tmp[:, :], csr[:, :], psums[pi][:, :], ALU.mult, [PS[pi], SM[4]], [PT[hh]])
                    pi = ps_next()
                    mm(pi, 64, T, si16[:, :], tmp[:, :], True, True, [PT[hh], c0])
                    cp("act", qrT[0:64, h, :], psums[pi][0:64, :], [PS[pi]], [QR[h]])
            dbg_dump("qnT", qnT[:, :, :], (128, B_HEADS, T), BF16, QN)
            dbg_dump("qrT", qrT[0:64, :, :], (64, B_HEADS, T), BF16, QR)

            sc_a = HD ** -0.5
            for n in range(4):
                for g in range(A_KV):
                    pO, pL = 5, 6
                    for j in range(3):
                        pi = ps_next()
                        kb = n + j
                        mm(pi, 128, 384, kaT[:, g, kb * 128:(kb + 1) * 128], qaT[:, 3 * g:3 * g + 3, n * 128:(n + 1) * 128],
                           True, True, [KA[g]] + QA[3 * g:3 * g + 3])
                        s32 = sm32[j % 2]
                        stt(s32[:, 0:384].rearrange("p (h q) -> p h q", h=3), psums[pi][:, 0:384].rearrange("p (h q) -> p h q", h=3),
                            sc_a, biasT[:, 3 * g:3 * g + 3, j, :], ALU.mult, ALU.add, [PS[pi], c0], [SM[j % 2]])
                        p16 = pT[j]
                        act(p16[:, 0:384], s32[:, 0:384], AF.Exp, [SM[j % 2]], [PT[j]])
                        first_edge = (t == 0 and n == 0 and j == 0)
                        last_edge = (t == nt_own - 1 and n == 3 and j == 2)
                        if first_edge or last_edge:
                            fc = (0 if first_edge else 1) + (0 if k == "p" else 2)
                            ts("dve", p16[:, 0:384], p16[:, 0:384], flags[:, fc:fc + 1], None, ALU.mult, None,
                               [PT[j], c0], [PT[j]])
                        mm(pO, 128, 384, va[:, kb, g * 128:(g + 1) * 128], p16[:, 0:384], j == 0, j == 2, [VA, PT[j]])
                        mm(pL, 128, 384, ones16[:, :], p16[:, 0:384], j == 0, j == 2, [PT[j], c0])
                    l32 = sm32[2]
                    for r in range(3):
                        h = 3 * g + r
                        ts("dve", l32[:, r * 128:(r + 1) * 128], psums[pL][:, r * 128:(r + 1) * 128], se[:, h:h + 1], None,
                           ALU.add, None, [PS[pL], c0], [SM[2]])
                    S.op("dve", "reciprocal", (l32[:, 0:384], l32[:, 0:384]), {}, reads=[SM[2]], writes=[SM[2]])
                    tt("dve", aoT[:, 3 * g:3 * g + 3, n * 128:(n + 1) * 128], l32[:, 0:384].rearrange("p (h q) -> p h q", h=3),
                       psums[pO][:, 0:384].rearrange("p (h q) -> p h q", h=3), ALU.mult, [PS[pO], SM[2]], AO[3 * g:3 * g + 3])
            dbg_dump("aoT", aoT[:, :, :], (128, A_HEADS, T), BF16, AO)

            for h in range(C_HEADS):
                pO, pL = 5, 6
                for mb in range(2):
                    pi = ps_next()
                    mm(pi, 128, T, kcT[k][:, h, mb * 128:(mb + 1) * 128], qcT[:, h, :], True, True, [KC_T[k], QC[h]])
                    act(pT[mb][:, :], psums[pi][:, :], AF.Exp, [PS[pi]], [PT[mb]], scale=sc_a)
                    mm(pO, 128, T, vc[k][:, mb, h * 128:(h + 1) * 128], pT[mb][:, :], mb == 0, mb == 1, [VC_T[k], PT[mb]])
                    mm(pL, 128, T, ones16[:, :], pT[mb][:, :], mb == 0, mb == 1, [PT[mb], c0])
                S.op("dve", "reciprocal", (sm32[2][:, :], psums[pL][:, :]), {}, reads=[PS[pL]], writes=[SM[2]])
                tt("dve", coT[:, h, :], sm32[2][:, :], psums[pO][:, :], ALU.mult, [PS[pO], SM[2]], [CO[h]])
            dbg_dump("coT", coT[:, :, :], (128, C_HEADS, T), BF16, CO)

            sc_b = (128 + 64) ** -0.5
            nch = Sq // KCH
            kvi = 0
            items = [(h_, c_) for h_ in range(B_HEADS) for c_ in range(nch)]

            def kv_load(idx):
                h_, c_ = items[idx]
                b_ = idx % 2
                dma("act", knb[b_][:, :], KnT[k][h_, :, c_ * KCH:(c_ + 1) * KCH], [], [KVB[b_]])
                dma("act", krb[b_][0:64, :], KrT[k][:, c_ * KCH:(c_ + 1) * KCH], [], [KVB[b_]])
                dma("act", vb[b_][:, :, :], Vsc[k][h_, :, c_ * (KCH // 128):(c_ + 1) * (KCH // 128), :], [], [KVB[b_]])

            LOOK = 2
            ntile_ch = KCH // 128
            nkt = Sq // 128
            allk = [(h_, c_, kt_) for h_ in range(B_HEADS) for c_ in range(nch) for kt_ in range(ntile_ch)]
            kv_load(0)
            if len(items) > 1:
                kv_load(1)
            pend = {}

            def emit_S(i):
                h_, c_, kt_ = allk[i]
                bb_ = (h_ * nch + c_) % 2
                pi_ = ps_next()
                mm(pi_, 128, T, knb[bb_][:, kt_ * 128:(kt_ + 1) * 128], qnT[:, h_, :], True, False, [KVB[bb_], QN[h_]])
                mm(pi_, 128, T, krb[bb_][:, kt_ * 128:(kt_ + 1) * 128], qrT[:, h_, :], False, True, [KVB[bb_], QR[h_]])
                pend[i] = pi_

            for i in range(min(LOOK, len(allk))):
                emit_S(i)
            pO, pL = 5, 6
            for gi, (h, ch, kt) in enumerate(allk):
                if gi + LOOK < len(allk):
                    emit_S(gi + LOOK)
                pi = pend.pop(gi)
                g_ch = h * nch + ch
                bb = g_ch % 2
                it = ch * ntile_ch + kt
                pp = gi % 3
                act(pT[pp][:, :], psums[pi][:, :], AF.Exp, [PS[pi]], [PT[pp]], scale=sc_b)
                a = it % 2
                if it < 2:
                    cp("dve", acc[a][:, :], pT[pp][:, :], [PT[pp]], [ACC[a]])
                else:
                    tt("dve", acc[a][:, :], acc[a][:, :], pT[pp][:, :], ALU.add, [ACC[a], PT[pp]], [ACC[a]])
                mm(pO, 128, T, vb[bb][:, kt, :], pT[pp][:, :], it == 0, it == nkt - 1, [KVB[bb], PT[pp]])
                if kt == ntile_ch - 1 and g_ch + 2 < len(items):
                    kv_load(g_ch + 2)
                if it == nkt - 1:
                    tt("dve", acc[0][:, :], acc[0][:, :], acc[1][:, :], ALU.add, [ACC[0], ACC[1]], [ACC[0]])
                    cp("dve", sm32[0][:, :].bitcast(BF16)[:, 0:T], acc[0][:, :], [ACC[0]], [SM[0]])
                    tt("dve", sm32[1][:, :].bitcast(BF16)[:, 0:T], acc[0][:, :], sm32[0][:, :].bitcast(BF16)[:, 0:T],
                       ALU.subtract, [ACC[0], SM[0]], [SM[1]])
                    mm(pL, 128, T, ones16[:, :], sm32[0][:, :].bitcast(BF16)[:, 0:T], True, False, [SM[0], c0])
                    mm(pL, 128, T, ones16[:, :], sm32[1][:, :].bitcast(BF16)[:, 0:T], False, True, [SM[1], c0])
                    S.op("dve", "reciprocal", (sm32[2][:, :], psums[pL][:, :]), {}, reads=[PS[pL]], writes=[SM[2]])
                    tt("dve", boT[:, h, :], sm32[2][:, :], psums[pO][:, :], ALU.mult, [PS[pO], SM[2]], [BO[h]])
            dbg_dump("boT", boT[:, :, :], (128, B_HEADS, T), BF16, BO)

            for c in range(DC):
                gts = []

                def gate(bi, wt, wg):
                    pi = ps_next()
                    for kc in range(DC):
                        mm(pi, 128, T, wg[:, kc, :], xT16[:, kc, :], kc == 0, kc == DC - 1, [wt, XT[kc]])
                    gt = sm32[bi]
                    act(gt[:, :], psums[pi][:, :], AF.Sigmoid, [PS[pi], c0], [SM[bi]],
                        bias=bgc[:, bi * 16 + c:bi * 16 + c + 1], scale=1.0)
                    gts.append(gt)

                wt1, (wg0, wg1) = w_next()
                gate(0, wt1, wg0)
                gate(1, wt1, wg1)
                wt2, (wg2, wa, wb_, wc) = w_next()
                gate(2, wt2, wg2)
                for bi, (wv, src, srct, nk) in enumerate(((wa, aoT, AO, 6), (wb_, boT, BO, 6), (wc, coT, CO, 4))):
                    pi = ps_next()
                    for kc in range(nk):
                        mm(pi, 128, T, wv[:, kc, :], src[:, kc, :], kc == 0, kc == nk - 1, [wt2, srct[kc]])
                    tt("dve", gts[bi][:, :], gts[bi][:, :], psums[pi][:, :], ALU.mult, [SM[bi], PS[pi]], [SM[bi]])
                tt("dve", gts[0][:, :], gts[0][:, :], gts[1][:, :], ALU.add, [SM[0], SM[1]], [SM[0]])
                tt("dve", mgT[:, c, :], gts[0][:, :], gts[2][:, :], ALU.add, [SM[0], SM[2]], [MG[c]])
            for i in range(8):
                wt, (wv,) = w_next()
                for j in range(2):
                    m = 2 * i + j
                    pi = ps_next()
                    for kc in range(DC):
                        mm(pi, 128, T, wv[:, kc, j * 128:(j + 1) * 128], mgT[:, kc, :], kc == 0, kc == DC - 1, [wt, MG[kc]])
                    tt("dve", hT32[:, m, :], hT32[:, m, :], psums[pi][:, :], ALU.add, [HT[m], PS[pi]], [HT[m]])
                    stats_accum(m)
            dbg_dump("pre1", hT32[:, :, :], (128, DC, T), F32, HT)
            layer_norm("ln1_g", "ln1_b", True)
            dbg_dump("mean1", sm32[0][:, :], (128, T), F32, [SM[0]])
            dbg_dump("rstd1", sm32[2][:, :], (128, T), F32, [SM[2]])
            dbg_dump("h1", hT32[:, :, :], (128, DC, T), F32, HT)
            fence_region(ATT_ALL, FFN_ALL)
            for j in range(FC):
                wt, (wgt, wup) = w_next()
                pg, pu = ps_next(), ps_next()
                for kc in range(DC):
                    mm(pg, 128, T, wgt[:, kc, :], xT16[:, kc, :], kc == 0, kc == DC - 1, [wt, XT[kc]])
                for kc in range(DC):
                    mm(pu, 128, T, wup[:, kc, :], xT16[:, kc, :], kc == 0, kc == DC - 1, [wt, XT[kc]])
                sg = sm32[3 + (j % 2)] if False else None
                fq = j % 2
                act(fsc[fq][:, :], psums[pg][:, :], AF.Silu, [PS[pg]], [FSC[fq]])
                tt("dve", actT[:, j, :], fsc[fq][:, :], psums[pu][:, :], ALU.mult, [FSC[fq], PS[pu]], [ACTT[j]])
            for m in range(DC):
                pi = ps_next()
                for half in range(2):
                    wt, (wv,) = w_next()
                    for kc in range(22):
                        jj = half * 22 + kc
                        mm(pi, 128, T, wv[:, kc, :], actT[:, jj, :], jj == 0, jj == FC - 1, [wt, ACTT[jj]])
                stt(hT32[:, m, :], hT32[:, m, :], ALPHA, psums[pi][:, :], ALU.mult, ALU.add, [HT[m], PS[pi]], [HT[m]])
                u16 = ystage[0].bitcast(BF16)[:, 0:T]
                s16 = ystage[0].bitcast(BF16)[:, T:2 * T]
                cp("act", u16, hT32[:, m, :], [HT[m]], [YS[0]])
                mm(5, 128, T, ones16[:, :], u16, m == 0, m == DC - 1, [YS[0], c0])
                act(s16, hT32[:, m, :], AF.Square, [HT[m]], [YS[0]])
                mm(6, 128, T, ones16[:, :], s16, m == 0, m == DC - 1, [YS[0], c0])
            sm_save = (sm32[0], sm32[1], sm32[2], sm32[3])
            SM_save = (SM[0], SM[1], SM[2], SM[3])
            ys1 = ystage[1]
            for q in range(4):
                sm32[q] = ys1[:, q * T:(q + 1) * T]
                SM[q] = YS[1]
            layer_norm("ln2_g", "ln2_b", False)
            for q in range(4):
                sm32[q] = sm_save[q]
                SM[q] = SM_save[q]
            for c in range(DC):
                cp(("pool", "act")[c % 2], xT16[:, c, :], hT32[:, c, :], [HT[c]], [XT[c]])
                tt("dve", ylo[:, c, :], hT32[:, c, :], xT16[:, c, :], ALU.subtract, [HT[c], XT[c]], [YL[c]])
            for b in range(4):
                yb = b % 2
                for c4 in range(4):
                    pi = ps_next()
                    for cc in range(4):
                        c = c4 * 4 + cc
                        tr16(pi, cc * 128, xT16[:, c, b * 128:(b + 1) * 128], [XT[c]])
                    for cc in range(4):
                        c = c4 * 4 + cc
                        tr16(pi, 512 + cc * 128, ylo[:, c, b * 128:(b + 1) * 128], [YL[c]])
                    fq = c4 % 2
                    cp("act", ystage[yb][:, c4 * 512:(c4 + 1) * 512], psums16[pi][:, 0:512], [PS[pi]], [YS[yb]])
                    cp("act", fsc[fq][:, :], psums16[pi][:, 512:1024], [PS[pi]], [FSC[fq]])
                    tt("dve", ystage[yb][:, c4 * 512:(c4 + 1) * 512], ystage[yb][:, c4 * 512:(c4 + 1) * 512],
                       fsc[fq][:, :], ALU.add, [YS[yb], FSC[fq]], [YS[yb]])
                dma("pool", yout[k][r0 + b * 128:r0 + (b + 1) * 128, :], ystage[yb][:, :], [YS[yb]], [])
        S.finish()

        sem_ctx = {}
        for e in ("pe", "act", "dve", "pool"):
            sem_ctx[e] = es.enter_context(nc.semaphore("s_" + e))
        for e in ("act", "pool", "sp"):
            for i in range(NSLOT):
                sem_ctx[(e, i)] = es.enter_context(nc.semaphore("d_%s%d" % (e, i)))
        block = es.enter_context(nc.Block())
        S.emit(nc, block, sem_ctx)
    return nc, S, list(dbg_out.keys())


def _t5_bucket(rel):
    half = 16
    max_exact = 8
    ret = (rel > 0).astype(np.int32) * half
    n = np.abs(rel)
    large = max_exact + (np.log(np.maximum(n, 1).astype(np.float32) / max_exact)
                         / math.log(128 / max_exact) * (half - max_exact)).astype(np.int32)
    large = np.minimum(large, half - 1)
    return ret + np.where(n < max_exact, n, large)


def _constants():
    ident = np.eye(128, dtype=np.float32)
    si = np.zeros((128, 64), np.float32)
    si[np.arange(64), np.arange(64)] = 1.0
    si[np.arange(64) + 64, np.arange(64)] = 1.0
    kk = np.arange(128)[:, None]
    qq = np.arange(128)[None, :]
    sel = np.zeros((32, 3, 128, 128), np.float32)
    msk = np.zeros((128, 3, 128), np.float32)
    for j in range(3):
        rel = (j - 1) * 128 + kk - qq
        b = _t5_bucket(rel)
        for bb in range(32):
            sel[bb, j][b == bb] = 1.0
        msk[:, j, :] = np.where(np.abs(rel) <= 128, 0.0, NEG)
    return ident, si, sel.reshape(32, -1), msk.reshape(128, -1)


def _rope_cs(pos):
    half = 32
    inv = (1.0 / (np.float32(10000.0) ** (np.arange(half, dtype=np.float32) / np.float32(half)))).astype(np.float32)
    ang = (pos.astype(np.float32)[None, :] * inv[:, None]).astype(np.float32)
    c, s = np.cos(ang).astype(np.float32), np.sin(ang).astype(np.float32)
    return np.ascontiguousarray(np.concatenate([c, c, s, s], axis=0))


def make_in_maps(inp, n_seq_cores, n_cores):
    xp_all = np.asarray(inp["x_prompt"])
    xs_all = np.asarray(inp["x_sample"])[0]
    P_SEQ = xp_all.shape[1]
    S_SEQ = xs_all.shape[0]
    own_p = P_SEQ // n_seq_cores
    own_s = S_SEQ // n_cores
    ident, si, sel, msk = _constants()
    shared = {"ident": ident, "stack_ident": si, "t5_sel": sel, "win_mask": msk,
              "rel_bias": np.asarray(inp["rel_bias"], np.float32)}
    for n in WNAMES:
        shared[n] = np.ascontiguousarray(np.asarray(inp[n])[0])
    for n in ("q_norm_g", "kv_norm_g", "b_gate", "ln1_g", "ln1_b", "ln2_g", "ln2_b"):
        v = np.asarray(inp[n], np.float32).reshape(-1)
        shared[n] = np.ascontiguousarray(v.reshape(-1, 128).T)
    shared["sink"] = np.ascontiguousarray(np.broadcast_to(np.asarray(inp["sink"], np.float32).reshape(1, 6), (128, 6)))
    maps = []
    for c in range(n_cores):
        seq, pc = c // n_seq_cores, c % n_seq_cores
        m = dict(shared)
        m["xp"] = np.ascontiguousarray(np.roll(xp_all[seq], -pc * own_p, axis=0))
        m["xs"] = np.ascontiguousarray(np.roll(xs_all, -c * own_s, axis=0))
        m["mem_p"] = np.ascontiguousarray(np.asarray(inp["mem_prompt"])[seq])
        m["mem_s"] = np.ascontiguousarray(np.asarray(inp["mem_sample"])[0])
        m["cs_p"] = _rope_cs((np.arange(P_SEQ) + pc * own_p) % P_SEQ)
        m["cs_s"] = _rope_cs((np.arange(S_SEQ) + c * own_s) % S_SEQ)
        fl = np.zeros((128, 4), np.float32)
        fl[:, 0] = 0.0 if pc == 0 else 1.0
        fl[:, 1] = 0.0 if pc == n_seq_cores - 1 else 1.0
        fl[:, 2] = 0.0 if c == 0 else 1.0
        fl[:, 3] = 0.0 if c == n_cores - 1 else 1.0
        m["flags"] = fl
        maps.append(m)
    return maps, P_SEQ, S_SEQ, own_p, own_s


def run(inp, dbg=()):
    n_cores = 8
    n_seq_cores = 4
    maps, P_SEQ, S_SEQ, own_p, own_s = make_in_maps(inp, n_seq_cores, n_cores)
    nc, S, dnames = build(P_SEQ, S_SEQ, own_p, own_s, dbg)
    if os.environ.get("KTRACE"):
        res = run_bass_kernel_spmd(nc, maps, core_ids=list(range(n_cores)), trace=True)
        print("EXEC_TIME_NS", res.exec_time_ns, flush=True)
    else:
        res = run_bass_kernel_spmd(nc, maps, core_ids=list(range(n_cores)))
    B = np.asarray(inp["x_prompt"]).shape[0]
    yp = np.zeros((B, P_SEQ, D), np.float32)
    ys = np.zeros((1, S_SEQ, D), np.float32)
    for c in range(n_cores):
        seq, pc = c // n_seq_cores, c % n_seq_cores
        r = res.results[c]
        yp[seq, pc * own_p:(pc + 1) * own_p] = r["y_p"]
        ys[0, c * own_s:(c + 1) * own_s] = r["y_s"]
    return (yp, ys), res, dnames


def kernel(**inputs):
    (yp, ys), _, _ = run(inputs)
    return (yp, ys)
```

```python
import math
import os
import numpy as np
import concourse.bass as bass
import concourse.mybir as mybir
from concourse.bass_utils import run_bass_kernel_spmd

F32 = mybir.dt.float32
BF16 = mybir.dt.bfloat16
AF = mybir.ActivationFunctionType
ALU = mybir.AluOpType

D = 2048
DC = 16
T = 512
HD = 128
A_HEADS, A_KV = 6, 2
B_HEADS = 6
C_HEADS = 4
N_MEM = 256
D_FF = 5632
FC = 44
ALPHA = 2.0 ** 0.25
LN_EPS = 1e-5
RMS_EPS = 1e-6
NEG = -1e30
UNIT = 4096
NWBUF = 3
NSLOT = 8


class Tl:
    __slots__ = ("w", "r", "rd", "const")

    def __init__(self, const=False):
        self.w = None
        self.r = {}
        self.rd = []
        self.const = const


class Op:
    __slots__ = ("eng", "name", "args", "kw", "deps", "sig", "val", "dma", "semkey", "idx")


class Sched:
    ENGS = ("pe", "act", "dve", "pool", "sp")

    def __init__(self):
        self.q = {e: [] for e in self.ENGS}
        self.ndma = {e: 0 for e in self.ENGS}
        self.slot_last = {}
        self.slot_cnt = {}
        self.fence = []
        self.fence_pending = set()
        self.nops = 0

    def op(self, eng, name, args=(), kw=None, reads=(), writes=(), dma=False):
        o = Op()
        o.eng, o.name, o.args, o.kw = eng, name, args, (kw or {})
        o.dma, o.sig, o.val = dma, False, 0
        deps = []
        for t in reads:
            if t.w is not None:
                deps.append(t.w)
        for t in writes:
            if t.w is not None:
                deps.append(t.w)
            deps.extend(t.r.values())
            deps.extend(t.rd)
        if dma:
            k = self.ndma[eng]
            self.ndma[eng] = k + 1
            s = (eng, k % NSLOT)
            o.semkey = s
            prev = self.slot_last.get(s)
            if prev is not None:
                deps.append(prev)
            self.slot_last[s] = o
            self.slot_cnt[s] = self.slot_cnt.get(s, 0) + 1
            o.val = 16 * self.slot_cnt[s]
        else:
            o.semkey = eng
        if eng in self.fence_pending:
            deps.extend(self.fence)
            self.fence_pending.discard(eng)
        dd = []
        seen = set()
        for d in deps:
            if d is o or id(d) in seen:
                continue
            seen.add(id(d))
            if (not d.dma) and (not dma) and d.eng == "pe" and eng == "pe":
                continue
            dd.append(d)
        best = {}
        for d in dd:
            b = best.get(d.semkey)
            if b is None or d.idx > b.idx:
                best[d.semkey] = d
        o.deps = list(best.values())
        for d in o.deps:
            d.sig = True
        o.idx = len(self.q[eng])
        for t in reads:
            if t.const:
                continue
            if dma:
                t.rd.append(o)
            else:
                t.r[eng] = o
        for t in writes:
            t.w = o
            t.r = {}
            t.rd = []
        self.q[eng].append(o)
        self.nops += 1
        return o

    def barrier(self):
        deps = []
        for e in self.ENGS:
            for o in reversed(self.q[e]):
                if not o.dma:
                    o.sig = True
                    deps.append(o)
                    break
        deps.extend(self.slot_last.values())
        self.fence = deps
        self.fence_pending = set(self.ENGS)

    def finish(self):
        for e in self.ENGS:
            lasts = [o for (s, o) in self.slot_last.items() if s[0] == e]
            if lasts:
                f = Op()
                f.eng, f.name, f.args, f.kw = e, None, (), {}
                f.dma, f.sig, f.val, f.semkey = False, False, 0, e
                f.idx = len(self.q[e])
                f.deps = lasts
                self.q[e].append(f)

    def emit(self, nc, block, sems):
        for e in self.ENGS:
            c = 0
            for o in self.q[e]:
                if o.dma:
                    continue
                if o.sig:
                    c += 1
                o.val = c
        handles = {"pe": block.tensor, "act": block.scalar, "dve": block.vector,
                   "pool": block.gpsimd, "sp": block.sync}
        for e in self.ENGS:
            ops = self.q[e]
            if not ops:
                continue

            def body(h, ops=ops):
                waited = {}
                for o in ops:
                    need = {}
                    for d in o.deps:
                        if need.get(d.semkey, 0) < d.val:
                            need[d.semkey] = d.val
                    for k, v in need.items():
                        if waited.get(k, 0) < v:
                            h.wait_ge(sems[k], v)
                            waited[k] = v
                    if o.name is None:
                        continue
                    ins = getattr(h, o.name)(*o.args, **o.kw)
                    if o.dma:
                        ins.then_inc(sems[o.semkey], 16)
                    elif o.sig:
                        ins.then_inc(sems[o.semkey], 1)

            handles[e](body)


def plan_units():
    U = []
    for i in range(3):
        U.append([("w_in", 0, 16, i * 256, 256, None)])
    U.append([("w_in", 0, 16, 768, 256, None)])
    U.append([("w_in", 0, 16, 1024, 256, None)])
    for i in range(2):
        U.append([("w_in", 0, 16, 1280 + i * 256, 256, None)])
    for i in range(2):
        U.append([("w_in", 0, 16, 2368 + i * 256, 256, None)])
    for half in range(2):
        segs = []
        for h in range(3 * half, 3 * half + 3):
            segs.append(("w_uq", 0, 4, h * 192, 128, "gq"))
            segs.append(("w_uq", 0, 4, h * 192 + 128, 128, "gq_rope"))
        U.append(segs)
    for c in range(DC):
        U.append([("w_gate", 0, 16, c * 128, 128, None), ("w_gate", 0, 16, D + c * 128, 128, None)])
        U.append([("w_gate", 0, 16, 2 * D + c * 128, 128, None), ("w_br_a", 0, 6, c * 128, 128, None),
                  ("w_br_b", 0, 6, c * 128, 128, None), ("w_br_c", 0, 4, c * 128, 128, None)])
    for i in range(8):
        U.append([("w_o", 0, 16, i * 256, 256, None)])
    for j in range(FC):
        U.append([("w_ffn_in", 0, 16, j * 128, 128, None), ("w_ffn_in", 0, 16, D_FF + j * 128, 128, None)])
    for m in range(DC):
        for half in range(2):
            U.append([("w_ffn_down", half * 22 * 128, 22, m * 128, 128, None)])
    return U


WNAMES = ["w_in", "w_uq", "w_ukv", "w_mem_kv", "w_gate", "w_br_a", "w_br_b", "w_br_c", "w_o",
          "w_ffn_in", "w_ffn_down"]
WSHAPES = {"w_in": (2048, 2880), "w_uq": (512, 1152), "w_ukv": (512, 1536), "w_mem_kv": (2048, 1024),
           "w_gate": (2048, 6144), "w_br_a": (768, 2048), "w_br_b": (768, 2048), "w_br_c": (512, 2048),
           "w_o": (2048, 2048), "w_ffn_in": (2048, 11264), "w_ffn_down": (5632, 2048)}


def _rs(n):
    names = "abcdefg"[:n]
    return names


def build(P_SEQ, S_SEQ, OWN_P, OWN_S, dbg=()):
    nc = bass.Bass("TRN2", target_bir_lowering=False)
    S = Sched()
    dbg = set(dbg)

    def din(name, shape, dt=F32):
        return nc.dram_tensor(name, list(shape), dt, kind="ExternalInput").ap()

    xin = {"p": din("xp", (P_SEQ, D)), "s": din("xs", (S_SEQ, D))}
    memin = {"p": din("mem_p", (N_MEM, D)), "s": din("mem_s", (N_MEM, D))}
    csin = {"p": din("cs_p", (128, P_SEQ)), "s": din("cs_s", (128, S_SEQ))}
    W = {n: din(n, WSHAPES[n]) for n in WNAMES}
    rel_bias = din("rel_bias", (32, 6))
    sink = din("sink", (128, 6))
    q_norm_g = din("q_norm_g", (128, 4))
    kv_norm_g = din("kv_norm_g", (128, 4))
    b_gate = din("b_gate", (128, 48))
    lnp = {n: din(n, (128, 16)) for n in ("ln1_g", "ln1_b", "ln2_g", "ln2_b")}
    flags_in = din("flags", (128, 4))
    ident_in = din("ident", (128, 128))
    si_in = din("stack_ident", (128, 64))
    sel_in = din("t5_sel", (32, 3 * 128 * 128))
    wmask_in = din("win_mask", (128, 3 * 128))
    yout = {"p": nc.dram_tensor("y_p", [OWN_P, D], F32, kind="ExternalOutput").ap(),
            "s": nc.dram_tensor("y_s", [OWN_S, D], F32, kind="ExternalOutput").ap()}
    SEQ = {"p": P_SEQ, "s": S_SEQ}
    OWN = {"p": OWN_P, "s": OWN_S}

    units = plan_units()
    NU = len(units)
    wst = nc.dram_tensor("wst", [NU, 128, UNIT], BF16, kind="Internal").ap()
    KnT = {k: nc.dram_tensor("knT_" + k, [B_HEADS, 128, SEQ[k]], BF16, kind="Internal").ap() for k in "ps"}
    KrT = {k: nc.dram_tensor("krT_" + k, [64, SEQ[k]], BF16, kind="Internal").ap() for k in "ps"}
    Vsc = {k: nc.dram_tensor("v_" + k, [B_HEADS, 128, SEQ[k] // 128, 128], BF16, kind="Internal").ap() for k in "ps"}
    biasd = nc.dram_tensor("biasd", [6, 3 * 128 * 128], F32, kind="Internal").ap()
    dbg_out = {}

    import contextlib
    with contextlib.ExitStack() as es:
        def sb(name, shape, dt):
            return es.enter_context(nc.sbuf_tensor(name, list(shape), dt))

        ARENA_F32 = 45 * 1024
        arena = sb("arena", (128, ARENA_F32), F32)
        psums = [es.enter_context(nc.psum_tensor("ps%d" % i, [128, 512], F32)) for i in range(8)]
        PS = [Tl() for _ in range(8)]

        ident32 = sb("ident32", (128, 128), F32)
        ident16 = sb("ident16", (128, 128), BF16)
        ones16 = sb("ones16", (128, 128), BF16)
        ones32 = sb("ones32", (128, 128), F32)
        si16 = sb("si16", (128, 64), BF16)
        si32 = sb("si32", (128, 64), F32)
        biasT = sb("biasT", (128, 6, 3, 128), F32)
        wmask = sb("wmask", (128, 3, 128), F32)
        se = sb("se", (128, 6), F32)
        flags = sb("flags_sb", (128, 4), F32)
        bgc = sb("bgc", (128, 48), F32)
        lncol = {n: sb(n + "_c", (128, 16), F32) for n in lnp}
        gqc = sb("gqc", (128, 4), F32)
        gkvc = sb("gkvc", (128, 4), F32)
        kcT = {k: sb("kcT_" + k, (128, C_HEADS, N_MEM), BF16) for k in "ps"}
        vc = {k: sb("vc_" + k, (128, 2, C_HEADS * HD), BF16) for k in "ps"}
        eps_ln = sb("eps_ln", (128, 1), F32)
        eps_rms = sb("eps_rms", (128, 1), F32)
        CONST = Tl(const=True)
        KC_T = {k: Tl() for k in "ps"}
        VC_T = {k: Tl() for k in "ps"}

        class Arena:
            def __init__(self):
                self.off = 0

            def reset(self, off=0):
                self.off = off

            def take(self, free, dt):
                n = int(np.prod(free))
                nb = n * (4 if dt == F32 else 2)
                nw = (nb + 63) // 64 * 16
                assert self.off + nw <= ARENA_F32, ("arena overflow", self.off, nw)
                ap = arena[:, self.off:self.off + nw]
                self.off += nw
                if dt != F32:
                    ap = ap.bitcast(dt)
                ap = ap[:, 0:n]
                if len(free) > 1:
                    names = _rs(len(free))
                    pat = "p (" + " ".join(names) + ") -> p " + " ".join(names)
                    ap = ap.rearrange(pat, **{names[i]: int(free[i]) for i in range(len(free))})
                return ap

        AR = Arena()

        psrr = [0]

        def ps_next(pool=(0, 1, 2, 3, 4)):
            i = pool[psrr[0] % len(pool)]
            psrr[0] += 1
            return i

        def mm(pi, M, N, lhsT, rhs, start, stop, reads, n0=0):
            S.op("pe", "matmul", (psums[pi][0:M, n0:n0 + N],), dict(lhsT=lhsT, rhs=rhs, start=start, stop=stop),
                 reads=reads, writes=[PS[pi]])

        def tr(pi, n0, in_ap, reads):
            S.op("pe", "transpose", (psums[pi][:, n0:n0 + 128], in_ap, ident32[:, :]), {},
                 reads=list(reads) + [CONST], writes=[PS[pi]])

        psums16 = [p[:, :].bitcast(BF16) for p in psums]

        def tr16(pi, n0, in_ap, reads):
            S.op("pe", "transpose", (psums16[pi][:, n0:n0 + 128], in_ap, ident16[:, :]), {},
                 reads=list(reads) + [CONST], writes=[PS[pi]])

        def act(out, in_, func, reads, writes, bias=None, scale=None):
            kw = {}
            if bias is not None:
                kw["bias"] = bias
            if scale is not None:
                kw["scale"] = scale
            S.op("act", "activation", (out, in_, func), kw, reads=reads, writes=writes)

        def tt(eng, out, a, b, op, reads, writes):
            S.op(eng, "tensor_tensor", (out, a, b, op), {}, reads=reads, writes=writes)

        def ts(eng, out, a, s1, s2, op0, op1, reads, writes):
            if op1 is None:
                S.op(eng, "tensor_scalar", (out, a, s1, None, op0), {}, reads=reads, writes=writes)
            else:
                S.op(eng, "tensor_scalar", (out, a, s1, s2, op0, op1), {}, reads=reads, writes=writes)

        def stt(out, a, s, b, op0, op1, reads, writes):
            S.op("dve", "scalar_tensor_tensor", (out, a, s, b, op0, op1), {}, reads=reads, writes=writes)

        def cp(eng, out, in_, reads, writes):
            if eng == "act":
                S.op("act", "copy", (out, in_), {}, reads=reads, writes=writes)
            else:
                S.op(eng, "tensor_copy", (out, in_), {}, reads=reads, writes=writes)

        def dma(stream, out, in_, reads, writes):
            if stream == "pool":
                stream = os.environ.get("KPOOLQ", "pool")
            return S.op(stream, "dma_start", (), dict(out=out, in_=in_), reads=reads, writes=writes, dma=True)

        evrr = [0]

        def ev_eng():
            evrr[0] += 1
            return "act" if evrr[0] % 2 else "dve"

        def dbg_dump(name, ap, shape, dt, reads):
            if name not in dbg or name in dbg_out:
                return
            o = nc.dram_tensor("dbg_" + name, list(shape), dt, kind="ExternalOutput").ap()
            dbg_out[name] = o
            dma("pool", o, ap, reads, [])

        STOP = os.environ.get("KSTOP", "")
        SKIP = os.environ.get("KSKIP", "").split(",")
        c0 = Tl()
        dma("sp", ident32[:, :], ident_in, [], [c0])
        dma("sp", si32[:, :], si_in, [], [c0])
        dma("sp", wmask[:, :, :], wmask_in.rearrange("p (j q) -> p j q", j=3), [], [c0])
        dma("sp", flags[:, :], flags_in, [], [c0])
        dma("sp", se[:, :], sink, [], [c0])
        dma("sp", bgc[:, :], b_gate, [], [c0])
        for n in lnp:
            dma("sp", lncol[n][:, :], lnp[n], [], [c0])
        dma("sp", gqc[:, :], q_norm_g, [], [c0])
        dma("sp", gkvc[:, :], kv_norm_g, [], [c0])
        S.op("dve", "memset", (ones16[:, :], 1.0), {}, writes=[c0])
        S.op("dve", "memset", (ones32[:, :], 1.0), {}, writes=[c0])
        S.op("dve", "memset", (eps_ln[:, :], LN_EPS), {}, writes=[c0])
        S.op("dve", "memset", (eps_rms[:, :], RMS_EPS), {}, writes=[c0])
        cp("dve", si16[:, :], si32[:, :], [c0], [c0])
        cp("dve", ident16[:, :], ident32[:, :], [c0], [c0])
        act(se[:, :], se[:, :], AF.Exp, [c0], [c0])

        AR.reset()
        PIECE = 12288
        relb = AR.take((6,), F32)
        selb = AR.take((PIECE,), F32)
        bflat = AR.take((PIECE,), F32)
        t_rel, t_sel, t_bf, t_bd = Tl(), Tl(), Tl(), Tl()
        dma("sp", relb[0:32, :], rel_bias, [], [t_rel])
        for pc in range(4):
            dma("sp", selb[0:32, :], sel_in[:, pc * PIECE:(pc + 1) * PIECE], [], [t_sel])
            for i in range(PIECE // 512):
                pi = ps_next()
                mm(pi, 6, 512, relb[0:32, 0:6], selb[0:32, i * 512:(i + 1) * 512], True, True, [t_rel, t_sel])
                cp(ev_eng(), bflat[0:6, i * 512:(i + 1) * 512], psums[pi][0:6, :], [PS[pi]], [t_bf])
            dma("sp", biasd[:, pc * PIECE:(pc + 1) * PIECE], bflat[0:6, :], [t_bf], [t_bd])
        for h in range(6):
            dma("sp", biasT[:, h, :, :], biasd[h, :].rearrange("(j k q) -> k j q", j=3, k=128), [t_bd], [c0])
        for h in range(6):
            tt("dve", biasT[:, h, :, :], biasT[:, h, :, :], wmask[:, :, :], ALU.add, [c0], [c0])
        S.barrier()

        AR.reset()
        wckv = AR.take((16, 640), BF16)
        wukv = AR.take((4, 1536), BF16)
        wmem = AR.take((16, 1024), BF16)
        P1W = Tl()
        p1_base = AR.off
        st32 = [AR.take((UNIT,), F32) for _ in range(3)]
        st16 = [AR.take((UNIT,), BF16) for _ in range(3)]
        T32 = [Tl() for _ in range(3)]
        T16 = [Tl() for _ in range(3)]
        cvrr = [0]

        def cast_eng():
            cvrr[0] += 1
            return ("dve", "pool", "act")[cvrr[0] % 3]

        def conv_seg(seg, dst3, dst_t, slot):
            wname, row0, kcs, col0, ncols, special = seg
            s32 = st32[slot][:, 0:kcs * ncols].rearrange("p (k n) -> p k n", k=kcs)
            ncl = 64 if special in ("gq_rope", "rope") else ncols
            src = W[wname][row0:row0 + kcs * 128, col0:col0 + ncl].rearrange("(k p) n -> p k n", p=128)
            dma("sp", s32[:, :, 0:ncl], src, [], [T32[slot]])
            if special is None:
                cp(cast_eng(), dst3, s32, [T32[slot]], [dst_t])
            elif special in ("gq", "gkv"):
                gc = gqc if special == "gq" else gkvc
                for k in range(kcs):
                    ts("dve", dst3[:, k, :], s32[:, k, :], gc[:, k:k + 1], None, ALU.mult, None,
                       [T32[slot], c0], [dst_t])
            elif special in ("gq_rope", "rope"):
                for k in range(kcs):
                    if special == "gq_rope":
                        ts("dve", dst3[:, k, 0:64], s32[:, k, 0:64], gqc[:, k:k + 1], None, ALU.mult, None,
                           [T32[slot], c0], [dst_t])
                        ts("dve", dst3[:, k, 64:96], s32[:, k, 32:64], gqc[:, k:k + 1], -1.0, ALU.mult, ALU.mult,
                           [T32[slot], c0], [dst_t])
                        ts("dve", dst3[:, k, 96:128], s32[:, k, 0:32], gqc[:, k:k + 1], None, ALU.mult, None,
                           [T32[slot], c0], [dst_t])
                    else:
                        cp("dve", dst3[:, k, 0:64], s32[:, k, 0:64], [T32[slot]], [dst_t])
                        ts("dve", dst3[:, k, 64:96], s32[:, k, 32:64], -1.0, None, ALU.mult, None,
                           [T32[slot]], [dst_t])
                        cp("dve", dst3[:, k, 96:128], s32[:, k, 0:32], [T32[slot]], [dst_t])

        slot_rr = [0]

        def nslot():
            slot_rr[0] += 1
            return slot_rr[0] % 3

        for i in range(2):
            conv_seg(("w_in", 0, 16, 1792 + i * 256, 256, None), wckv[:, :, i * 256:(i + 1) * 256], P1W, nslot())
        conv_seg(("w_in", 0, 16, 2304, 128, "rope"), wckv[:, :, 512:640], P1W, nslot())
        for h in range(6):
            conv_seg(("w_ukv", 0, 4, h * 256, 128, "gkv"), wukv[:, :, h * 128:(h + 1) * 128], P1W, nslot())
            conv_seg(("w_ukv", 0, 4, h * 256 + 128, 128, "gkv"), wukv[:, :, 768 + h * 128:768 + (h + 1) * 128],
                     P1W, nslot())
        for i in range(4):
            conv_seg(("w_mem_kv", 0, 16, i * 256, 256, None), wmem[:, :, i * 256:(i + 1) * 256], P1W, nslot())
        for u, segs in enumerate(units):
            b = u % 3
            off = 0
            for seg in segs:
                kcs, ncols = seg[2], seg[4]
                dst3 = st16[b][:, off:off + kcs * ncols].rearrange("p (k n) -> p k n", k=kcs)
                conv_seg(seg, dst3, T16[b], nslot())
                off += kcs * ncols
            dma("act", wst[u, :, 0:off], st16[b][:, 0:off], [T16[b]], [])
        S.barrier()

        if STOP == "p0":
            SEQ = {"p": 0, "s": 0}
            OWN = {"p": 0, "s": 0}
            memin = {}
        AR.reset(p1_base)
        xrow = AR.take((4, D), F32)
        XR = [Tl() for _ in range(4)]
        hi16 = AR.take((4, D), BF16)
        HI = [Tl() for _ in range(4)]
        xT16 = AR.take((DC, T), BF16)
        XT = [Tl() for _ in range(DC)]
        ckvT = AR.take((4, T), BF16)
        CK = [Tl() for _ in range(4)]
        sq16 = AR.take((4, T), BF16)
        SQ = [Tl() for _ in range(4)]
        krc = AR.take((T,), BF16)
        KRC = Tl()
        krT = [AR.take((T,), BF16) for _ in range(2)]
        KRT = [Tl() for _ in range(2)]
        knst = [AR.take((B_HEADS, T), BF16) for _ in range(2)]
        KNS = [Tl() for _ in range(2)]
        vst = [AR.take((B_HEADS, 4, HD), BF16) for _ in range(2)]
        VS = [Tl() for _ in range(2)]
        cs2 = [AR.take((T,), F32) for _ in range(2)]
        CS = [Tl() for _ in range(2)]
        rstd_b = AR.take((T,), F32)
        RB = Tl()
        rstd_c = AR.take((4,), F32)
        RC = Tl()

        def load_x(stream, src_rows_ap, nblk, b0=0):
            for b in range(nblk):
                dma(stream, xrow[:, b0 + b, :], src_rows_ap[b * 128:(b + 1) * 128, :], [], [XR[b0 + b]])

        def transpose_x(nblk, with_lo=False):
            for b in range(nblk):
                cp(("pool", "dve")[b % 2], hi16[:, b, :], xrow[:, b, :], [XR[b]], [HI[b]])
                if with_lo:
                    tt("dve", lo16[:, b, :], xrow[:, b, :], hi16[:, b, :], ALU.subtract, [XR[b], HI[b]], [LO[b]])
            n = nblk * 128
            for dc in range(DC):
                pi = ps_next()
                for b in range(nblk):
                    tr16(pi, b * 128, hi16[:, b, dc * 128:(dc + 1) * 128], [HI[b]])
                if with_lo:
                    for b in range(nblk):
                        tr16(pi, 512 + b * 128, lo16[:, b, dc * 128:(dc + 1) * 128], [LO[b]])
                cp("act" if with_lo else ev_eng(), xT16[:, dc, 0:n], psums16[pi][:, 0:n], [PS[pi]], [XT[dc]])
                if with_lo:
                    tq = dc % 2
                    act(hT32[:, dc, :], psums16[pi][:, 0:512], AF.Copy, [PS[pi]], [HT[dc]], scale=ALPHA)
                    act(sm32[tq][:, :], psums16[pi][:, 512:1024], AF.Copy, [PS[pi]], [SM[tq]], scale=ALPHA)
                    tt("dve", hT32[:, dc, :], hT32[:, dc, :], sm32[tq][:, :], ALU.add, [HT[dc], SM[tq]], [HT[dc]])

        def rstd_from(pi, n, out_ap, out_t, d_in, eps_t):
            act(out_ap, psums[pi][:, 0:n], AF.Sqrt, [PS[pi], c0], [out_t], bias=eps_t[:, 0:1], scale=1.0 / d_in)
            S.op("dve", "reciprocal", (out_ap, out_ap), {}, reads=[out_t], writes=[out_t])

        for k in ([] if "mem" in SKIP else [kk for kk in memin if ("mem" + kk) not in SKIP]):
            load_x("act", memin[k], 2)
            transpose_x(2)
            for h in ([] if "memkc" in SKIP else range(C_HEADS)):
                pi = ps_next()
                for kc in range(DC):
                    mm(pi, 128, N_MEM, wmem[:, kc, h * 128:(h + 1) * 128], xT16[:, kc, 0:N_MEM], kc == 0, kc == DC - 1,
                       [P1W, XT[kc]])
                cp(ev_eng(), kcT[k][:, h, :], psums[pi][:, 0:N_MEM], [PS[pi]], [KC_T[k]])
            for mb in ([] if "memvc" in SKIP else range(2)):
                pi = ps_next()
                for kc in range(DC):
                    mm(pi, 128, 512, xT16[:, kc, mb * 128:(mb + 1) * 128], wmem[:, kc, 512:1024], kc == 0, kc == DC - 1,
                       [P1W, XT[kc]])
                cp(ev_eng(), vc[k][:, mb, :], psums[pi][:, :], [PS[pi]], [VC_T[k]])

        NOBAR = os.environ.get("KNOBAR", "p1tile").split(",")

        def SB(name):
            if name not in NOBAR and "all" not in NOBAR:
                S.barrier()

        SB("mem")
        tcount = 0
        p1tiles = [(k, t) for k in "ps" for t in range(SEQ[k] // T)]
        if STOP == "p1a":
            p1tiles = []
            tiles_override = True
        if STOP.startswith("p1n"):
            p1tiles = p1tiles[:int(STOP[3:])]
        XQ = os.environ.get("KXQ", "sp")
        if p1tiles:
            load_x(XQ, xin[p1tiles[0][0]][0:T, :], 4)
        for p1i, (k, t) in enumerate(p1tiles):
            if True:
                b2 = tcount % 2
                tcount += 1
                if "cs" not in SKIP:
                    dma(XQ, cs2[b2][:, :], csin[k][:, t * T:(t + 1) * T], [], [CS[b2]])
                transpose_x(4)
                if p1i + 1 < len(p1tiles):
                    k2, t2 = p1tiles[p1i + 1]
                    load_x(XQ, xin[k2][t2 * T:(t2 + 1) * T, :], 4)
                for m in range(4):
                    pi = ps_next()
                    for kc in range(DC):
                        mm(pi, 128, T, wckv[:, kc, m * 128:(m + 1) * 128], xT16[:, kc, :], kc == 0, kc == DC - 1,
                           [P1W, XT[kc]])
                    cp("dve", ckvT[:, m, :], psums[pi][:, :], [PS[pi]], [CK[m]])
                    if "sq" not in SKIP:
                        act(sq16[:, m, :], psums[pi][:, :], AF.Square, [PS[pi], CK[m]], [SQ[m]])
                if 'rope' not in SKIP:
                    pi = ps_next()
                    for kc in range(DC):
                        mm(pi, 128, T, wckv[:, kc, 512:640], xT16[:, kc, :], kc == 0, kc == DC - 1, [P1W, XT[kc]])
                    tt("dve", krc[:, :], cs2[b2][:, :], psums[pi][:, :], ALU.mult, [PS[pi], CS[b2]], [KRC])
                    pi = ps_next()
                    mm(pi, 64, T, si16[:, :], krc[:, :], True, True, [KRC, c0])
                    cp("act", krT[b2][0:64, :], psums[pi][0:64, :], [PS[pi]], [KRT[b2]])
                    if "kr" not in SKIP:
                        dma("pool", KrT[k][:, t * T:(t + 1) * T], krT[b2][0:64, :], [KRT[b2]], [])
                if 'stats' not in SKIP:
                    pi = ps_next()
                    for m in range(4):
                        mm(pi, 128, T, ones16[:, :], sq16[:, m, :], m == 0, m == 3, [SQ[m], c0])
                    rstd_from(pi, T, rstd_b[:, :], RB, 512.0, eps_rms)
                    pi = ps_next()
                    for b in range(4):
                        for m in range(4):
                            mm(pi, 128, 1, sq16[:, m, b * 128:(b + 1) * 128], ones16[:, 0:1], m == 0, m == 3, [SQ[m], c0],
                               n0=b)
                    rstd_from(pi, 4, rstd_c[:, :], RC, 512.0, eps_rms)
                if 'kn_c' not in SKIP:
                    for h in range(B_HEADS):
                        pi = ps_next()
                        for kc in range(4):
                            mm(pi, 128, T, wukv[:, kc, h * 128:(h + 1) * 128], ckvT[:, kc, :], kc == 0, kc == 3,
                               [P1W, CK[kc]])
                        tt("dve", knst[b2][:, h, :], rstd_b[:, :], psums[pi][:, :], ALU.mult, [PS[pi], RB], [KNS[b2]])
                    if "kn" not in SKIP:
                        dma("pool", KnT[k][:, :, t * T:(t + 1) * T].rearrange("h p n -> p h n"), knst[b2][:, :, :],
                            [KNS[b2]], [])
                if 'vv' not in SKIP:
                    for b in range(4):
                        pa, pb = ps_next(), ps_next()
                        for kc in range(4):
                            mm(pa, 128, 512, ckvT[:, kc, b * 128:(b + 1) * 128], wukv[:, kc, 768:1280], kc == 0, kc == 3,
                               [P1W, CK[kc]])
                        for kc in range(4):
                            mm(pb, 128, 256, ckvT[:, kc, b * 128:(b + 1) * 128], wukv[:, kc, 1280:1536], kc == 0, kc == 3,
                               [P1W, CK[kc]])
                        act(vst[b2][:, 0:4, b, :], psums[pa][:, :].rearrange("p (h d) -> p h d", h=4), AF.Copy,
                            [PS[pa], RC], [VS[b2]], scale=rstd_c[:, b:b + 1])
                        act(vst[b2][:, 4:6, b, :], psums[pb][:, 0:256].rearrange("p (h d) -> p h d", h=2), AF.Copy,
                            [PS[pb], RC], [VS[b2]], scale=rstd_c[:, b:b + 1])
                    if "v" not in SKIP:
                        dma("pool", Vsc[k][:, :, t * 4:(t + 1) * 4, :].rearrange("h p b d -> p h b d"), vst[b2][:, :, :, :],
                            [VS[b2]], [])
                SB("p1tile")
        S.barrier()

        AR.reset()
        hT32 = AR.take((DC, T), F32)
        HT = [Tl() for _ in range(DC)]
        xT16 = AR.take((DC, T), BF16)
        XT = [Tl() for _ in range(DC)]
        wbuf = [AR.take((UNIT,), BF16) for _ in range(NWBUF)]
        WB = [Tl() for _ in range(NWBUF)]
        base_b = AR.off
        xrow = AR.take((4, D), F32)
        XR = [Tl() for _ in range(4)]
        qaT = AR.take((A_HEADS, T), BF16)
        QA = [Tl() for _ in range(A_HEADS)]
        kaT = AR.take((A_KV, 6 * 128), BF16)
        KA = [Tl() for _ in range(A_KV)]
        va = AR.take((6, A_KV * HD), BF16)
        VA = Tl()
        cqT = AR.take((4, T), BF16)
        CQ = [Tl() for _ in range(4)]
        hilo_base = AR.off
        qnT = AR.take((B_HEADS, T), BF16)
        QN = [Tl() for _ in range(B_HEADS)]
        qrT = AR.take((B_HEADS, T), BF16)
        QR = [Tl() for _ in range(B_HEADS)]
        qcT = AR.take((C_HEADS, T), BF16)
        QC = [Tl() for _ in range(C_HEADS)]
        aoT = AR.take((A_HEADS, T), BF16)
        AO = [Tl() for _ in range(A_HEADS)]
        boT = AR.take((B_HEADS, T), BF16)
        BO = [Tl() for _ in range(B_HEADS)]
        coT = AR.take((C_HEADS, T), BF16)
        CO = [Tl() for _ in range(C_HEADS)]
        hilo_end = AR.off
        AR.reset(hilo_base)
        hi16 = AR.take((4, D), BF16)
        lo16 = AR.take((4, D), BF16)
        assert AR.off <= hilo_end, "hi/lo overlay too big"
        AR.reset(hilo_end)
        HI = [Tl() for _ in range(4)]
        LO = [Tl() for _ in range(4)]
        mg_off = AR.off
        mgT = AR.take((DC, T), BF16)
        MG = [Tl() for _ in range(DC)]
        for q_ in range(2):
            wbuf.append(arena[:, mg_off + q_ * (UNIT // 2):mg_off + (q_ + 1) * (UNIT // 2)].bitcast(BF16))
            WB.append(Tl())
        sm32 = [AR.take((T,), F32) for _ in range(6)]
        SM = [Tl() for _ in range(6)]
        end_attn = AR.off
        AR.reset(base_b)
        KCH = min(2048, P_SEQ, S_SEQ)
        knb = [AR.take((KCH,), BF16) for _ in range(2)]
        krb = [AR.take((KCH,), BF16) for _ in range(2)]
        vb = [AR.take((KCH // 128, HD), BF16) for _ in range(2)]
        KVB = [Tl() for _ in range(2)]
        pT = [AR.take((T,), BF16) for _ in range(3)]
        PT = [Tl() for _ in range(3)]
        acc = [AR.take((T,), F32) for _ in range(2)]
        ACC = [Tl() for _ in range(2)]
        assert AR.off <= base_b + 4 * D, "kv stream overflows xrow region"
        AR.reset(base_b)
        actT = AR.take((FC, T), BF16)
        ystage = [AR.take((D,), F32) for _ in range(2)]
        ylo = AR.take((DC, T), BF16)
        YL = [Tl() for _ in range(DC)]
        fsc = [AR.take((T,), F32) for _ in range(2)]
        FSC = [Tl() for _ in range(2)]
        assert AR.off <= end_attn, ("ffn region", AR.off, end_attn)
        ATT_ALL = XR + QA + KA + [VA] + CQ + QN + QR + QC + AO + BO + CO + MG + SM + KVB + PT + ACC + HI + LO
        ACTT = [Tl() for _ in range(FC)]
        YS = [Tl() for _ in range(2)]
        FFN_ALL = ACTT + YS + YL + FSC

        def fence_region(old, new):
            ws = []
            rs = {}
            rds = []
            for t in old:
                if t.w is not None:
                    ws.append(t.w)
                for e, o in t.r.items():
                    rs[(e, id(o))] = o
                rds.extend(t.rd)
            for t in new:
                t.w = None
                t.r = {}
                t.rd = list(rds) + ws + list(rs.values())

        wctr = [0]
        wload = [0]
        total_units = [0]

        N_EARLY = 51
        occ = {}

        def buf_of(i):
            u = i % NU
            return u % 3 if u < N_EARLY else (u - N_EARLY) % 5

        def w_prefetch():
            i_cur = wctr[0]
            while wload[0] < total_units[0]:
                j = wload[0]
                b = buf_of(j)
                prev = occ.get(b)
                if prev is not None and prev >= i_cur:
                    break
                u = j % NU
                n = sum(sg[2] * sg[4] for sg in units[u])
                dma("sp", wbuf[b][:, 0:n], wst[u, :, 0:n], [], [WB[b]])
                occ[b] = j
                wload[0] += 1

        def w_next():
            w_prefetch()
            i = wctr[0]
            assert wload[0] > i
            wctr[0] += 1
            b = buf_of(i)
            u = i % NU
            views = []
            off = 0
            for seg in units[u]:
                kcs, ncols = seg[2], seg[4]
                views.append(wbuf[b][:, off:off + kcs * ncols].rearrange("p (k n) -> p k n", k=kcs))
                off += kcs * ncols
            return WB[b], views

        tiles = [(k, t) for k in "ps" for t in range(OWN[k] // T)]
        if STOP.startswith("p1"):
            tiles = []
        total_units[0] = NU * len(tiles)

        def layer_norm(gname, bname, to_bf16):
            mean_b, msq, var_b, nmr = sm32[0], sm32[1], sm32[2], sm32[3]
            ts("dve", mean_b[:, :], psums[5][:, :], 1.0 / D, None, ALU.mult, None, [PS[5]], [SM[0]])
            tt("dve", msq[:, :], mean_b[:, :], mean_b[:, :], ALU.mult, [SM[0]], [SM[1]])
            stt(var_b[:, :], psums[6][:, :], 1.0 / D, msq[:, :], ALU.mult, ALU.subtract, [PS[6], SM[1]], [SM[2]])
            act(var_b[:, :], var_b[:, :], AF.Sqrt, [SM[2], c0], [SM[2]], bias=eps_ln[:, 0:1], scale=1.0)
            S.op("dve", "reciprocal", (var_b[:, :], var_b[:, :]), {}, reads=[SM[2]], writes=[SM[2]])
            stt(nmr[:, :], mean_b[:, :], -1.0, var_b[:, :], ALU.mult, ALU.mult, [SM[0], SM[2]], [SM[3]])
            for c in range(DC):
                tt("dve", hT32[:, c, :], hT32[:, c, :], var_b[:, :], ALU.mult, [HT[c], SM[2]], [HT[c]])
                tt("dve", hT32[:, c, :], hT32[:, c, :], nmr[:, :], ALU.add, [HT[c], SM[3]], [HT[c]])
                act(hT32[:, c, :], hT32[:, c, :], AF.Identity, [HT[c], c0], [HT[c]],
                    bias=lncol[bname][:, c:c + 1], scale=lncol[gname][:, c:c + 1])
                if to_bf16:
                    cp("pool", xT16[:, c, :], hT32[:, c, :], [HT[c]], [XT[c]])

        def stats_accum(c):
            u16, s16 = pT[0], pT[1]
            cp("act", u16[:, :], hT32[:, c, :], [HT[c]], [PT[0]])
            act(s16[:, :], hT32[:, c, :], AF.Square, [HT[c]], [PT[1]])
            mm(5, 128, T, ones16[:, :], u16[:, :], c == 0, c == DC - 1, [PT[0], c0])
            mm(6, 128, T, ones16[:, :], s16[:, :], c == 0, c == DC - 1, [PT[1], c0])

        for ti, (k, t) in enumerate(tiles):
            Sq = SEQ[k]
            nt_own = OWN[k] // T
            r0 = t * T
            fence_region(FFN_ALL + [WB[3], WB[4]], ATT_ALL)
            prev0 = (r0 - 128) % Sq
            next0 = (r0 + T) % Sq
            dma("act", xrow[:, 0, :], xin[k][prev0:prev0 + 128, :], [], [XR[0]])
            dma("act", xrow[:, 1, :], xin[k][next0:next0 + 128, :], [], [XR[1]])
            xh = mgT[:, :, 0:256]
            for b in range(2):
                cp(("pool", "dve")[b % 2], hi16[:, b, :], xrow[:, b, :], [XR[b]], [HI[b]])
            for dc in range(DC):
                pi = ps_next()
                for b in range(2):
                    tr16(pi, b * 128, hi16[:, b, dc * 128:(dc + 1) * 128], [HI[b]])
                cp(ev_eng(), xh[:, dc, :], psums16[pi][:, 0:256], [PS[pi]], [MG[dc]])
            load_x("act", xin[k][r0:r0 + T, :], 4)
            dma("act", sm32[4][:, :], csin[k][:, r0:r0 + T], [], [SM[4]])

            transpose_x(4, True)
            fence_region(XR, KVB + PT + ACC)
            fence_region(HI + LO, QN + QR + QC + AO + BO + CO)
            for b_ in range(2):
                S.op("dve", "memset", (krb[b_][64:128, :], 0.0), {}, writes=[KVB[b_]])
            for h_ in range(B_HEADS):
                S.op("dve", "memset", (qrT[64:128, h_, :], 0.0), {}, writes=[QR[h_]])
            for i in range(3):
                wt, (wv,) = w_next()
                for j in range(2):
                    m = 2 * i + j
                    pi = ps_next()
                    for kc in range(DC):
                        mm(pi, 128, T, wv[:, kc, j * 128:(j + 1) * 128], xT16[:, kc, :], kc == 0, kc == DC - 1,
                           [wt, XT[kc]])
                    cp(ev_eng(), qaT[:, m, :], psums[pi][:, :], [PS[pi]], [QA[m]])
            wt, (wv,) = w_next()
            for g in range(A_KV):
                pi = ps_next()
                for kc in range(DC):
                    mm(pi, 128, T, wv[:, kc, g * 128:(g + 1) * 128], xT16[:, kc, :], kc == 0, kc == DC - 1, [wt, XT[kc]])
                cp(ev_eng(), kaT[:, g, 128:640], psums[pi][:, :], [PS[pi]], [KA[g]])
                pi = ps_next()
                for kc in range(DC):
                    mm(pi, 128, 256, wv[:, kc, g * 128:(g + 1) * 128], xh[:, kc, :], kc == 0, kc == DC - 1, [wt, MG[kc]])
                cp(ev_eng(), kaT[:, g, 0:128], psums[pi][:, 0:128], [PS[pi]], [KA[g]])
                cp(ev_eng(), kaT[:, g, 640:768], psums[pi][:, 128:256], [PS[pi]], [KA[g]])
            wt, (wv,) = w_next()
            for b in range(6):
                pi = ps_next()
                for kc in range(DC):
                    if b == 0:
                        l = xh[:, kc, 0:128]
                        rd = MG[kc]
                    elif b == 5:
                        l = xh[:, kc, 128:256]
                        rd = MG[kc]
                    else:
                        l = xT16[:, kc, (b - 1) * 128:b * 128]
                        rd = XT[kc]
                    mm(pi, 128, 256, l, wv[:, kc, :], kc == 0, kc == DC - 1, [wt, rd])
                cp(ev_eng(), va[:, b, :], psums[pi][:, 0:256], [PS[pi]], [VA])
            for i in range(2):
                wt, (wv,) = w_next()
                for j in range(2):
                    m = 2 * i + j
                    pi = ps_next()
                    for kc in range(DC):
                        mm(pi, 128, T, wv[:, kc, j * 128:(j + 1) * 128], xT16[:, kc, :], kc == 0, kc == DC - 1,
                           [wt, XT[kc]])
                    cp("dve", cqT[:, m, :], psums[pi][:, :], [PS[pi]], [CQ[m]])
                    act(pT[m % 3][:, :], psums[pi][:, :], AF.Square, [PS[pi], CQ[m]], [PT[m % 3]])
                    mm(7, 128, T, ones16[:, :], pT[m % 3][:, :], m == 0, m == 3, [PT[m % 3], c0])
            rq = sm32[5]
            rstd_from(7, T, rq[:, :], SM[5], 512.0, eps_rms)
            csr = sm32[4]
            tt("dve", csr[:, :], csr[:, :], rq[:, :], ALU.mult, [SM[4], SM[5]], [SM[4]])
            for i in range(2):
                wt, (wv,) = w_next()
                for j in range(2):
                    m = 2 * i + j
                    pi = ps_next()
                    for kc in range(DC):
                        mm(pi, 128, T, wv[:, kc, j * 128:(j + 1) * 128], xT16[:, kc, :], kc == 0, kc == DC - 1,
                           [wt, XT[kc]])
                    cp(ev_eng(), qcT[:, m, :], psums[pi][:, :], [PS[pi]], [QC[m]])
            for half in range(2):
                wt, wvs = w_next()
                for hh in range(3):
                    h = 3 * half + hh
                    pi = ps_next()
                    for kc in range(4):
                        mm(pi, 128, T, wvs[2 * hh][:, kc, :], cqT[:, kc, :], kc == 0, kc == 3, [wt, CQ[kc]])
                    tt("dve", qnT[:, h, :], rq[:, :], psums[pi][:, :], ALU.mult, [PS[pi], SM[5]], [QN[h]])
                    pi = ps_next()
                    for kc in range(4):
                        mm(pi, 128, T, wvs[2 * hh + 1][:, kc, :], cqT[:, kc, :], kc == 0, kc == 3, [wt, CQ[kc]])
                    tmp = pT[hh]
                    tt("dve", tmp[:, :], csr[:, :], psums[pi][:, :], ALU.mult, [PS[pi], SM[4]], [PT[hh]])
                    pi = ps_next()
                    mm(pi, 64, T, si16[:, :], tmp[:, :], True, True, [PT[hh], c0])
                    cp("act", qrT[0:64, h, :], psums[pi][0:64, :], [PS[pi]], [QR[h]])
            dbg_dump("qnT", qnT[:, :, :], (128, B_HEADS, T), BF16, QN)
            dbg_dump("qrT", qrT[0:64, :, :], (64, B_HEADS, T), BF16, QR)

            sc_a = HD ** -0.5
            for n in range(4):
                for g in range(A_KV):
                    pO, pL = 5, 6
                    for j in range(3):
                        pi = ps_next()
                        kb = n + j
                        mm(pi, 128, 384, kaT[:, g, kb * 128:(kb + 1) * 128], qaT[:, 3 * g:3 * g + 3, n * 128:(n + 1) * 128],
                           True, True, [KA[g]] + QA[3 * g:3 * g + 3])
                        s32 = sm32[j % 2]
                        stt(s32[:, 0:384].rearrange("p (h q) -> p h q", h=3), psums[pi][:, 0:384].rearrange("p (h q) -> p h q", h=3),
                            sc_a, biasT[:, 3 * g:3 * g + 3, j, :], ALU.mult, ALU.add, [PS[pi], c0], [SM[j % 2]])
                        p16 = pT[j]
                        act(p16[:, 0:384], s32[:, 0:384], AF.Exp, [SM[j % 2]], [PT[j]])
                        first_edge = (t == 0 and n == 0 and j == 0)
                        last_edge = (t == nt_own - 1 and n == 3 and j == 2)
                        if first_edge or last_edge:
                            fc = (0 if first_edge else 1) + (0 if k == "p" else 2)
                            ts("dve", p16[:, 0:384], p16[:, 0:384], flags[:, fc:fc + 1], None, ALU.mult, None,
                               [PT[j], c0], [PT[j]])
                        mm(pO, 128, 384, va[:, kb, g * 128:(g + 1) * 128], p16[:, 0:384], j == 0, j == 2, [VA, PT[j]])
                        mm(pL, 128, 384, ones16[:, :], p16[:, 0:384], j == 0, j == 2, [PT[j], c0])
                    l32 = sm32[2]
                    for r in range(3):
                        h = 3 * g + r
                        ts("dve", l32[:, r * 128:(r + 1) * 128], psums[pL][:, r * 128:(r + 1) * 128], se[:, h:h + 1], None,
                           ALU.add, None, [PS[pL], c0], [SM[2]])
                    S.op("dve", "reciprocal", (l32[:, 0:384], l32[:, 0:384]), {}, reads=[SM[2]], writes=[SM[2]])
                    tt("dve", aoT[:, 3 * g:3 * g + 3, n * 128:(n + 1) * 128], l32[:, 0:384].rearrange("p (h q) -> p h q", h=3),
                       psums[pO][:, 0:384].rearrange("p (h q) -> p h q", h=3), ALU.mult, [PS[pO], SM[2]], AO[3 * g:3 * g + 3])
            dbg_dump("aoT", aoT[:, :, :], (128, A_HEADS, T), BF16, AO)

            for h in range(C_HEADS):
                pO, pL = 5, 6
                for mb in range(2):
                    pi = ps_next()
                    mm(pi, 128, T, kcT[k][:, h, mb * 128:(mb + 1) * 128], qcT[:, h, :], True, True, [KC_T[k], QC[h]])
                    act(pT[mb][:, :], psums[pi][:, :], AF.Exp, [PS[pi]], [PT[mb]], scale=sc_a)
                    mm(pO, 128, T, vc[k][:, mb, h * 128:(h + 1) * 128], pT[mb][:, :], mb == 0, mb == 1, [VC_T[k], PT[mb]])
                    mm(pL, 128, T, ones16[:, :], pT[mb][:, :], mb == 0, mb == 1, [PT[mb], c0])
                S.op("dve", "reciprocal", (sm32[2][:, :], psums[pL][:, :]), {}, reads=[PS[pL]], writes=[SM[2]])
                tt("dve", coT[:, h, :], sm32[2][:, :], psums[pO][:, :], ALU.mult, [PS[pO], SM[2]], [CO[h]])
            dbg_dump("coT", coT[:, :, :], (128, C_HEADS, T), BF16, CO)

            sc_b = (128 + 64) ** -0.5
            nch = Sq // KCH
            kvi = 0
            items = [(h_, c_) for h_ in range(B_HEADS) for c_ in range(nch)]

            def kv_load(idx):
                h_, c_ = items[idx]
                b_ = idx % 2
                dma("act", knb[b_][:, :], KnT[k][h_, :, c_ * KCH:(c_ + 1) * KCH], [], [KVB[b_]])
                dma("act", krb[b_][0:64, :], KrT[k][:, c_ * KCH:(c_ + 1) * KCH], [], [KVB[b_]])
                dma("act", vb[b_][:, :, :], Vsc[k][h_, :, c_ * (KCH // 128):(c_ + 1) * (KCH // 128), :], [], [KVB[b_]])

            LOOK = 2
            ntile_ch = KCH // 128
            nkt = Sq // 128
            allk = [(h_, c_, kt_) for h_ in range(B_HEADS) for c_ in range(nch) for kt_ in range(ntile_ch)]
            kv_load(0)
            if len(items) > 1:
                kv_load(1)
            pend = {}

            def emit_S(i):
                h_, c_, kt_ = allk[i]
                bb_ = (h_ * nch + c_) % 2
                pi_ = ps_next()
                mm(pi_, 128, T, knb[bb_][:, kt_ * 128:(kt_ + 1) * 128], qnT[:, h_, :], True, False, [KVB[bb_], QN[h_]])
                mm(pi_, 128, T, krb[bb_][:, kt_ * 128:(kt_ + 1) * 128], qrT[:, h_, :], False, True, [KVB[bb_], QR[h_]])
                pend[i] = pi_

            for i in range(min(LOOK, len(allk))):
                emit_S(i)
            pO, pL = 5, 6
            for gi, (h, ch, kt) in enumerate(allk):
                if gi + LOOK < len(allk):
                    emit_S(gi + LOOK)
                pi = pend.pop(gi)
                g_ch = h * nch + ch
                bb = g_ch % 2
                it = ch * ntile_ch + kt
                pp = gi % 3
                act(pT[pp][:, :], psums[pi][:, :], AF.Exp, [PS[pi]], [PT[pp]], scale=sc_b)
                a = it % 2
                if it < 2:
                    cp("dve", acc[a][:, :], pT[pp][:, :], [PT[pp]], [ACC[a]])
                else:
                    tt("dve", acc[a][:, :], acc[a][:, :], pT[pp][:, :], ALU.add, [ACC[a], PT[pp]], [ACC[a]])
                mm(pO, 128, T, vb[bb][:, kt, :], pT[pp][:, :], it == 0, it == nkt - 1, [KVB[bb], PT[pp]])
                if kt == ntile_ch - 1 and g_ch + 2 < len(items):
                    kv_load(g_ch + 2)
                if it == nkt - 1:
                    tt("dve", acc[0][:, :], acc[0][:, :], acc[1][:, :], ALU.add, [ACC[0], ACC[1]], [ACC[0]])
                    cp("dve", sm32[0][:, :].bitcast(BF16)[:, 0:T], acc[0][:, :], [ACC[0]], [SM[0]])
                    tt("dve", sm32[1][:, :].bitcast(BF16)[:, 0:T], acc[0][:, :], sm32[0][:, :].bitcast(BF16)[:, 0:T],
                       ALU.subtract, [ACC[0], SM[0]], [SM[1]])
                    mm(pL, 128, T, ones16[:, :], sm32[0][:, :].bitcast(BF16)[:, 0:T], True, False, [SM[0], c0])
                    mm(pL, 128, T, ones16[:, :], sm32[1][:, :].bitcast(BF16)[:, 0:T], False, True, [SM[1], c0])
                    S.op("dve", "reciprocal", (sm32[2][:, :], psums[pL][:, :]), {}, reads=[PS[pL]], writes=[SM[2]])
                    tt("dve", boT[:, h, :], sm32[2][:, :], psums[pO][:, :], ALU.mult, [PS[pO], SM[2]], [BO[h]])
            dbg_dump("boT", boT[:, :, :], (128, B_HEADS, T), BF16, BO)

            for c in range(DC):
                gts = []

                def gate(bi, wt, wg):
                    pi = ps_next()
                    for kc in range(DC):
                        mm(pi, 128, T, wg[:, kc, :], xT16[:, kc, :], kc == 0, kc == DC - 1, [wt, XT[kc]])
                    gt = sm32[bi]
                    act(gt[:, :], psums[pi][:, :], AF.Sigmoid, [PS[pi], c0], [SM[bi]],
                        bias=bgc[:, bi * 16 + c:bi * 16 + c + 1], scale=1.0)
                    gts.append(gt)

                wt1, (wg0, wg1) = w_next()
                gate(0, wt1, wg0)
                gate(1, wt1, wg1)
                wt2, (wg2, wa, wb_, wc) = w_next()
                gate(2, wt2, wg2)
                for bi, (wv, src, srct, nk) in enumerate(((wa, aoT, AO, 6), (wb_, boT, BO, 6), (wc, coT, CO, 4))):
                    pi = ps_next()
                    for kc in range(nk):
                        mm(pi, 128, T, wv[:, kc, :], src[:, kc, :], kc == 0, kc == nk - 1, [wt2, srct[kc]])
                    tt("dve", gts[bi][:, :], gts[bi][:, :], psums[pi][:, :], ALU.mult, [SM[bi], PS[pi]], [SM[bi]])
                tt("dve", gts[0][:, :], gts[0][:, :], gts[1][:, :], ALU.add, [SM[0], SM[1]], [SM[0]])
                tt("dve", mgT[:, c, :], gts[0][:, :], gts[2][:, :], ALU.add, [SM[0], SM[2]], [MG[c]])
            for i in range(8):
                wt, (wv,) = w_next()
                for j in range(2):
                    m = 2 * i + j
                    pi = ps_next()
                    for kc in range(DC):
                        mm(pi, 128, T, wv[:, kc, j * 128:(j + 1) * 128], mgT[:, kc, :], kc == 0, kc == DC - 1, [wt, MG[kc]])
                    tt("dve", hT32[:, m, :], hT32[:, m, :], psums[pi][:, :], ALU.add, [HT[m], PS[pi]], [HT[m]])
                    stats_accum(m)
            dbg_dump("pre1", hT32[:, :, :], (128, DC, T), F32, HT)
            fence_region(MG, [WB[3], WB[4]])
            layer_norm("ln1_g", "ln1_b", True)
            dbg_dump("mean1", sm32[0][:, :], (128, T), F32, [SM[0]])
            dbg_dump("rstd1", sm32[2][:, :], (128, T), F32, [SM[2]])
            dbg_dump("h1", hT32[:, :, :], (128, DC, T), F32, HT)
            fence_region(ATT_ALL, FFN_ALL)
            for j in range(FC):
                wt, (wgt, wup) = w_next()
                pg, pu = ps_next(), ps_next()
                for kc in range(DC):
                    mm(pg, 128, T, wgt[:, kc, :], xT16[:, kc, :], kc == 0, kc == DC - 1, [wt, XT[kc]])
                for kc in range(DC):
                    mm(pu, 128, T, wup[:, kc, :], xT16[:, kc, :], kc == 0, kc == DC - 1, [wt, XT[kc]])
                sg = sm32[3 + (j % 2)] if False else None
                fq = j % 2
                act(fsc[fq][:, :], psums[pg][:, :], AF.Silu, [PS[pg]], [FSC[fq]])
                tt("dve", actT[:, j, :], fsc[fq][:, :], psums[pu][:, :], ALU.mult, [FSC[fq], PS[pu]], [ACTT[j]])
            for m in range(DC):
                pi = ps_next()
                for half in range(2):
                    wt, (wv,) = w_next()
                    for kc in range(22):
                        jj = half * 22 + kc
                        mm(pi, 128, T, wv[:, kc, :], actT[:, jj, :], jj == 0, jj == FC - 1, [wt, ACTT[jj]])
                stt(hT32[:, m, :], hT32[:, m, :], ALPHA, psums[pi][:, :], ALU.mult, ALU.add, [HT[m], PS[pi]], [HT[m]])
                u16 = ystage[0].bitcast(BF16)[:, 0:T]
                s16 = ystage[0].bitcast(BF16)[:, T:2 * T]
                cp("act", u16, hT32[:, m, :], [HT[m]], [YS[0]])
                mm(5, 128, T, ones16[:, :], u16, m == 0, m == DC - 1, [YS[0], c0])
                act(s16, hT32[:, m, :], AF.Square, [HT[m]], [YS[0]])
                mm(6, 128, T, ones16[:, :], s16, m == 0, m == DC - 1, [YS[0], c0])
            sm_save = (sm32[0], sm32[1], sm32[2], sm32[3])
            SM_save = (SM[0], SM[1], SM[2], SM[3])
            ys1 = ystage[1]
            for q in range(4):
                sm32[q] = ys1[:, q * T:(q + 1) * T]
                SM[q] = YS[1]
            layer_norm("ln2_g", "ln2_b", False)
            for q in range(4):
                sm32[q] = sm_save[q]
                SM[q] = SM_save[q]
            for c in range(DC):
                cp(("pool", "act")[c % 2], xT16[:, c, :], hT32[:, c, :], [HT[c]], [XT[c]])
                tt("dve", ylo[:, c, :], hT32[:, c, :], xT16[:, c, :], ALU.subtract, [HT[c], XT[c]], [YL[c]])
            for b in range(4):
                yb = b % 2
                for c4 in range(4):
                    pi = ps_next()
                    for cc in range(4):
                        c = c4 * 4 + cc
                        tr16(pi, cc * 128, xT16[:, c, b * 128:(b + 1) * 128], [XT[c]])
                    for cc in range(4):
                        c = c4 * 4 + cc
                        tr16(pi, 512 + cc * 128, ylo[:, c, b * 128:(b + 1) * 128], [YL[c]])
                    fq = c4 % 2
                    cp("act", ystage[yb][:, c4 * 512:(c4 + 1) * 512], psums16[pi][:, 0:512], [PS[pi]], [YS[yb]])
                    cp("act", fsc[fq][:, :], psums16[pi][:, 512:1024], [PS[pi]], [FSC[fq]])
                    tt("dve", ystage[yb][:, c4 * 512:(c4 + 1) * 512], ystage[yb][:, c4 * 512:(c4 + 1) * 512],
                       fsc[fq][:, :], ALU.add, [YS[yb], FSC[fq]], [YS[yb]])
                dma("pool", yout[k][r0 + b * 128:r0 + (b + 1) * 128, :], ystage[yb][:, :], [YS[yb]], [])
        S.finish()

        sem_ctx = {}
        for e in ("pe", "act", "dve", "pool"):
            sem_ctx[e] = es.enter_context(nc.semaphore("s_" + e))
        for e in ("act", "pool", "sp"):
            for i in range(NSLOT):
                sem_ctx[(e, i)] = es.enter_context(nc.semaphore("d_%s%d" % (e, i)))
        block = es.enter_context(nc.Block())
        S.emit(nc, block, sem_ctx)
    return nc, S, list(dbg_out.keys())


def _t5_bucket(rel):
    half = 16
    max_exact = 8
    ret = (rel > 0).astype(np.int32) * half
    n = np.abs(rel)
    large = max_exact + (np.log(np.maximum(n, 1).astype(np.float32) / max_exact)
                         / math.log(128 / max_exact) * (half - max_exact)).astype(np.int32)
    large = np.minimum(large, half - 1)
    return ret + np.where(n < max_exact, n, large)


def _constants():
    ident = np.eye(128, dtype=np.float32)
    si = np.zeros((128, 64), np.float32)
    si[np.arange(64), np.arange(64)] = 1.0
    si[np.arange(64) + 64, np.arange(64)] = 1.0
    kk = np.arange(128)[:, None]
    qq = np.arange(128)[None, :]
    sel = np.zeros((32, 3, 128, 128), np.float32)
    msk = np.zeros((128, 3, 128), np.float32)
    for j in range(3):
        rel = (j - 1) * 128 + kk - qq
        b = _t5_bucket(rel)
        for bb in range(32):
            sel[bb, j][b == bb] = 1.0
        msk[:, j, :] = np.where(np.abs(rel) <= 128, 0.0, NEG)
    return ident, si, sel.reshape(32, -1), msk.reshape(128, -1)


def _rope_cs(pos):
    half = 32
    inv = (1.0 / (np.float32(10000.0) ** (np.arange(half, dtype=np.float32) / np.float32(half)))).astype(np.float32)
    ang = (pos.astype(np.float32)[None, :] * inv[:, None]).astype(np.float32)
    c, s = np.cos(ang).astype(np.float32), np.sin(ang).astype(np.float32)
    return np.ascontiguousarray(np.concatenate([c, c, s, s], axis=0))


def make_in_maps(inp, n_seq_cores, n_cores):
    xp_all = np.asarray(inp["x_prompt"])
    xs_all = np.asarray(inp["x_sample"])[0]
    P_SEQ = xp_all.shape[1]
    S_SEQ = xs_all.shape[0]
    own_p = P_SEQ // n_seq_cores
    own_s = S_SEQ // n_cores
    ident, si, sel, msk = _constants()
    shared = {"ident": ident, "stack_ident": si, "t5_sel": sel, "win_mask": msk,
              "rel_bias": np.asarray(inp["rel_bias"], np.float32)}
    for n in WNAMES:
        shared[n] = np.ascontiguousarray(np.asarray(inp[n])[0])
    for n in ("q_norm_g", "kv_norm_g", "b_gate", "ln1_g", "ln1_b", "ln2_g", "ln2_b"):
        v = np.asarray(inp[n], np.float32).reshape(-1)
        shared[n] = np.ascontiguousarray(v.reshape(-1, 128).T)
    shared["sink"] = np.ascontiguousarray(np.broadcast_to(np.asarray(inp["sink"], np.float32).reshape(1, 6), (128, 6)))
    maps = []
    for c in range(n_cores):
        seq, pc = c // n_seq_cores, c % n_seq_cores
        m = dict(shared)
        m["xp"] = np.ascontiguousarray(np.roll(xp_all[seq], -pc * own_p, axis=0))
        m["xs"] = np.ascontiguousarray(np.roll(xs_all, -c * own_s, axis=0))
        m["mem_p"] = np.ascontiguousarray(np.asarray(inp["mem_prompt"])[seq])
        m["mem_s"] = np.ascontiguousarray(np.asarray(inp["mem_sample"])[0])
        m["cs_p"] = _rope_cs((np.arange(P_SEQ) + pc * own_p) % P_SEQ)
        m["cs_s"] = _rope_cs((np.arange(S_SEQ) + c * own_s) % S_SEQ)
        fl = np.zeros((128, 4), np.float32)
        fl[:, 0] = 0.0 if pc == 0 else 1.0
        fl[:, 1] = 0.0 if pc == n_seq_cores - 1 else 1.0
        fl[:, 2] = 0.0 if c == 0 else 1.0
        fl[:, 3] = 0.0 if c == n_cores - 1 else 1.0
        m["flags"] = fl
        maps.append(m)
    return maps, P_SEQ, S_SEQ, own_p, own_s


def run(inp, dbg=()):
    n_cores = 8
    n_seq_cores = 4
    maps, P_SEQ, S_SEQ, own_p, own_s = make_in_maps(inp, n_seq_cores, n_cores)
    nc, S, dnames = build(P_SEQ, S_SEQ, own_p, own_s, dbg)
    if os.environ.get("KTRACE"):
        res = run_bass_kernel_spmd(nc, maps, core_ids=list(range(n_cores)), trace=True)
        print("EXEC_TIME_NS", res.exec_time_ns, flush=True)
    else:
        res = run_bass_kernel_spmd(nc, maps, core_ids=list(range(n_cores)))
    B = np.asarray(inp["x_prompt"]).shape[0]
    yp = np.zeros((B, P_SEQ, D), np.float32)
    ys = np.zeros((1, S_SEQ, D), np.float32)
    for c in range(n_cores):
        seq, pc = c // n_seq_cores, c % n_seq_cores
        r = res.results[c]
        yp[seq, pc * own_p:(pc + 1) * own_p] = r["y_p"]
        ys[0, c * own_s:(c + 1) * own_s] = r["y_s"]
    return (yp, ys), res, dnames


def kernel(**inputs):
    (yp, ys), _, _ = run(inputs)
    return (yp, ys)
```
